# Optimizing a Trainium2 kernel written in Bass

```python
import math
import jax, jax.numpy as jnp
from jax import lax
import numpy as np

D_MODEL = 1024
BATCH = 8
SEQ = 2048
DEPTH = 1
DEC_BATCH = 128
DEC_SEQ = 4
PAST_LEN = 8192
PAGE_SIZE = 128

N_HEADS_A = 8
N_KV_A = 2
HEAD_DIM_A = 64
GROUP_A = N_HEADS_A // N_KV_A
WINDOW = 128
ATTN_BLOCK = 128
N_HEADS_M = 4
HEAD_DIM_M = 128
MLSTM_CHUNK = 64
D_FF = 2816
CONV_W = 3
PLE_DIM = 256
EPS = 1e-6

QA = N_HEADS_A * HEAD_DIM_A
KVA = N_KV_A * HEAD_DIM_A
DM = N_HEADS_M * HEAD_DIM_M
SPLITS = (QA, KVA, KVA, DM, DM, DM, DM, N_HEADS_M, N_HEADS_M, D_MODEL, D_MODEL)
D_IN = sum(SPLITS)

kernel_name = 'hybrid_swa_mlstm_convffn_decode_step'


def rmsnorm(x, g):
    xf = x.astype(jnp.float32)
    y = xf * lax.rsqrt(jnp.mean(xf * xf, axis=-1, keepdims=True) + EPS)
    return (y * g.astype(jnp.float32)).astype(x.dtype)


def alibi_slopes():
    h = jnp.arange(N_HEADS_A, dtype=jnp.float32) + 1.0
    return (2.0 ** (-8.0 * h / N_HEADS_A)).reshape(N_KV_A, GROUP_A)


def sink_attend(q, k, v, dist, valid, sinks):
    s = jnp.einsum('...qkgd,...skd->...kgqs', q, k).astype(jnp.float32) * (HEAD_DIM_A ** -0.5)
    s = s - alibi_slopes()[:, :, None, None] * dist[..., None, None, :, :]
    s = jnp.where(valid[..., None, None, :, :], s, -jnp.inf)
    sink = sinks.astype(jnp.float32).reshape(N_KV_A, GROUP_A)[:, :, None, None]
    mx = jnp.maximum(jnp.max(s, axis=-1, keepdims=True), sink)
    e = jnp.exp(s - mx)
    p = e / (jnp.sum(e, axis=-1, keepdims=True) + jnp.exp(sink - mx))
    return jnp.einsum('...kgqs,...skd->...qkgd', p.astype(v.dtype), v)


def window_attn_prompt(q, k, v, sinks):
    B, S = q.shape[:2]
    nb = S // ATTN_BLOCK
    qb = q.reshape(B, nb, ATTN_BLOCK, N_KV_A, GROUP_A, HEAD_DIM_A)
    pad = ((0, 0), (ATTN_BLOCK, 0), (0, 0), (0, 0))
    kb = jnp.pad(k, pad).reshape(B, nb + 1, ATTN_BLOCK, N_KV_A, HEAD_DIM_A)
    vb = jnp.pad(v, pad).reshape(B, nb + 1, ATTN_BLOCK, N_KV_A, HEAD_DIM_A)
    kc = jnp.concatenate([kb[:, :-1], kb[:, 1:]], axis=2)
    vc = jnp.concatenate([vb[:, :-1], vb[:, 1:]], axis=2)
    i = jnp.arange(ATTN_BLOCK)[:, None]
    j = jnp.arange(2 * ATTN_BLOCK)[None, :]
    dist = i + ATTN_BLOCK - j
    band = (dist >= 0) & (dist <= WINDOW)
    real = (jnp.arange(nb)[:, None, None] > 0) | (j[None] >= ATTN_BLOCK)
    valid = band[None] & real
    o = sink_attend(qb, kc, vc, dist.astype(jnp.float32), valid, sinks)
    return o.reshape(B, S, QA)


def window_attn_sample(q, k_new, v_new, k_buf, v_buf, sinks):
    DB, T = q.shape[:2]
    kc = jnp.concatenate([k_buf.astype(k_new.dtype), k_new], axis=1)
    vc = jnp.concatenate([v_buf.astype(v_new.dtype), v_new], axis=1)
    i = jnp.arange(T)[:, None]
    j = jnp.arange(WINDOW + T)[None, :]
    dist = i + WINDOW - j
    valid = (dist >= 0) & (dist <= WINDOW)
    o = sink_attend(q.reshape(DB, T, N_KV_A, GROUP_A, HEAD_DIM_A), kc, vc,
                    dist.astype(jnp.float32), valid, sinks)
    return o.reshape(DB, T, QA), kc[:, -WINDOW:], vc[:, -WINDOW:]


def mlstm_chunkwise(q, k, v, ig, lf, C0, n0, m0):
    B, T, H, d = q.shape
    L = math.gcd(T, MLSTM_CHUNK)
    nc = T // L

    def to_chunks(a):
        a = a.reshape((B, nc, L) + a.shape[2:])
        return jnp.moveaxis(jnp.moveaxis(a, 3, 2), 1, 0)

    causal = jnp.tril(jnp.ones((L, L), dtype=bool))

    def step(carry, xs):
        C, n, m = carry
        qc, kc, vc, ic, fc = xs
        F = jnp.cumsum(fc, axis=-1)
        Dm = jnp.where(causal, F[..., :, None] - F[..., None, :] + ic[..., None, :], -jnp.inf)
        inter = F + m[..., None]
        mt = jnp.maximum(inter, jnp.max(Dm, axis=-1))
        a_inter = jnp.exp(inter - mt)
        qk = jnp.einsum('bhtd,bhsd->bhts', qc, kc) * jnp.exp(Dm - mt[..., None])
        num = a_inter[..., None] * jnp.einsum('bhtd,bhde->bhte', qc, C) + jnp.einsum('bhts,bhse->bhte', qk, vc)
        den = a_inter * jnp.einsum('bhtd,bhd->bht', qc, n) + jnp.sum(qk, axis=-1)
        h = num / jnp.maximum(jnp.abs(den), jnp.exp(-mt))[..., None]
        m_new = mt[..., -1]
        FL = F[..., -1]
        dec = jnp.exp(FL + m - m_new)
        ws = jnp.exp(FL[..., None] - F + ic - m_new[..., None])
        C_new = dec[..., None, None] * C + jnp.einsum('bhs,bhsd,bhse->bhde', ws, kc, vc)
        n_new = dec[..., None] * n + jnp.einsum('bhs,bhsd->bhd', ws, kc)
        return (C_new, n_new, m_new), h

    xs = (to_chunks(q), to_chunks(k), to_chunks(v), to_chunks(ig), to_chunks(lf))
    (C1, n1, m1), hs = lax.scan(step, (C0, n0, m0), xs)
    h = jnp.moveaxis(jnp.moveaxis(hs, 0, 1), 2, 3).reshape(B, T, H, d)
    return h, C1, n1, m1


def conv_ffn(xn, conv_buf, w_up, conv_w, conv_b, w_down):
    B, T, _ = xn.shape
    u = xn @ w_up
    if conv_buf is None:
        conv_buf = jnp.zeros((B, CONV_W - 1, u.shape[-1]), u.dtype)
    uc = jnp.concatenate([conv_buf.astype(u.dtype), u], axis=1)
    y = conv_b
    for j in range(CONV_W):
        y = y + conv_w[j] * uc[:, j:j + T]
    a, b = jnp.split(y, 2, axis=-1)
    return (jax.nn.gelu(a) * b) @ w_down, uc[:, -(CONV_W - 1):]


def block(x, p, win_k, win_v, C0, n0, m0, conv_buf,
          g_mix, w_in, b_i, b_f, g_q, g_k, sinks, g_hm, w_oa, w_om, w_out,
          g_ffn, w_up, conv_w, conv_b, w_down, w_ple, w_ple_gate):
    B, T, _ = x.shape
    f32 = jnp.float32
    xn = rmsnorm(x, g_mix)
    z = xn @ w_in
    idx = np.cumsum(SPLITS)[:-1].tolist()
    q_a, k_a, v_a, q_m, k_m, v_m, o_m, i_m, f_m, ga, gm = jnp.split(z, idx, axis=-1)

    q_a = rmsnorm(q_a.reshape(B, T, N_HEADS_A, HEAD_DIM_A), g_q)
    k_a = rmsnorm(k_a.reshape(B, T, N_KV_A, HEAD_DIM_A), g_k)
    v_a = v_a.reshape(B, T, N_KV_A, HEAD_DIM_A)
    if win_k is None:
        y_a = window_attn_prompt(q_a, k_a, v_a, sinks)
        new_k, new_v = k_a[:, -WINDOW:], v_a[:, -WINDOW:]
    else:
        y_a, new_k, new_v = window_attn_sample(q_a, k_a, v_a, win_k, win_v, sinks)

    qm = q_m.reshape(B, T, N_HEADS_M, HEAD_DIM_M).astype(f32)
    km = k_m.reshape(B, T, N_HEADS_M, HEAD_DIM_M).astype(f32) * (HEAD_DIM_M ** -0.5)
    vm = v_m.reshape(B, T, N_HEADS_M, HEAD_DIM_M).astype(f32)
    ig = i_m.astype(f32) + b_i.astype(f32)
    lf = jax.nn.log_sigmoid(f_m.astype(f32) + b_f.astype(f32))
    if C0 is None:
        C0 = jnp.zeros((B, N_HEADS_M, HEAD_DIM_M, HEAD_DIM_M), f32)
        n0 = jnp.zeros((B, N_HEADS_M, HEAD_DIM_M), f32)
        m0 = jnp.zeros((B, N_HEADS_M), f32)
    hm, C1, n1, m1 = mlstm_chunkwise(qm, km, vm, ig, lf, C0.astype(f32), n0.astype(f32), m0.astype(f32))
    hm = rmsnorm(hm, g_hm.reshape(N_HEADS_M, HEAD_DIM_M)).reshape(B, T, DM).astype(x.dtype)
    hm = hm * jax.nn.sigmoid(o_m)

    mixed = jax.nn.sigmoid(ga) * (y_a @ w_oa) + jax.nn.sigmoid(gm) * (hm @ w_om)
    x = x + mixed @ w_out

    f, conv_new = conv_ffn(rmsnorm(x, g_ffn), conv_buf, w_up, conv_w, conv_b, w_down)
    x = x + f

    x = x + jax.nn.sigmoid(x @ w_ple_gate) * (p.astype(x.dtype) @ w_ple)
    return x, (new_k, new_v, C1, n1, m1, conv_new)


def setup_inputs(seed: int = 0) -> dict:
    key = jax.random.key(seed)
    ks = iter(jax.random.split(key, 40))

    def nrm(shape, scale):
        return jax.random.normal(next(ks), shape, jnp.float32) * scale

    F2 = 2 * D_FF
    return {
        'x_prompt': nrm((BATCH, SEQ, D_MODEL), 1.0),
        'x_sample': nrm((DEC_BATCH, DEC_SEQ, D_MODEL), 1.0),
        'cache_win_k': nrm((DEPTH, DEC_BATCH, WINDOW, N_KV_A, HEAD_DIM_A), 1.0),
        'cache_win_v': nrm((DEPTH, DEC_BATCH, WINDOW, N_KV_A, HEAD_DIM_A), 1.0),
        'state_mlstm_C': nrm((DEPTH, DEC_BATCH, N_HEADS_M, HEAD_DIM_M, HEAD_DIM_M), 0.05),
        'state_mlstm_n': nrm((DEPTH, DEC_BATCH, N_HEADS_M, HEAD_DIM_M), 0.1),
        'state_mlstm_m': nrm((DEPTH, DEC_BATCH, N_HEADS_M), 0.5),
        'state_ffn_conv': nrm((DEPTH, DEC_BATCH, CONV_W - 1, F2), 1.0),
        'p_prompt': nrm((DEPTH, BATCH, SEQ, PLE_DIM), 1.0),
        'p_sample': nrm((DEPTH, DEC_BATCH, DEC_SEQ, PLE_DIM), 1.0),
        'g_mix': 1.0 + nrm((DEPTH, D_MODEL), 0.02),
        'w_in': nrm((DEPTH, D_MODEL, D_IN), D_MODEL ** -0.5),
        'b_i': nrm((DEPTH, N_HEADS_M), 0.1),
        'b_f': jnp.linspace(3.0, 6.0, N_HEADS_M, dtype=jnp.float32)[None] + nrm((DEPTH, N_HEADS_M), 0.01),
        'g_q': 1.0 + nrm((DEPTH, HEAD_DIM_A), 0.02),
        'g_k': 1.0 + nrm((DEPTH, HEAD_DIM_A), 0.02),
        'sinks': nrm((DEPTH, N_HEADS_A), 0.5),
        'g_hm': 1.0 + nrm((DEPTH, DM), 0.02),
        'w_oa': nrm((DEPTH, QA, D_MODEL), QA ** -0.5),
        'w_om': nrm((DEPTH, DM, D_MODEL), DM ** -0.5),
        'w_out': nrm((DEPTH, D_MODEL, D_MODEL), D_MODEL ** -0.5),
        'g_ffn': 1.0 + nrm((DEPTH, D_MODEL), 0.02),
        'w_up': nrm((DEPTH, D_MODEL, F2), D_MODEL ** -0.5),
        'conv_w': nrm((DEPTH, CONV_W, F2), CONV_W ** -0.5),
        'conv_b': nrm((DEPTH, F2), 0.01),
        'w_down': nrm((DEPTH, D_FF, D_MODEL), D_FF ** -0.5),
        'w_ple': nrm((DEPTH, PLE_DIM, D_MODEL), PLE_DIM ** -0.5),
        'w_ple_gate': nrm((DEPTH, D_MODEL, D_MODEL), D_MODEL ** -0.5),
    }


def reference(x_prompt, x_sample, cache_win_k, cache_win_v, state_mlstm_C, state_mlstm_n,
              state_mlstm_m, state_ffn_conv, p_prompt, p_sample,
              g_mix, w_in, b_i, b_f, g_q, g_k, sinks, g_hm, w_oa, w_om, w_out,
              g_ffn, w_up, conv_w, conv_b, w_down, w_ple, w_ple_gate):
    hp, hs = x_prompt, x_sample
    st_p, st_s = [], []
    for l in range(DEPTH):
        w = (g_mix[l], w_in[l], b_i[l], b_f[l], g_q[l], g_k[l], sinks[l], g_hm[l], w_oa[l], w_om[l],
             w_out[l], g_ffn[l], w_up[l], conv_w[l], conv_b[l], w_down[l], w_ple[l], w_ple_gate[l])
        hp, sp = block(hp, p_prompt[l], None, None, None, None, None, None, *w)
        hs, ss = block(hs, p_sample[l], cache_win_k[l], cache_win_v[l], state_mlstm_C[l],
                       state_mlstm_n[l], state_mlstm_m[l], state_ffn_conv[l], *w)
        st_p.append(sp)
        st_s.append(ss)
    kp, vp, Cp, np_, mp, cp = [jnp.stack(a) for a in zip(*st_p)]
    ks_, vs_, Cs, ns, ms, cs = [jnp.stack(a) for a in zip(*st_s)]
    return (hp, hs, kp, vp, Cp, np_, mp, cp, ks_, vs_, Cs, ns, ms, cs)
```

```python
import contextlib
import numpy as np
import concourse.bass as bass
import concourse.mybir as mybir
from concourse.bass_utils import run_bass_kernel_spmd

F32 = mybir.dt.float32
BF16 = mybir.dt.bfloat16
AF = mybir.ActivationFunctionType
ALU = mybir.AluOpType
AX = mybir.AxisListType

ENGS = ("pe", "act", "dve", "pool", "sp")
NCORES = 8
D = 1024
DIN = 4872
F2 = 5632
DFF = 2816
EPS = 1e-6
NEG = -30000.0


class Op:
    __slots__ = ("eng", "fn", "deps", "is_dma", "ndma", "needs_inc", "ordinal",
                 "sem", "target", "idx", "capwait")

    def __init__(self, eng, fn, is_dma, ndma):
        self.eng = eng
        self.fn = fn
        self.deps = []
        self.is_dma = is_dma
        self.ndma = ndma
        self.needs_inc = False
        self.ordinal = None
        self.sem = None
        self.target = None
        self.capwait = None


class Prog:
    NDMASEM = 16

    def __init__(self, nc):
        self.nc = nc
        self.ops = []
        self.last_w = {}
        self.readers = {}
        self.dma_count = {"sp": 0, "pool": 0, "act": 0}
        self.dma_hist = {"sp": [], "pool": [], "act": []}

    def _add(self, op, reads, writes):
        deps = set()
        for k in reads:
            w = self.last_w.get(k)
            if w is not None:
                deps.add(w)
        for k in writes:
            w = self.last_w.get(k)
            if w is not None:
                deps.add(w)
            for r in self.readers.get(k, ()):
                deps.add(r)
        deps.discard(op)
        op.deps = sorted(deps, key=lambda o: o.idx)
        for k in reads:
            self.readers.setdefault(k, []).append(op)
        for k in writes:
            self.last_w[k] = op
            self.readers[k] = []
        op.idx = len(self.ops)
        self.ops.append(op)
        return op

    def op(self, eng, fn, reads=(), writes=()):
        o = Op(eng, fn, False, 0)
        o.idx = len(self.ops)
        return self._add(o, reads, writes)

    def dma(self, q, fn, reads=(), writes=(), n=1):
        op = Op(q, fn, True, n)
        op.idx = len(self.ops)
        i = self.dma_count[q]
        self.dma_count[q] += 1
        op.sem = (q, i % self.NDMASEM)
        hist = self.dma_hist[q]
        prev = hist[i - self.NDMASEM] if i >= self.NDMASEM else None
        base = prev.target if prev is not None else 0
        op.target = base + 16 * n
        op.capwait = prev
        hist.append(op)
        return self._add(op, reads, writes)

    def emit(self):
        nc = self.nc
        for op in self.ops:
            for d in op.deps:
                if d.is_dma:
                    continue
                if d.eng == "pe" and op.eng == "pe" and not op.is_dma:
                    continue
                d.needs_inc = True
        cnt = {e: 0 for e in ENGS}
        for op in self.ops:
            if not op.is_dma and op.needs_inc:
                cnt[op.eng] += 1
                op.ordinal = cnt[op.eng]
        with contextlib.ExitStack() as st:
            esem = {e: st.enter_context(nc.semaphore("s_" + e)) for e in ENGS}
            dsem = {}
            for q in ("sp", "pool", "act"):
                for j in range(min(self.NDMASEM, self.dma_count[q])):
                    dsem[(q, j)] = st.enter_context(nc.semaphore("d_%s_%d" % (q, j)))
            block = st.enter_context(nc.Block())
            per_eng = {e: [o for o in self.ops if o.eng == e] for e in ENGS}

            def run(e, eng):
                waited = {}

                def wait(sem_key, sem, val):
                    if waited.get(sem_key, 0) >= val:
                        return
                    waited[sem_key] = val
                    eng.wait_ge(sem, val)

                for op in per_eng[e]:
                    for d in op.deps:
                        if d.is_dma:
                            wait(d.sem, dsem[d.sem], d.target)
                        else:
                            if d.eng == "pe" and e == "pe" and not op.is_dma:
                                continue
                            wait(d.eng, esem[d.eng], d.ordinal)
                    if op.is_dma:
                        if op.capwait is not None:
                            wait(op.capwait.sem, dsem[op.capwait.sem], op.capwait.target)
                        insts = op.fn(eng)
                        assert len(insts) == op.ndma, (len(insts), op.ndma)
                        for ins in insts:
                            ins.then_inc(dsem[op.sem], 16)
                    else:
                        ins = op.fn(eng)
                        if op.needs_inc:
                            ins.then_inc(esem[e], 1)
                if e in self.dma_hist:
                    for d in self.dma_hist[e][-self.NDMASEM:]:
                        wait(d.sem, dsem[d.sem], d.target)

            block.tensor(lambda eng: run("pe", eng))
            block.scalar(lambda eng: run("act", eng))
            block.vector(lambda eng: run("dve", eng))
            block.gpsimd(lambda eng: run("pool", eng))
            block.sync(lambda eng: run("sp", eng))


def _consts():
    c = {}
    c["ident"] = np.eye(128, dtype=np.float32)
    slopes = 2.0 ** (-(np.arange(8) + 1.0))
    i = np.arange(128)[None, :]
    j = np.arange(128)[:, None]
    ab = np.zeros((128, 2, 8, 128), np.float32)
    dprev = (i + 128 - j).astype(np.float32)
    dcur = (i - j).astype(np.float32)
    for h in range(8):
        ab[:, 0, h, :] = np.where(dprev <= 128, -slopes[h] * dprev, NEG)
        ab[:, 1, h, :] = np.where(dcur >= 0, -slopes[h] * dcur, NEG)
    c["abias"] = ab.reshape(128, 2 * 8 * 128)
    qi = np.arange(64)[None, :]
    kj = np.arange(64)[:, None]
    same = (qi // 4) == (kj // 4)
    ok = same & (kj <= qi)
    sb = np.zeros((64, 8, 64), np.float32)
    for h in range(8):
        sb[:, h, :] = np.where(ok, -slopes[h] * (qi - kj), NEG)
    c["sbias"] = sb.reshape(64, 8 * 64)
    c["maskp"] = (j <= i).astype(np.float32)
    c["masks"] = ok.astype(np.float32)
    oh = ((np.arange(64)[:, None] // 4) == np.arange(16)[None, :]).astype(np.float32)
    c["onehot"] = oh
    bm = ((np.arange(64)[None, :] // 4) == np.arange(16)[:, None]).astype(np.float32)
    c["blockmask"] = np.ascontiguousarray(np.broadcast_to(bm.reshape(1, 16 * 64), (128, 16 * 64)))
    c["ones"] = np.ones((4, 128), np.float32)
    return c


DEBUG = False


def build_nc():
    nc = bass.Bass("TRN2", target_bir_lowering=False)
    P = Prog(nc)

    def din(name, shape):
        return nc.dram_tensor(name, list(shape), F32, kind="ExternalInput").ap()

    def dout(name, shape):
        return nc.dram_tensor(name, list(shape), F32, kind="ExternalOutput").ap()

    x_p = din("x_p", (2048, D))
    x_s = din("x_s", (64, D))
    ck_s = din("ck_s", (16, 128, 128))
    cv_s = din("cv_s", (16, 128, 128))
    C_s = din("C_s", (64, 128, 128))
    n_s = din("n_s", (64, 128))
    m_s = din("m_s", (16, 4))
    cst_s = din("cst_s", (32, F2))
    p_p = din("p_p", (2048, 256))
    p_s = din("p_s", (64, 256))
    g_mix = din("g_mix", (1, D))
    w_in = din("w_in", (D, DIN))
    b_i = din("b_i", (4, 1))
    b_f = din("b_f", (4, 1))
    g_q = din("g_q", (1, 64))
    g_k = din("g_k", (1, 64))
    sinks = din("sinks", (1, 8))
    g_hm = din("g_hm", (1, 512))
    w_oa = din("w_oa", (512, D))
    w_om = din("w_om", (512, D))
    w_out = din("w_out", (D, D))
    g_ffn = din("g_ffn", (1, D))
    w_up = din("w_up", (D, F2))
    conv_w = din("conv_w", (3, F2))
    conv_b = din("conv_b", (1, F2))
    w_down = din("w_down", (DFF, D))
    w_ple = din("w_ple", (256, D))
    w_pg = din("w_pg", (D, D))
    c_ident = din("c_ident", (128, 128))
    c_abias = din("c_abias", (128, 2048))
    c_sbias = din("c_sbias", (64, 512))
    c_maskp = din("c_maskp", (128, 128))
    c_masks = din("c_masks", (64, 64))
    c_onehot = din("c_onehot", (64, 16))
    c_blockmask = din("c_blockmask", (128, 1024))
    c_ones = din("c_ones", (4, 128))

    o_yp = dout("o_yp", (2048, D))
    o_ys = dout("o_ys", (64, D))
    o_kp = dout("o_kp", (128, 128))
    o_vp = dout("o_vp", (128, 128))
    o_Cp = dout("o_Cp", (4, 128, 128))
    o_np = dout("o_np", (4, 128))
    o_mp = dout("o_mp", (4, 1))
    o_cp = dout("o_cp", (2, F2))
    o_ks = dout("o_ks", (16, 128, 128))
    o_vs = dout("o_vs", (16, 128, 128))
    o_Cs = dout("o_Cs", (64, 128, 128))
    o_ns = dout("o_ns", (64, 128))
    o_ms = dout("o_ms", (16, 4))
    o_cs = dout("o_cs", (32, F2))

    _cnt = [0]

    def dbg(name, ap, reads):
        if not DEBUG:
            return
        o = nc.dram_tensor("dbg_" + name, list(ap.shape), F32, kind="ExternalOutput").ap()
        q = "pool" if ap.dtype == BF16 else "sp"
        P.dma(q, lambda e: [e.dma_start(out=o, in_=ap)], reads=reads)

    def sb(shape, dt=F32, name=None):
        _cnt[0] += 1
        return nc.alloc_sbuf_tensor(name or ("t%d" % _cnt[0]), list(shape), dt).ap()

    banks = [nc.alloc_psum_tensor("bank%d" % i, [128, 512], F32).ap() for i in range(8)]
    bank_i = [0]

    def nb():
        i = bank_i[0] % 8
        bank_i[0] += 1
        return i

    def BK(i):
        return ("bank", i)

    idf = sb([128, 128]); idb = sb([128, 128], BF16)
    abias = sb([128, 2, 8, 128], BF16); sbias = sb([64, 8, 64], BF16)
    maskp = sb([128, 128]); masks = sb([64, 64]); onehot = sb([64, 16])
    blockmask = sb([128, 16, 64], BF16)
    ones4 = sb([4, 128])
    gmixT = sb([128, 8]); gffnT = sb([128, 8]); ghm_b = sb([128, 512])
    gqk_b = sb([128, 10, 64]); esink = sb([128, 8])
    bi_t = sb([4, 1]); nbf_t = sb([4, 1])
    cw = sb([128, 4, 44])
    cwraw = sb([44, 4, 128])

    def ld(dst, src, key, q="sp"):
        P.dma(q, lambda e: [e.dma_start(out=dst, in_=src)], writes=[key])

    ld(idf, c_ident, "idf")
    ld(maskp, c_maskp, "maskp")
    ld(masks, c_masks, "masks")
    ld(onehot, c_onehot, "onehot")
    ld(ones4, c_ones, "ones4")
    ld(abias.rearrange("p a h q -> p (a h q)"), c_abias, "abias", q="pool")
    ld(sbias.rearrange("p h q -> p (h q)"), c_sbias, "sbias", q="pool")
    ld(blockmask.rearrange("p b t -> p (b t)"), c_blockmask, "blockmask", q="pool")
    P.dma("sp", lambda e: [e.dma_start(out=gmixT, in_=g_mix[0].rearrange("(c p) -> p c", p=128), allow_slow_non_contiguous=True)],
          writes=["gmix"])
    P.dma("sp", lambda e: [e.dma_start(out=gffnT, in_=g_ffn[0].rearrange("(c p) -> p c", p=128), allow_slow_non_contiguous=True)],
          writes=["gffn"])
    ld(ghm_b, g_hm[0].partition_broadcast(128), "ghm")
    for h in range(8):
        ld(gqk_b[:, h, :], g_q[0].partition_broadcast(128), ("gqk", h))
    for h in range(2):
        ld(gqk_b[:, 8 + h, :], g_k[0].partition_broadcast(128), ("gqk", 8 + h))
    ld(esink, sinks[0].partition_broadcast(128), "esink")
    ld(bi_t, b_i, "bi")
    ld(nbf_t, b_f, "nbf")
    for j in range(3):
        ld(cwraw[:, j, :], conv_w[j].rearrange("(c p) -> c p", p=128), ("cwraw", j))
    ld(cwraw[:, 3, :], conv_b[0].rearrange("(c p) -> c p", p=128), ("cwraw", 3))

    P.op("dve", lambda e: e.tensor_copy(out=idb, in_=idf), reads=["idf"], writes=["idb"])
    P.op("dve", lambda e: e.tensor_scalar(out=gqk_b[:, 0:8, :], in0=gqk_b[:, 0:8, :], scalar1=0.125,
                                          scalar2=None, op0=ALU.mult),
         reads=[("gqk", h) for h in range(8)], writes=[("gqk", h) for h in range(8)])
    GQK = [("gqk", h) for h in range(10)]
    P.op("act", lambda e: e.activation(out=esink, in_=esink, func=AF.Exp), reads=["esink"], writes=["esink"])
    P.op("dve", lambda e: e.tensor_scalar(out=nbf_t, in0=nbf_t, scalar1=-1.0, scalar2=None, op0=ALU.mult),
         reads=["nbf"], writes=["nbf"])
    bcw = nb()

    def tr_cw(e):
        for j in range(4):
            ins = e.transpose(out=banks[bcw][:, j * 44:(j + 1) * 44], in_=cwraw[:, j, :], identity=idf[0:44, 0:44])
        return ins
    P.op("pe", tr_cw, reads=[("cwraw", j) for j in range(4)] + ["idf"], writes=[BK(bcw)])
    P.op("dve", lambda e: e.tensor_copy(out=cw.rearrange("p j c -> p (j c)"), in_=banks[bcw][:, 0:176]),
         reads=[BK(bcw)], writes=["cw"])

    NS = 8
    slots = [sb([128, 4096], BF16, name="slot%d" % i) for i in range(NS)]
    piece_i = [0]
    NPIECE = 40
    wscr = nc.dram_tensor("wscratch", [NPIECE, 128, 4096], BF16).ap()
    grp_state = {"first": True, "idx": 0, "pending": []}
    pf = {}
    STORE_LAG = 3

    def flush_stores(keep):
        while len(grp_state["pending"]) > keep:
            idx, sl, key, used = grp_state["pending"].pop(0)
            P.dma("pool", lambda e, idx=idx, sl=sl, used=used: [e.dma_start(out=wscr[idx][:, 0:used], in_=sl[:, 0:used])],
                  reads=[key], writes=[("scr", idx)])

    def wpiece(loads, used=4096):
        s = piece_i[0] % NS
        piece_i[0] += 1
        sl = slots[s]
        key = ("slot", s)
        idx = grp_state["idx"]
        grp_state["idx"] += 1
        assert idx < NPIECE
        if grp_state["first"]:
            P.dma("pool", lambda e: [e.dma_start(out=dv(sl), in_=src) for dv, src in loads],
                  writes=[key], n=len(loads))
            grp_state["pending"].append((idx, sl, key, used))
            flush_stores(STORE_LAG)
        else:
            P.dma("sp", lambda e: [e.dma_start(out=sl[:, 0:used], in_=wscr[idx][:, 0:used])], reads=[("scr", idx)], writes=[key])
        return sl, key

    def wview(sl, nk, ncols):
        return sl[:, 0:nk * ncols].rearrange("p (k n) -> p k n", n=ncols)

    w_in_v = w_in.rearrange("(k p) n -> p k n", p=128)
    w_up_v = w_up.rearrange("(k p) n -> p k n", p=128)
    w_out_v = w_out.rearrange("(k p) n -> p k n", p=128)
    w_pg_v = w_pg.rearrange("(k p) n -> p k n", p=128)
    w_oa_v = w_oa.rearrange("(k p) n -> p k n", p=128)
    w_om_v = w_om.rearrange("(k p) n -> p k n", p=128)
    w_ple_v = w_ple.rearrange("(k p) n -> p k n", p=128)
    w_down_v = w_down.rearrange("(k p) n -> p k n", p=128)

    TMAX = 256
    NTP = 2
    NG = 2048 // TMAX
    xs2 = [sb([128, NTP, D]) for _ in range(2)]
    ps_t = sb([128, NTP, 256])
    xnTf2 = [sb([128, 8, TMAX + 2], BF16) for _ in range(2)]
    ubb = [sb([128, TMAX + 2]) for _ in range(2)]
    xnb = sb([128, D], BF16)
    stat = sb([128, 16])
    tokA = sb([128, 768])
    qkn = sb([128, 640]); qknb2 = [sb([128, 640], BF16) for _ in range(2)]
    qaT = sb([64, 8, TMAX], BF16)
    kaT = sb([64, 2, (NTP + 1) * 128], BF16)
    vaug = sb([128, NTP + 1, 2, 65], BF16)
    PT = [sb([128, 512], BF16) for _ in range(2)]
    ya = sb([128, 512], BF16)
    yaT = sb([128, 4, TMAX], BF16)
    qmT = sb([128, 4, TMAX], BF16)
    kt = sb([128, NTP, 512], BF16)
    ktT = sb([128, 4, TMAX], BF16)
    vmaug = sb([128, NTP, 4, 129], BF16)
    osig = sb([128, NTP, 512], BF16)
    igs = sb([4, TMAX]); lfs = sb([4, TMAX]); Fb = sb([4, TMAX]); mbuf = sb([4, 1 + TMAX])
    Ab = sb([4, TMAX]); tmp4 = sb([4, TMAX])
    Y4 = sb([4, 64])
    scal = sb([128, NTP, 12])
    alast = sb([128, 64])
    Shat = sb([128, 4, 129]); Cb = sb([128, 4, 129], BF16)
    PmT = sb([128, 4, 128], BF16)
    hraw = sb([128, 4, 129]); hrawB = sb([128, 4, 129]); hraw2 = [hraw, hrawB]; hn = sb([128, 512]); hsq = qkn[:, 0:512]
    hm2 = [sb([128, 512], BF16) for _ in range(2)]
    hmT = sb([128, 4, TMAX], BF16)
    big = sb([128, 8256], BF16)
    mixedT = big[:, 0:8 * TMAX].rearrange("p (k t) -> p k t", t=TMAX)
    hT = big[:, 0:22 * TMAX].rearrange("p (k t) -> p k t", t=TMAX)
    sgaT = big[:, 8 * TMAX:16 * TMAX].rearrange("p (k t) -> p k t", t=TMAX)
    sgmT = big[:, 16 * TMAX:24 * TMAX].rearrange("p (k t) -> p k t", t=TMAX)
    sg1 = sb([128, TMAX]); sg2 = sb([128, TMAX])
    yb = [[sb([128, TMAX]) for _ in range(2)] for _ in range(2)]
    gact = [yb[0][0], yb[0][1]]
    convst = sb([128, 2, 44])
    pb16 = sb([128, 256], BF16)
    pT = sb([128, 2, TMAX], BF16)
    x2b = xnb
    junk = xnb

    P.op("dve", lambda e: e.memset(vaug.rearrange("p a g d -> p (a g d)"), 1.0), writes=["vaug_init"])
    P.op("dve", lambda e: e.memset(vmaug.rearrange("p a g d -> p (a g d)"), 1.0), writes=["vmaug_init"])
    P.op("dve", lambda e: e.memset(mbuf, 0.0), writes=["mbuf"])
    for par_ in range(2):
        P.op("dve", lambda e, par_=par_: e.memset(xnTf2[par_][:, :, 0:2], 0.0), writes=[("xcarry", par_)])
    P.op("dve", lambda e: e.memset(convst.rearrange("p j c -> p (j c)"), 0.0), writes=[("convst", cc) for cc in range(44)])

    kcb = sb([128, 16, 128], BF16)
    vcaug = sb([128, 16, 2, 65], BF16)
    kcT2 = [sb([64, 256], BF16) for _ in range(2)]
    PTpad = big[:, 0:8192].rearrange("p (b h q) -> p b h q", h=8, q=64)
    LP = (NG - 1) % 2
    SP_ = NG % 2
    C32q = xs2[LP].rearrange("p a b -> p (a b)").rearrange("p (b d) -> p b d", d=128); Cs16 = big[:, 0:8256].rearrange("p (b d) -> p b d", d=129)
    n32 = sb([64, 128]); nT = sb([128, 64]); nTo = sb([128, 64])
    Snewq = C32q
    m0T = sb([4, 16])
    qTpad = sb([128, 16, 64], BF16)
    ktz = xs2[SP_][0:64, 1, :].bitcast(BF16).rearrange("p (b f) -> p b f", f=512)
    stg_in = [cwraw[0:32].rearrange("p a b -> p (a b)"), hn[0:32, :]]
    stg_out = [hn[0:32, :]] * 2

    junk2 = qkn.bitcast(BF16)[:, 0:D]

    def rmsnorm_T(rows, t, g_b, gkey, xkey, dstT, dkey, xs):
        rms_chain(rows, t, xkey, xs)
        rms_tr(rows, t, g_b, gkey, dstT, dkey)

    def rms_tr(rows, t, g_b, gkey, dstT, dkey):
        transpose_to(xnb, rows, 8, dstT, slice(t * 128, t * 128 + rows), "xnb", dkey, gT=g_b, gkey=gkey)

    def rms_chain(rows, t, xkey, xs):
        P.op("act", lambda e: e.activation(out=junk2[:rows], in_=xs[:rows, t, :], func=AF.Square,
                                           accum_out=stat[:rows, 0:1]),
             reads=[xkey], writes=["qkn", "stat"])
        P.op("dve", lambda e: e.tensor_scalar(out=stat[:rows, 1:2], in0=stat[:rows, 0:1], scalar1=1.0 / D,
                                              scalar2=EPS, op0=ALU.mult, op1=ALU.add),
             reads=["stat"], writes=["stat"])
        P.op("act", lambda e: e.activation(out=stat[:rows, 2:3], in_=stat[:rows, 1:2], func=AF.Ln),
             reads=["stat"], writes=["stat"])
        P.op("act", lambda e: e.activation(out=stat[:rows, 3:4], in_=stat[:rows, 2:3], func=AF.Exp, scale=-0.5),
             reads=["stat"], writes=["stat"])
        P.op("dve", lambda e: e.tensor_scalar(out=xnb[:rows], in0=xs[:rows, t, :], scalar1=stat[:rows, 3:4],
                                              scalar2=None, op0=ALU.mult),
             reads=[xkey, "stat"], writes=["xnb"])

    def transpose_to(src, rows, nchunk, dstT, tsl, skey, dkey, gT=None, gkey=None):
        b = nb()
        pv = banks[b].bitcast(BF16)

        def tr(e):
            for c in range(nchunk):
                ins = e.transpose(out=pv[:, c * 128:c * 128 + rows], in_=src[:rows, c * 128:(c + 1) * 128],
                                  identity=idb[:rows, :rows])
            return ins
        skeys = list(skey) if isinstance(skey, list) else [skey]
        P.op("pe", tr, reads=skeys + ["idb"], writes=[BK(b)])
        pin = pv[:, 0:nchunk * 128].rearrange("p (c t) -> p c t", t=128)[:, :, 0:rows]
        if gT is None:
            P.op("act", lambda e: e.activation(out=dstT[:, 0:nchunk, tsl], in_=pin, func=AF.Copy),
                 reads=[BK(b)], writes=[dkey])
        else:
            P.op("dve", lambda e: e.tensor_tensor(out=dstT[:, 0:nchunk, tsl], in0=pin,
                                                  in1=gT.unsqueeze(2).broadcast_to([128, nchunk, rows]), op=ALU.mult),
                 reads=[BK(b), gkey], writes=[dkey])

    def load_x(kind, gi, t, xs, key):
        samp_ = kind == "S"
        R_ = 64 if samp_ else 128
        src = x_s if samp_ else x_p
        r0 = 0 if samp_ else gi * TMAX + t * 128
        P.dma("sp", lambda e: [e.dma_start(out=xs[:R_, t, :], in_=src[r0:r0 + R_, :])], writes=[key])

    def load_p(kind, gi, t):
        samp_ = kind == "S"
        R_ = 64 if samp_ else 128
        psrc_ = p_s if samp_ else p_p
        r0 = 0 if samp_ else gi * TMAX + t * 128
        P.dma("sp", lambda e: [e.dma_start(out=ps_t[:R_, t, :], in_=psrc_[r0:r0 + R_, :])], writes=[("ps", t)])

    def fence(old, new):
        P.op("dve", lambda e: e.memset(stat[:, 15:16], 0.0), writes=list(old) + list(new) + ["statf"])

    def run_group(kind, gi):
        samp = kind == "S"
        NT = 1 if samp else NTP
        TS = 64 if samp else 128
        T = NT * TS
        R = TS
        xsrc = x_s if samp else x_p
        psrc = p_s if samp else p_p
        yout = o_ys if samp else o_yp
        row0 = 0 if samp else gi * TMAX
        seq = NG if samp else gi
        gpar = seq % 2
        xs = xs2[gpar]
        xnT_full = xnTf2[gpar]
        xnT = xnT_full[:, :, 2:2 + TMAX]
        xky = lambda t: ("xs", gpar, t)
        nky = lambda t: ("xnT", gpar, t)
        XK = [xky(t) for t in range(NT)]

        if gi == 0 and not samp:
            for t in range(NT):
                load_x("P", 0, t, xs, xky(t))
                load_p("P", 0, t)
            for t in range(NT):
                rmsnorm_T(R, t, gmixT, "gmix", xky(t), xnT, nky(t), xs)
        XNT = [nky(t) for t in range(NT)]

        if "A" in pf:
            (wA1, kA1), (wA2, kA2) = pf.pop("A")
        else:
            wA1, kA1 = wpiece([(lambda sl: wview(sl, 8, 512), w_in_v[:, :, 0:512])])
            wA2, kA2 = wpiece([(lambda sl: wview(sl, 8, 256), w_in_v[:, :, 512:768])], used=2048)
        wA1v = wview(wA1, 8, 512)
        wA2v = wview(wA2, 8, 256)
        cur_slot = 0 if samp else None
        for t in range(NT):
            tsl = slice(t * 128, t * 128 + R)
            b1, b2 = nb(), nb()

            def mmA(e, t=t, tsl=tsl, b1=b1, b2=b2):
                for k in range(8):
                    e.matmul(banks[b1][:R, :], lhsT=xnT[:, k, tsl], rhs=wA1v[:, k, :], start=(k == 0), stop=(k == 7))
                for k in range(8):
                    ins = e.matmul(banks[b2][:R, 0:256], lhsT=xnT[:, k, tsl], rhs=wA2v[:, k, :],
                                   start=(k == 0), stop=(k == 7))
                return ins
            P.op("pe", mmA, reads=[nky(t), kA1, kA2], writes=[BK(b1), BK(b2)])
            P.op("act", lambda e, b1=b1: e.activation(out=tokA[:R, 0:512], in_=banks[b1][:R, :], func=AF.Copy),
                 reads=[BK(b1)], writes=["tokA0"])
            P.op("dve", lambda e, b2=b2: e.tensor_copy(out=tokA[:R, 512:768], in_=banks[b2][:R, 0:256]),
                 reads=[BK(b2)], writes=["tokA1"])
            TK = ["tokA0", "tokA1"]
            P.op("dve", lambda e: e.tensor_tensor(out=qkn[:R], in0=tokA[:R, 0:640], in1=tokA[:R, 0:640], op=ALU.mult),
                 reads=TK, writes=["qkn"])
            P.op("dve", lambda e: e.tensor_reduce(out=stat[:R, 4:14], in_=qkn[:R].rearrange("p (h d) -> p h d", d=64),
                                                  axis=AX.X, op=ALU.add),
                 reads=["qkn"], writes=["stat"])
            P.op("dve", lambda e: e.tensor_scalar(out=stat[:R, 4:14], in0=stat[:R, 4:14], scalar1=1.0 / 64, scalar2=EPS,
                                                  op0=ALU.mult, op1=ALU.add), reads=["stat"], writes=["stat"])
            P.op("act", lambda e: e.activation(out=stat[:R, 4:14], in_=stat[:R, 4:14], func=AF.Ln),
                 reads=["stat"], writes=["stat"])
            P.op("act", lambda e: e.activation(out=stat[:R, 4:14], in_=stat[:R, 4:14], func=AF.Exp, scale=-0.5),
                 reads=["stat"], writes=["stat"])
            P.op("dve", lambda e: e.tensor_tensor(out=qkn[:R].rearrange("p (h d) -> p h d", d=64),
                                                  in0=tokA[:R, 0:640].rearrange("p (h d) -> p h d", d=64),
                                                  in1=stat[:R, 4:14].unsqueeze(2).broadcast_to([R, 10, 64]), op=ALU.mult),
                 reads=TK + ["stat", "qkn"], writes=["qkn"])
            P.op("dve", lambda e: e.tensor_tensor(out=qkn[:R], in0=qkn[:R], in1=gqk_b[:R].rearrange("p h d -> p (h d)"),
                                                  op=ALU.mult), reads=["qkn"] + GQK, writes=["qkn"])
            qknb = qknb2[t % 2]
            P.op("act", lambda e, qknb=qknb: e.activation(out=qknb[:R], in_=qkn[:R], func=AF.Copy), reads=["qkn"], writes=[("qknb", t % 2)])
            slot = 0 if samp else t + 1
            P.op("dve", lambda e, slot=slot: e.tensor_copy(
                out=vaug[:R, slot, :, 0:64], in_=tokA[:R, 640:768].rearrange("p (g d) -> p g d", d=64)),
                reads=TK + ["vaug_init"], writes=[("vaug", slot)])
            if samp:
                pass
            elif gi == NG - 1 and t == NTP - 1:
                P.dma("pool", lambda e: [e.dma_start(out=o_kp, in_=qkn[:, 512:640])], reads=["qkn"])
                P.dma("pool", lambda e: [e.dma_start(out=o_vp, in_=tokA[:, 640:768])], reads=TK)


        if "I" in pf:
            wI, kI = pf.pop("I")
        else:
            wI, kI = wpiece([(lambda sl: wview(sl, 8, 8), w_in_v[:, :, 2816:2824])], used=64)
        wIv = wview(wI, 8, 8)
        bi_, bf_ = nb(), nb()

        def mmIF(e):
            for k in range(8):
                e.matmul(banks[bi_][0:4, 0:T], lhsT=wIv[:, k, 0:4], rhs=xnT[:, k, 0:T], start=(k == 0), stop=(k == 7))
            for k in range(8):
                ins = e.matmul(banks[bf_][0:4, 0:T], lhsT=wIv[:, k, 4:8], rhs=xnT[:, k, 0:T], start=(k == 0), stop=(k == 7))
            return ins
        P.op("pe", mmIF, reads=XNT + [kI], writes=[BK(bi_), BK(bf_)])
        P.op("act", lambda e: e.activation(out=igs[:, 0:T], in_=banks[bi_][0:4, 0:T], func=AF.Identity, bias=bi_t),
             reads=[BK(bi_), "bi"], writes=["igs"])
        P.op("act", lambda e: e.activation(out=lfs[:, 0:T], in_=banks[bf_][0:4, 0:T], func=AF.Exp, scale=-1.0, bias=nbf_t),
             reads=[BK(bf_), "nbf"], writes=["lfs"])
        P.op("act", lambda e: e.activation(out=lfs[:, 0:T], in_=lfs[:, 0:T], func=AF.Ln, bias=1.0),
             reads=["lfs"], writes=["lfs"])
        P.op("dve", lambda e: e.tensor_scalar(out=lfs[:, 0:T], in0=lfs[:, 0:T], scalar1=-1.0, scalar2=None, op0=ALU.mult),
             reads=["lfs"], writes=["lfs"])
        if not samp:
            for t in range(NT):
                tsl = slice(t * 128, (t + 1) * 128)
                P.op("dve", lambda e, tsl=tsl: e.tensor_tensor_scan(out=Fb[:, tsl], data0=ones4[:, 0:128], data1=lfs[:, tsl],
                                                                    initial=0.0, op0=ALU.mult, op1=ALU.add),
                     reads=["lfs", "ones4"], writes=["Fb"])
                P.op("dve", lambda e, t=t, tsl=tsl: e.tensor_tensor_scan(
                    out=mbuf[:, 1 + t * 128:1 + (t + 1) * 128], data0=lfs[:, tsl], data1=igs[:, tsl],
                    initial=mbuf[:, t * 128:t * 128 + 1], op0=ALU.add, op1=ALU.max),
                    reads=["lfs", "igs", "mbuf"], writes=["mbuf"])
                P.op("dve", lambda e, t=t, tsl=tsl: e.scalar_tensor_tensor(
                    out=tmp4[:, tsl], in0=Fb[:, tsl], scalar=mbuf[:, t * 128:t * 128 + 1], in1=igs[:, tsl],
                    op0=ALU.add, op1=ALU.subtract), reads=["Fb", "mbuf", "igs"], writes=["nBb"])
                P.op("dve", lambda e, t=t, tsl=tsl: e.scalar_tensor_tensor(
                    out=Ab[:, tsl], in0=Fb[:, tsl], scalar=mbuf[:, t * 128:t * 128 + 1],
                    in1=mbuf[:, 1 + t * 128:1 + (t + 1) * 128], op0=ALU.add, op1=ALU.subtract),
                    reads=["Fb", "mbuf"], writes=["Ab"])
            nBsrc = tmp4
            msrc = lambda tsl: mbuf[:, 1 + tsl.start:1 + tsl.stop]
        else:
            Fv = Fb[:, 0:64].rearrange("p (b t) -> p b t", t=4)
            lv = lfs[:, 0:64].rearrange("p (b t) -> p b t", t=4)
            iv = igs[:, 0:64].rearrange("p (b t) -> p b t", t=4)
            mv = mbuf[:, 1:65].rearrange("p (b t) -> p b t", t=4)
            P.op("dve", lambda e: e.tensor_copy(out=Fv[:, :, 0], in_=lv[:, :, 0]), reads=["lfs"], writes=["Fb"])
            for j in range(1, 4):
                P.op("dve", lambda e, j=j: e.tensor_tensor(out=Fv[:, :, j], in0=Fv[:, :, j - 1], in1=lv[:, :, j], op=ALU.add),
                     reads=["lfs", "Fb"], writes=["Fb"])
            for j in range(4):
                prev = m0T if j == 0 else mv[:, :, j - 1]
                P.op("dve", lambda e, j=j, prev=prev: e.tensor_tensor(out=mv[:, :, j], in0=prev, in1=lv[:, :, j], op=ALU.add),
                     reads=["lfs", "m0T", "mbuf"], writes=["mbuf"])
                P.op("dve", lambda e, j=j: e.tensor_tensor(out=mv[:, :, j], in0=mv[:, :, j], in1=iv[:, :, j], op=ALU.max),
                     reads=["igs", "mbuf"], writes=["mbuf"])
            m0b = m0T.unsqueeze(2).broadcast_to([4, 16, 4])
            P.op("dve", lambda e: e.tensor_tensor(out=Ab[:, 0:64].rearrange("p (b t) -> p b t", t=4), in0=Fv, in1=m0b,
                                                  op=ALU.add), reads=["Fb", "m0T"], writes=["Ab"])
            P.op("dve", lambda e: e.tensor_tensor(out=tmp4[:, 0:64], in0=Ab[:, 0:64], in1=igs[:, 0:64], op=ALU.subtract),
                 reads=["Ab", "igs"], writes=["nBb"])
            P.op("dve", lambda e: e.tensor_tensor(out=Ab[:, 0:64], in0=Ab[:, 0:64], in1=mbuf[:, 1:65], op=ALU.subtract),
                 reads=["Ab", "mbuf"], writes=["Ab"])
            nBsrc = tmp4
            msrc = lambda tsl: mbuf[:, 1 + tsl.start:1 + tsl.stop]
            P.dma("pool", lambda e: [e.dma_start(out=o_ms.rearrange("b h -> h b"),
                                               in_=mbuf[:, 1:65].rearrange("p (b t) -> p b t", t=4)[:, :, 3],
                                               allow_slow_non_contiguous=True)], reads=["mbuf"])
        if "Q" in pf:
            wQ, kQ = pf.pop("Q")
        else:
            wQ, kQ = wpiece([(lambda sl: wview(sl, 8, 512), w_in_v[:, :, 768:1280])])
        wQv = wview(wQ, 8, 512)
        for c in range(4):
            bq_ = nb()

            def mmQ(e, c=c, bq_=bq_):
                for k in range(8):
                    ins = e.matmul(banks[bq_][:, 0:T], lhsT=wQv[:, k, c * 128:(c + 1) * 128], rhs=xnT[:, k, 0:T],
                                   start=(k == 0), stop=(k == 7))
                return ins
            P.op("pe", mmQ, reads=XNT + [kQ], writes=[BK(bq_)])
            P.op("act", lambda e, c=c, bq_=bq_: e.activation(out=qmT[:, c, 0:T], in_=banks[bq_][:, 0:T], func=AF.Copy),
                 reads=[BK(bq_)], writes=[("qmT", c)])
        wV, kV = wpiece([(lambda sl: wview(sl, 8, 512), w_in_v[:, :, 1792:2304])])
        wO_, kO_ = wpiece([(lambda sl: wview(sl, 8, 512), w_in_v[:, :, 2304:2816])])
        wVv = wview(wV, 8, 512)
        wOv = wview(wO_, 8, 512)
        for t in range(NT):
            tsl = slice(t * 128, t * 128 + R)
            bv_, bo_ = nb(), nb()

            def mmV(e, tsl=tsl, bv_=bv_, bo_=bo_):
                for k in range(8):
                    e.matmul(banks[bv_][:R, :], lhsT=xnT[:, k, tsl], rhs=wVv[:, k, :], start=(k == 0), stop=(k == 7))
                for k in range(8):
                    ins = e.matmul(banks[bo_][:R, :], lhsT=xnT[:, k, tsl], rhs=wOv[:, k, :], start=(k == 0), stop=(k == 7))
                return ins
            P.op("pe", mmV, reads=[nky(t), kV, kO_], writes=[BK(bv_), BK(bo_)])
            P.op("act", lambda e, t=t, bv_=bv_: e.activation(
                out=vmaug[:R, t, :, 0:128], in_=banks[bv_][:R, :].rearrange("p (h d) -> p h d", d=128), func=AF.Copy),
                reads=[BK(bv_), "vmaug_init"], writes=[("vmaug", t)])
            P.op("act", lambda e, t=t, bo_=bo_: e.activation(out=osig[:R, t, :], in_=banks[bo_][:R, :], func=AF.Sigmoid),
                 reads=[BK(bo_)], writes=[("osig", t)])

        for t in range(NT):
            tsl = slice(t * 128, t * 128 + R)
            qknb = qknb2[t % 2]
            slot = 0 if samp else t + 1
            bq, bk = nb(), nb()
            pq = banks[bq].bitcast(BF16)
            pk = banks[bk].bitcast(BF16)

            def trq(e, pq=pq, pk=pk, qknb=qknb):
                for h in range(8):
                    e.transpose(out=pq[0:64, h * 128:h * 128 + R], in_=qknb[:R, h * 64:(h + 1) * 64], identity=idb[:R, :R])
                for h in range(2):
                    ins = e.transpose(out=pk[0:64, h * 128:h * 128 + R], in_=qknb[:R, 512 + h * 64:512 + (h + 1) * 64],
                                      identity=idb[:R, :R])
                return ins
            P.op("pe", trq, reads=[("qknb", t % 2), "idb"], writes=[BK(bq), BK(bk)])
            P.op("act", lambda e, pq=pq, tsl=tsl: e.activation(
                out=qaT[:, :, tsl], in_=pq[0:64, :].rearrange("p (h t) -> p h t", t=128)[:, :, 0:R], func=AF.Copy),
                reads=[BK(bq)], writes=[("qaT", t)])
            P.op("dve", lambda e, pk=pk, slot=slot: e.tensor_copy(
                out=kaT[:, :, slot * 128:slot * 128 + R],
                in_=pk[0:64, 0:256].rearrange("p (h t) -> p h t", t=128)[:, :, 0:R]),
                reads=[BK(bk)], writes=[("kaT", slot)])
        wK, kK = wpiece([(lambda sl: wview(sl, 8, 512), w_in_v[:, :, 1280:1792])])
        wKv = wview(wK, 8, 512)
        kbanks = []
        for t in range(NT):
            tsl = slice(t * 128, t * 128 + R)
            bk_ = nb()
            kbanks.append(bk_)

            def mmK(e, tsl=tsl, bk_=bk_):
                for k in range(8):
                    ins = e.matmul(banks[bk_][:R, :], lhsT=xnT[:, k, tsl], rhs=wKv[:, k, :], start=(k == 0), stop=(k == 7))
                return ins
            P.op("pe", mmK, reads=[nky(t), kK], writes=[BK(bk_)])
        for t in range(NT):
            tsl = slice(t * 128, t * 128 + R)
            bsc = nb()

            def trs(e, tsl=tsl, bsc=bsc):
                e.transpose(out=banks[bsc][:R, 0:4], in_=Ab[:, tsl], identity=idf[0:4, 0:4])
                e.transpose(out=banks[bsc][:R, 4:8], in_=nBsrc[:, tsl], identity=idf[0:4, 0:4])
                return e.transpose(out=banks[bsc][:R, 8:12], in_=msrc(tsl), identity=idf[0:4, 0:4])
            P.op("pe", trs, reads=["Ab", "nBb", "mbuf", "idf"], writes=[BK(bsc)])
            P.op("act", lambda e, t=t, bsc=bsc: e.activation(out=scal[:R, t, 0:4], in_=banks[bsc][:R, 0:4], func=AF.Exp),
                 reads=[BK(bsc)], writes=[("scal", t, 0)])
            P.op("act", lambda e, t=t, bsc=bsc: e.activation(out=scal[:R, t, 4:12], in_=banks[bsc][:R, 4:12],
                                                             func=AF.Exp, scale=-1.0),
                 reads=[BK(bsc)], writes=[("scal", t, 1)])
            P.op("dve", lambda e, t=t: e.tensor_scalar(out=scal[:R, t, 4:8], in0=scal[:R, t, 4:8], scalar1=128.0 ** -0.5,
                                                       scalar2=None, op0=ALU.mult),
                 reads=[("scal", t, 1)], writes=[("scal", t, 1)])
        nY = NT * 4 if not samp else 64
        if not samp:
            alv = Ab[:, 0:T].rearrange("p (t s) -> p t s", s=128)[:, :, 127:128].broadcast_to([4, NT, 4])
            idv = idf[0:4, 0:4].unsqueeze(1).broadcast_to([4, NT, 4])
            yv = Y4[:, 0:nY].rearrange("p (t n) -> p t n", n=4)
        else:
            alv = Ab[:, 0:64].rearrange("p (b s) -> p b s", s=4)[:, :, 3:4].broadcast_to([4, 16, 4])
            idv = idf[0:4, 0:4].unsqueeze(1).broadcast_to([4, 16, 4])
            yv = Y4[:, 0:64].rearrange("p (b n) -> p b n", n=4)
        P.op("dve", lambda e: e.tensor_tensor(out=yv, in0=idv, in1=alv, op=ALU.mult), reads=["Ab", "idf"], writes=["Y4"])
        bal = nb()
        P.op("pe", lambda e: e.matmul(banks[bal][:, 0:nY], lhsT=ones4[:, :], rhs=Y4[:, 0:nY], start=True, stop=True),
             reads=["Y4", "ones4"], writes=[BK(bal)])
        P.op("act", lambda e: e.activation(out=alast[:, 0:nY], in_=banks[bal][:, 0:nY], func=AF.Exp),
             reads=[BK(bal)], writes=["alast"])

        for t in range(NT):
            tsl = slice(t * 128, t * 128 + R)
            bk_ = kbanks[t]
            P.op("dve", lambda e, t=t, bk_=bk_: e.tensor_tensor(
                out=kt[:R, t, :].rearrange("p (h d) -> p h d", d=128),
                in0=banks[bk_][:R, :].rearrange("p (h d) -> p h d", d=128),
                in1=scal[:R, t, 4:8].unsqueeze(2).broadcast_to([R, 4, 128]), op=ALU.mult),
                reads=[BK(bk_), ("scal", t, 1)], writes=[("kt", t)])
            transpose_to(kt[:, t, :], R, 4, ktT, tsl, ("kt", t), ("ktT", t))
        s8_list = []
        mo_list = []
        if not samp:
            for t in range(NT):
                gt = gi * NTP + t
                tsl = slice(t * 128, (t + 1) * 128)
                bs_ = nb()

                def mmST(e, tsl=tsl, bs_=bs_):
                    for h in range(4):
                        ins = e.matmul(banks[bs_][:, h * 128:(h + 1) * 128], lhsT=ktT[:, h, tsl], rhs=qmT[:, h, tsl],
                                       start=True, stop=True)
                    return ins
                P.op("pe", mmST, reads=[("ktT", t)] + [("qmT", c) for c in range(4)], writes=[BK(bs_)])
                P.op("dve", lambda e, bs_=bs_: e.tensor_tensor(
                    out=PmT, in0=banks[bs_].rearrange("p (h t) -> p h t", t=128),
                    in1=maskp.unsqueeze(1).broadcast_to([128, 4, 128]), op=ALU.mult),
                    reads=[BK(bs_), "maskp"], writes=["PmT"])
                bu1, bu2 = nb(), nb()

                def mmU(e, t=t, bu1=bu1, bu2=bu2):
                    for h in range(4):
                        dst = banks[bu1 if h < 2 else bu2][:, (h % 2) * 129:(h % 2) * 129 + 129]
                        ins = e.matmul(dst, lhsT=kt[:, t, h * 128:(h + 1) * 128], rhs=vmaug[:, t, h, :], start=True, stop=True)
                    return ins
                P.op("pe", mmU, reads=[("kt", t), ("vmaug", t)], writes=[BK(bu1), BK(bu2)])
                bo1, bo2 = nb(), nb()

                def mmO(e, tsl=tsl, t=t, gt=gt, bo1=bo1, bo2=bo2):
                    for h in range(4):
                        dst = banks[bo1 if h < 2 else bo2][:, (h % 2) * 129:(h % 2) * 129 + 129]
                        if gt > 0:
                            e.matmul(dst, lhsT=qmT[:, h, tsl], rhs=Cb[:, h, :], start=True, stop=False)
                        ins = e.matmul(dst, lhsT=PmT[:, h, :], rhs=vmaug[:, t, h, :], start=(gt == 0), stop=True)
                    return ins
                P.op("pe", mmO, reads=["PmT", ("vmaug", t), "Cb"] + [("qmT", c) for c in range(4)],
                     writes=[BK(bo1), BK(bo2)])
                for h in range(4):
                    src = banks[bu1 if h < 2 else bu2][:, (h % 2) * 129:(h % 2) * 129 + 129]
                    if gt == 0:
                        P.op("dve", lambda e, h=h, src=src: e.tensor_copy(out=Shat[:, h, :], in_=src),
                             reads=[BK(bu1 if h < 2 else bu2)], writes=[("Shat", h)])
                    else:
                        pcol = 0
                        P.op("dve", lambda e, h=h, src=src, pcol=pcol: e.scalar_tensor_tensor(
                            out=Shat[:, h, :], in0=Shat[:, h, :], scalar=alprev[:, h:h + 1], in1=src,
                            op0=ALU.mult, op1=ALU.add),
                            reads=[BK(bu1 if h < 2 else bu2), ("Shat", h), "alprev"], writes=[("Shat", h)])
                P.op("dve", lambda e, t=t: e.tensor_copy(out=alprev, in_=alast[:, t * 4:t * 4 + 4]),
                     reads=["alast"] + [("Shat", h) for h in range(4)], writes=["alprev"])
                for h in range(4):
                    P.op("act", lambda e, h=h: e.activation(out=Cb[:, h, :], in_=Shat[:, h, :], func=AF.Copy,
                                                            scale=alprev[:, h:h + 1]),
                         reads=[("Shat", h), "alprev"], writes=["Cb"])
                mlstm_out_a(128, [(bo1, 0), (bo1, 1), (bo2, 0), (bo2, 1)], t % 2)
                mo_list.append(t)
        else:
            sample_mlstm()

        for t in range(NT):
            tsl = slice(t * 128, t * 128 + R)
            if not samp:
                gt = gi * NTP + t
                for g in range(2):
                    kinds = ([0] if gt > 0 else []) + [1]
                    ptk = []
                    for kind_ in kinds:
                        kslot = t if kind_ == 0 else t + 1
                        bs = nb()
                        pti = len(ptk)

                        def mmS(e, g=g, kslot=kslot, kind_=kind_, bs=bs, tsl=tsl):
                            e.matmul(banks[bs], lhsT=kaT[:, g, kslot * 128:(kslot + 1) * 128],
                                     rhs=qaT[:, 4 * g:4 * g + 4, tsl], start=True, stop=False)
                            return e.matmul(banks[bs], lhsT=idb, rhs=abias[:, kind_, 4 * g:4 * g + 4, :],
                                            start=False, stop=True)
                        P.op("pe", mmS, reads=[("kaT", kslot), ("qaT", t), "idb", "abias"], writes=[BK(bs)])
                        P.op("act", lambda e, bs=bs, pti=pti: e.activation(out=PT[pti], in_=banks[bs], func=AF.Exp),
                             reads=[BK(bs)], writes=[("PT", pti)])
                        ptk.append((pti, kslot))
                    bo = nb()
                    po = banks[bo][:, 0:260].rearrange("p (h d) -> p h d", d=65)

                    def mmPV(e, g=g, ptk=ptk, po=po):
                        for hl in range(4):
                            for n_, (pti, kslot) in enumerate(ptk):
                                ins = e.matmul(po[:, hl, :], lhsT=PT[pti][:, hl * 128:(hl + 1) * 128],
                                               rhs=vaug[:, kslot, g, :], start=(n_ == 0), stop=(n_ == len(ptk) - 1))
                        return ins
                    P.op("pe", mmPV, reads=[("PT", i_) for i_, _ in ptk] + [("vaug", ks_) for _, ks_ in ptk],
                         writes=[BK(bo)])
                    P.op("dve", lambda e, g=g, po=po: e.tensor_tensor(out=stat[:, 4:8], in0=po[:, :, 64],
                                                                      in1=esink[:, 4 * g:4 * g + 4], op=ALU.add),
                         reads=[BK(bo), "esink"], writes=["stat"])
                    P.op("dve", lambda e: e.reciprocal(out=stat[:, 4:8], in_=stat[:, 4:8]), reads=["stat"], writes=["stat"])
                    P.op("dve", lambda e, g=g, po=po: e.tensor_tensor(
                        out=ya[:, g * 256:(g + 1) * 256].rearrange("p (h d) -> p h d", d=64), in0=po[:, :, 0:64],
                        in1=stat[:, 4:8].unsqueeze(2).broadcast_to([128, 4, 64]), op=ALU.mult),
                        reads=[BK(bo), "stat"], writes=[("ya", g)])
                transpose_to(ya, 128, 4, yaT, tsl, [("ya", 0), ("ya", 1)], ("yaT", t))
        if not samp:
            P.op("dve", lambda e: e.tensor_copy(out=kaT[:, :, 0:128], in_=kaT[:, :, NTP * 128:(NTP + 1) * 128]),
                 reads=[("kaT", NTP)], writes=[("kaT", 0)])
            P.op("dve", lambda e: e.tensor_copy(out=vaug[:, 0, :, :], in_=vaug[:, NTP, :, :]),
                 reads=[("vaug", NTP)], writes=[("vaug", 0)])
        else:
            sample_attention()

        for t in mo_list:
            mlstm_out_b(128, t, t % 2)
        if not samp and gi == NG - 1:
            for h in range(4):
                P.op("dve", lambda e, h=h: e.tensor_scalar(out=hraw[:, h, :], in0=Shat[:, h, :], scalar1=alprev[:, h:h + 1],
                                                           scalar2=None, op0=ALU.mult),
                     reads=[("Shat", h), "alprev"], writes=[("hraw", 0, 0), ("hraw", 0, 1)])
            P.dma("pool", lambda e: [e.dma_start(out=o_Cp.rearrange("h d e -> d h e"), in_=hraw[:, :, 0:128])], reads=[("hraw", 0, 0), ("hraw", 0, 1)])
            P.dma("pool", lambda e: [e.dma_start(out=o_np.rearrange("h d -> d h"), in_=hraw[:, :, 128],
                                               allow_slow_non_contiguous=True)], reads=[("hraw", 0, 0), ("hraw", 0, 1)])
            P.dma("pool", lambda e: [e.dma_start(out=o_mp, in_=mbuf[:, TMAX:TMAX + 1])], reads=["mbuf"])
        elif not samp:
            P.op("dve", lambda e: e.tensor_copy(out=mbuf[:, 0:1], in_=mbuf[:, TMAX:TMAX + 1]), reads=["mbuf"], writes=["mbuf"])
        wG = []
        for hf in range(2):
            wga, kga = wpiece([(lambda sl: wview(sl, 8, 512), w_in_v[:, :, 2824 + 512 * hf:2824 + 512 * (hf + 1)])])
            wgm, kgm = wpiece([(lambda sl: wview(sl, 8, 512), w_in_v[:, :, 3848 + 512 * hf:3848 + 512 * (hf + 1)])])
            wG.append((wview(wga, 8, 512), kga, wview(wgm, 8, 512), kgm))
        for c in range(8):
            bGA, bGM = nb(), nb()
            wGAv, kGA, wGMv, kGM = wG[c // 4]
            gsl = slice((c % 4) * 128, (c % 4 + 1) * 128)

            def mmG(e, gsl=gsl, bGA=bGA, bGM=bGM, wGAv=wGAv, wGMv=wGMv):
                for k in range(8):
                    e.matmul(banks[bGA][:, 0:T], lhsT=wGAv[:, k, gsl], rhs=xnT[:, k, 0:T], start=(k == 0), stop=(k == 7))
                for k in range(8):
                    ins = e.matmul(banks[bGM][:, 0:T], lhsT=wGMv[:, k, gsl], rhs=xnT[:, k, 0:T], start=(k == 0), stop=(k == 7))
                return ins
            P.op("pe", mmG, reads=XNT + [kGA, kGM], writes=[BK(bGA), BK(bGM)])
            P.op("act", lambda e, c=c, bGA=bGA: e.activation(out=sgaT[:, c, 0:T], in_=banks[bGA][:, 0:T], func=AF.Sigmoid),
                 reads=[BK(bGA), "bigfence"], writes=[("sga", c)])
            P.op("act", lambda e, c=c, bGM=bGM: e.activation(out=sgmT[:, c, 0:T], in_=banks[bGM][:, 0:T], func=AF.Sigmoid),
                 reads=[BK(bGM), "bigfence"], writes=[("sgm", c)])
        for t in range(NT):
            transpose_to(hm2[t % 2], R, 4, hmT, slice(t * 128, t * 128 + R), ("hm", t % 2), ("hmT", t))
        if gi == 0 and not samp:
            dbg("yaT", yaT, [("yaT", t) for t in range(NT)])
            dbg("qaT", qaT, [("qaT", t) for t in range(NT)])
        if gi == 0 and not samp:
            dbg("hmT", hmT, [("hmT", t) for t in range(NT)])
        wOA, kOA = wpiece([(lambda sl: wview(sl, 4, 1024), w_oa_v)])
        wOM, kOM = wpiece([(lambda sl: wview(sl, 4, 1024), w_om_v)])
        wOAv = wview(wOA, 4, 1024)
        wOMv = wview(wOM, 4, 1024)
        YAT = [("yaT", t) for t in range(NT)]
        HMT = [("hmT", t) for t in range(NT)]
        for c in range(8):
            csl = slice(c * 128, (c + 1) * 128)
            bA, bB = nb(), nb()

            def mmM(e, csl=csl, bA=bA, bB=bB):
                for k in range(4):
                    e.matmul(banks[bA][:, 0:T], lhsT=wOAv[:, k, csl], rhs=yaT[:, k, 0:T], start=(k == 0), stop=(k == 3))
                for k in range(4):
                    ins = e.matmul(banks[bB][:, 0:T], lhsT=wOMv[:, k, csl], rhs=hmT[:, k, 0:T], start=(k == 0), stop=(k == 3))
                return ins
            P.op("pe", mmM, reads=YAT + HMT + [kOA, kOM], writes=[BK(bA), BK(bB)])
            P.op("dve", lambda e, c=c, bA=bA: e.tensor_tensor(out=sg1[:, 0:T], in0=sgaT[:, c, 0:T], in1=banks[bA][:, 0:T], op=ALU.mult),
                 reads=[("sga", c), BK(bA)], writes=["sg1"])
            P.op("dve", lambda e, c=c, bB=bB: e.tensor_tensor(out=sg2[:, 0:T], in0=sgmT[:, c, 0:T], in1=banks[bB][:, 0:T], op=ALU.mult),
                 reads=[("sgm", c), BK(bB)], writes=["sg2"])
            mix_eng = "pool" if (not samp and not grp_state["first"]) else "dve"
            P.op(mix_eng, lambda e, c=c: e.tensor_tensor(out=mixedT[:, c, 0:T], in0=sg1[:, 0:T], in1=sg2[:, 0:T], op=ALU.add),
                 reads=["sg1", "sg2", "bigfence"], writes=[("mixedT", c)])
        MIX = [("mixedT", c) for c in range(8)]

        wOUh = []
        for half in range(2):
            wOU, kOU = wpiece([(lambda sl: wview(sl, 8, 512), w_out_v[:, :, half * 512:(half + 1) * 512])])
            wOUh.append((wview(wOU, 8, 512), kOU))
        for t in range(NT):
            tsl = slice(t * 128, t * 128 + R)
            for half in range(2):
                wOUv, kOU = wOUh[half]
                bb = nb()

                def mmOut1(e, tsl=tsl, bb=bb, wOUv=wOUv):
                    for k in range(5):
                        ins = e.matmul(banks[bb][:R, :], lhsT=mixedT[:, k, tsl], rhs=wOUv[:, k, :],
                                       start=(k == 0), stop=False)
                    return ins

                def mmOut2(e, tsl=tsl, bb=bb, wOUv=wOUv):
                    for k in range(5, 8):
                        ins = e.matmul(banks[bb][:R, :], lhsT=mixedT[:, k, tsl], rhs=wOUv[:, k, :],
                                       start=False, stop=(k == 7))
                    return ins
                P.op("pe", mmOut1, reads=MIX[0:5] + [kOU], writes=[BK(bb)])
                P.op("pe", mmOut2, reads=MIX[5:8] + [kOU], writes=[BK(bb)])
                P.op("dve", lambda e, t=t, half=half, bb=bb: e.tensor_tensor(
                    out=xs[:R, t, half * 512:(half + 1) * 512], in0=xs[:R, t, half * 512:(half + 1) * 512],
                    in1=banks[bb][:R, :], op=ALU.add), reads=[xky(t), BK(bb)], writes=[xky(t)])
            if gi == 0 and not samp:
                dbg("x1_%d" % t, xs[:, t, :], [xky(t)])
            if t == 0:
                rms_chain(R, 0, xky(0), xs)
            s8_list.append(t)
        if gi == 0 and not samp:
            dbg("mixedT", mixedT, MIX)
        HT = [("hT", c) for c in range(22)]
        fence(MIX + [("sga", c) for c in range(8)] + [("sgm", c) for c in range(8)], HT)

        for t in s8_list:
            if t > 0:
                rms_chain(R, t, xky(t), xs)
            rms_tr(R, t, gffnT, "gffn", xnT, nky(t))

        if gi == 0 and not samp:
            dbg("xn2T", xnT, XNT)
        if samp:
            sample_conv_state_load()
        nxt = None
        if not samp:
            nxt = ("P", gi + 1, NTP, 128) if gi + 1 < NG else ("S", 0, 1, 64)
            xs_n = xs2[1 - gpar]
            for t in range(nxt[2]):
                load_x(nxt[0], nxt[1], t, xs_n, ("xs", 1 - gpar, t))
        for i in range(11):
            wU, kU = wpiece([(lambda sl: wview(sl, 8, 512)[:, :, 0:256], w_up_v[:, :, 256 * i:256 * i + 256]),
                             (lambda sl: wview(sl, 8, 512)[:, :, 256:512], w_up_v[:, :, DFF + 256 * i:DFF + 256 * i + 256])])
            wUv = wview(wU, 8, 512)
            for sub in range(2):
                c = 2 * i + sub
                ba_, bb_ = nb(), nb()

                TU = T if samp else T + 2
                xsrcT = xnT if samp else xnT_full

                def mmUp(e, sub=sub, ba_=ba_, bb_=bb_, wUv=wUv):
                    for k in range(8):
                        e.matmul(banks[ba_][:, 0:TU], lhsT=wUv[:, k, sub * 128:(sub + 1) * 128], rhs=xsrcT[:, k, 0:TU],
                                 start=(k == 0), stop=(k == 7))
                    for k in range(8):
                        ins = e.matmul(banks[bb_][:, 0:TU], lhsT=wUv[:, k, 256 + sub * 128:256 + (sub + 1) * 128],
                                       rhs=xsrcT[:, k, 0:TU], start=(k == 0), stop=(k == 7))
                    return ins
                P.op("pe", mmUp, reads=XNT + [kU, ("xcarry", gpar)], writes=[BK(ba_), BK(bb_)])
                par = c % 2
                for which, bnk in ((0, ba_), (1, bb_)):
                    cc = c + 22 * which
                    conv_chunk(samp, T, which, par, bnk, cc, gi == NG - 1, not grp_state["first"])
                P.op("act", lambda e, par=par: e.activation(out=gact[par][:, 0:T], in_=yb[0][par][:, 0:T], func=AF.Gelu_apprx_tanh),
                     reads=[("yb", 0, par)], writes=[("yb", 0, par)])
                h_eng = "pool" if (not samp and not grp_state["first"]) else "dve"
                P.op(h_eng, lambda e, c=c, par=par: e.tensor_tensor(out=hT[:, c, 0:T], in0=gact[par][:, 0:T], in1=yb[1][par][:, 0:T],
                                                                    op=ALU.mult),
                     reads=[("yb", 0, par), ("yb", 1, par)], writes=[("hT", c)])
        if not samp:
            xnf_n = xnTf2[1 - gpar]
            P.op("act", lambda e: e.activation(out=xnf_n[:, :, 0:2], in_=xnT_full[:, :, T:T + 2], func=AF.Copy),
                 reads=XNT, writes=[("xcarry", 1 - gpar)])
        if samp:
            sample_conv_out()
        elif gi == NG - 1:
            bcv = nb()
            P.op("pe", lambda e: e.transpose(out=banks[bcv][0:88, 0:128], in_=convst.rearrange("p j c -> p (j c)"), identity=idf),
                 reads=[("convst", cc) for cc in range(44)] + ["idf"], writes=[BK(bcv)])
            cvp = sb([88, 128], name="cvp")
            P.op("dve", lambda e: e.tensor_copy(out=cvp, in_=banks[bcv][0:88, 0:128]), reads=[BK(bcv)], writes=["cvp"])
            for j in range(2):
                P.dma("pool", lambda e, j=j: [e.dma_start(out=o_cp[j].rearrange("(c p) -> c p", p=128), in_=cvp[j * 44:(j + 1) * 44, :])],
                      reads=["cvp"])

        if gi == 0 and not samp:
            dbg("hT", hT, HT)
        wD = []
        for (k0, nk) in ((0, 4), (4, 4), (8, 4), (12, 4), (16, 4), (20, 2)):
            w_, k_ = wpiece([(lambda sl, nk=nk: wview(sl, nk, 1024), w_down_v[:, k0:k0 + nk, :])], used=nk * 1024)
            wD.append((wview(w_, nk, 1024), k_))
        dn = []
        for t in range(NT):
            tsl = slice(t * 128, t * 128 + R)
            for half in range(2):
                bb = nb()
                dn.append((t, tsl, half, bb))

                def mmDn1(e, tsl=tsl, half=half, bb=bb):
                    for k in range(16):
                        ins = e.matmul(banks[bb][:R, :], lhsT=hT[:, k, tsl],
                                       rhs=wD[k // 4][0][:, k % 4, half * 512:(half + 1) * 512], start=(k == 0), stop=False)
                    return ins
                P.op("pe", mmDn1, reads=HT[0:16] + [k_ for _, k_ in wD[0:4]], writes=[BK(bb)])
        if not samp:
            for t in range(nxt[2]):
                rmsnorm_T(nxt[3], t, gmixT, "gmix", ("xs", 1 - gpar, t), xnf_n[:, :, 2:2 + TMAX], ("xnT", 1 - gpar, t), xs_n)
        for (t, tsl, half, bb) in dn:
            def mmDn2(e, tsl=tsl, half=half, bb=bb):
                for k in range(16, 22):
                    ins = e.matmul(banks[bb][:R, :], lhsT=hT[:, k, tsl],
                                   rhs=wD[k // 4][0][:, k % 4, half * 512:(half + 1) * 512], start=False, stop=(k == 21))
                return ins
            P.op("pe", mmDn2, reads=HT[16:22] + [k_ for _, k_ in wD[4:6]], writes=[BK(bb)])
            P.op("dve", lambda e, t=t, half=half, bb=bb: e.tensor_tensor(
                out=xs[:R, t, half * 512:(half + 1) * 512], in0=xs[:R, t, half * 512:(half + 1) * 512],
                in1=banks[bb][:R, :], op=ALU.add), reads=[xky(t), BK(bb)], writes=[xky(t)])
        if gi == 0 and not samp:
            dbg("x2", xs, XK)
        fence(HT, ["bigfence"])

        wPGh = []
        for half in range(2):
            w_, k_ = wpiece([(lambda sl: wview(sl, 8, 512), w_pg_v[:, :, half * 512:(half + 1) * 512])])
            wPGh.append((wview(w_, 8, 512), k_))
        wPL, kPL = wpiece([(lambda sl: wview(sl, 2, 1024), w_ple_v)], used=2048)
        wPLv = wview(wPL, 2, 1024)
        if not samp:
            sv = (grp_state["idx"], grp_state["first"])
            grp_state["idx"], grp_state["first"] = 0, False
            pf["A"] = (wpiece([(lambda sl: wview(sl, 8, 512), w_in_v[:, :, 0:512])]),
                       wpiece([(lambda sl: wview(sl, 8, 256), w_in_v[:, :, 512:768])], used=2048))
            pf["I"] = wpiece([(lambda sl: wview(sl, 8, 8), w_in_v[:, :, 2816:2824])], used=64)
            pf["Q"] = wpiece([(lambda sl: wview(sl, 8, 512), w_in_v[:, :, 768:1280])])
            grp_state["idx"], grp_state["first"] = sv
        for t in range(NT):
            tsl = slice(t * 128, t * 128 + R)
            P.op("act", lambda e, t=t: e.activation(out=x2b[:R], in_=xs[:R, t, :], func=AF.Copy),
                 reads=[xky(t)], writes=["xnb"])
            transpose_to(x2b, R, 8, xnT, tsl, "xnb", nky(t))
            P.op("act", lambda e, t=t: e.activation(out=pb16[:R], in_=ps_t[:R, t, :], func=AF.Copy),
                 reads=[("ps", t)], writes=["pb16"])
            transpose_to(pb16, R, 2, pT, tsl, "pb16", ("pT", t))
            for half in range(2):
                bg, bp = nb(), nb()
                hs = slice(half * 512, (half + 1) * 512)

                wPGv, kPG = wPGh[half]

                def mmP(e, tsl=tsl, hs=hs, bg=bg, bp=bp, wPGv=wPGv):
                    for k in range(8):
                        e.matmul(banks[bg][:R, :], lhsT=xnT[:, k, tsl], rhs=wPGv[:, k, :], start=(k == 0), stop=(k == 7))
                    for k in range(2):
                        ins = e.matmul(banks[bp][:R, :], lhsT=pT[:, k, tsl], rhs=wPLv[:, k, hs], start=(k == 0), stop=(k == 1))
                    return ins
                P.op("pe", mmP, reads=[nky(t), ("pT", t), kPG, kPL], writes=[BK(bg), BK(bp)])
                P.op("act", lambda e, bg=bg: e.activation(out=hn[:R, 0:512], in_=banks[bg][:R, :], func=AF.Sigmoid),
                     reads=[BK(bg)], writes=["hn"])
                P.op("dve", lambda e, bp=bp: e.tensor_tensor(out=hn[:R, 0:512], in0=hn[:R, 0:512], in1=banks[bp][:R, :],
                                                            op=ALU.mult), reads=["hn", BK(bp)], writes=["hn"])
                P.op("dve", lambda e, t=t, hs=hs: e.tensor_tensor(out=xs[:R, t, hs], in0=xs[:R, t, hs], in1=hn[:R, 0:512],
                                                                 op=ALU.add), reads=["hn", xky(t)], writes=[xky(t)])
            P.dma("sp", lambda e, t=t: [e.dma_start(out=yout[row0 + t * 128:row0 + t * 128 + R, :], in_=xs[:R, t, :])],
                  reads=[xky(t)])
            if not samp:
                if gi + 1 < NG:
                    load_p("P", gi + 1, t)
                elif t == 0:
                    load_p("S", 0, 0)

        if samp:
            TK = ["tokA0", "tokA1"]
            for b in range(16):
                P.dma("sp", lambda e, b=b: [e.dma_start(out=o_ks[b, 124:128, :], in_=qkn[4 * b:4 * b + 4, 512:640])],
                      reads=["qkn"])
                P.dma("sp", lambda e, b=b: [e.dma_start(out=o_vs[b, 124:128, :], in_=tokA[4 * b:4 * b + 4, 640:768])],
                      reads=TK)

    alprev = sb([128, 4])

    def mlstm_out_a(R, srcs, hi):
        hraw = hraw2[hi]
        for j, (bnk, half) in enumerate([(srcs[0][0], 0), (srcs[2][0], 1)]):
            P.op("act", lambda e, bnk=bnk, half=half: e.activation(
                out=hraw[:R, 2 * half:2 * half + 2, :], in_=banks[bnk][:R, 0:258].rearrange("p (h d) -> p h d", d=129),
                func=AF.Copy), reads=[BK(bnk)], writes=[("hraw", hi, half)])

    def mlstm_out_b(R, t, hi):
        hm = hm2[hi]
        hraw = hraw2[hi]
        HR = [("hraw", hi, 0), ("hraw", hi, 1)]
        al = scal[:R, t, 0:4]
        gm = scal[:R, t, 8:12]
        d_ = stat[:R, 4:8]
        P.op("act", lambda e: e.activation(out=d_, in_=hraw[:R, :, 128], func=AF.Abs),
             reads=HR, writes=["stat"])
        P.op("dve", lambda e: e.tensor_tensor(out=d_, in0=d_, in1=al, op=ALU.mult), reads=["stat", ("scal", t, 0)], writes=["stat"])
        P.op("dve", lambda e: e.tensor_tensor(out=d_, in0=d_, in1=gm, op=ALU.max), reads=["stat", ("scal", t, 1)], writes=["stat"])
        P.op("dve", lambda e: e.reciprocal(out=d_, in_=d_), reads=["stat"], writes=["stat"])
        P.op("dve", lambda e: e.tensor_tensor(out=d_, in0=d_, in1=al, op=ALU.mult), reads=["stat", ("scal", t, 0)], writes=["stat"])
        hn3 = hn[:R].rearrange("p (h d) -> p h d", d=128)
        P.op("dve", lambda e: e.tensor_tensor(out=hn3, in0=hraw[:R, :, 0:128], in1=d_.unsqueeze(2).broadcast_to([R, 4, 128]),
                                              op=ALU.mult), reads=HR + ["stat"], writes=["hn"])
        P.op("dve", lambda e: e.tensor_tensor(out=hsq[:R], in0=hn[:R], in1=hn[:R], op=ALU.mult), reads=["hn"], writes=["qkn"])
        r_ = stat[:R, 8:12]
        P.op("dve", lambda e: e.tensor_reduce(out=r_, in_=hsq[:R].rearrange("p (h d) -> p h d", d=128), axis=AX.X, op=ALU.add),
             reads=["qkn"], writes=["stat"])
        P.op("dve", lambda e: e.tensor_scalar(out=r_, in0=r_, scalar1=1.0 / 128, scalar2=EPS, op0=ALU.mult, op1=ALU.add),
             reads=["stat"], writes=["stat"])
        P.op("act", lambda e: e.activation(out=r_, in_=r_, func=AF.Ln), reads=["stat"], writes=["stat"])
        P.op("act", lambda e: e.activation(out=r_, in_=r_, func=AF.Exp, scale=-0.5), reads=["stat"], writes=["stat"])
        P.op("dve", lambda e: e.tensor_tensor(out=hn3, in0=hn3, in1=r_.unsqueeze(2).broadcast_to([R, 4, 128]), op=ALU.mult),
             reads=["hn", "stat"], writes=["hn"])
        P.op("dve", lambda e: e.tensor_tensor(out=hn[:R], in0=hn[:R], in1=ghm_b[:R], op=ALU.mult), reads=["hn", "ghm"], writes=["hn"])
        P.op("dve", lambda e: e.tensor_tensor(out=hm[:R], in0=hn[:R], in1=osig[:R, t, :], op=ALU.mult),
             reads=["hn", ("osig", t)], writes=[("hm", hi)])

    def conv_chunk(samp, T, which, par, bnk, cc, last, use_pool):
        y = yb[which][par]
        w0 = cw[:, 0, cc:cc + 1]
        w1 = cw[:, 1, cc:cc + 1]
        w2 = cw[:, 2, cc:cc + 1]
        bb = cw[:, 3, cc:cc + 1]
        YK = ("yb", which, par)
        if not samp:
            ps = banks[bnk]
            yf = y[:, 0:T]
            P.op("act", lambda e: e.activation(out=yf, in_=ps[:, 2:T + 2], func=AF.Identity, scale=w2, bias=bb),
                 reads=[BK(bnk), "cw"], writes=[YK])
            if which == 1 and use_pool:
                t1 = ubb[par][:, 0:T]
                TK1 = ("t1", par)
                P.op("act", lambda e: e.activation(out=t1, in_=ps[:, 1:T + 1], func=AF.Identity, scale=w1),
                     reads=[BK(bnk), "cw"], writes=[TK1])
                P.op("dve", lambda e: e.scalar_tensor_tensor(out=yf, in0=ps[:, 0:T], scalar=w0, in1=yf,
                                                             op0=ALU.mult, op1=ALU.add), reads=[BK(bnk), YK, "cw"], writes=[YK])
                P.op("pool", lambda e: e.tensor_tensor(out=yf, in0=yf, in1=t1, op=ALU.add), reads=[TK1, YK], writes=[YK])
            else:
                P.op("dve", lambda e: e.scalar_tensor_tensor(out=yf, in0=ps[:, 1:T + 1], scalar=w1, in1=yf,
                                                             op0=ALU.mult, op1=ALU.add), reads=[BK(bnk), YK, "cw"], writes=[YK])
                P.op("dve", lambda e: e.scalar_tensor_tensor(out=yf, in0=ps[:, 0:T], scalar=w0, in1=yf,
                                                             op0=ALU.mult, op1=ALU.add), reads=[BK(bnk), YK, "cw"], writes=[YK])
            if last:
                P.op("act", lambda e: e.activation(out=convst[:, :, cc], in_=ps[:, T:T + 2], func=AF.Copy),
                     reads=[BK(bnk)], writes=[("convst", cc)])
            return
        pf = banks[bnk][:, 0:64].rearrange("p (b s) -> p b s", s=4)
        yf = y[:, 0:64].rearrange("p (b s) -> p b s", s=4)
        SK = ("cstT", cc)
        st = cstT[:, cc, :].rearrange("p (b j) -> p b j", j=2)
        sh = lambda a, lo, hi: a[:, :, lo:hi]
        L = 4
        P.op("act", lambda e: e.activation(out=yf, in_=pf, func=AF.Identity, scale=w2, bias=bb),
             reads=[BK(bnk), "cw"], writes=[YK])
        P.op("dve", lambda e: e.scalar_tensor_tensor(out=sh(yf, 1, L), in0=sh(pf, 0, L - 1), scalar=w1, in1=sh(yf, 1, L),
                                                     op0=ALU.mult, op1=ALU.add), reads=[BK(bnk), YK, "cw"], writes=[YK])
        P.op("dve", lambda e: e.scalar_tensor_tensor(out=sh(yf, 2, L), in0=sh(pf, 0, L - 2), scalar=w0, in1=sh(yf, 2, L),
                                                     op0=ALU.mult, op1=ALU.add), reads=[BK(bnk), YK, "cw"], writes=[YK])
        P.op("dve", lambda e: e.scalar_tensor_tensor(out=sh(yf, 0, 2), in0=sh(st, 0, 2), scalar=w0, in1=sh(yf, 0, 2),
                                                     op0=ALU.mult, op1=ALU.add), reads=[SK, YK, "cw"], writes=[YK])
        P.op("dve", lambda e: e.scalar_tensor_tensor(out=sh(yf, 0, 1), in0=sh(st, 1, 2), scalar=w1, in1=sh(yf, 0, 1),
                                                     op0=ALU.mult, op1=ALU.add), reads=[SK, YK, "cw"], writes=[YK])
        P.op("dve", lambda e: e.tensor_copy(out=st, in_=sh(pf, L - 2, L)), reads=[BK(bnk)], writes=[SK])

    cstT = sb([128, 44, 32])

    def sample_conv_state_load():
        for q in range(11):
            b_ = nb()
            st_ = stg_in[q % 2]
            skeys = [("stg_in", 0)] + [("cwraw", j) for j in range(4)] if q % 2 == 0 else ["hn"]
            P.dma("pool", lambda e, q=q, st_=st_: [e.dma_start(out=st_, in_=cst_s[:, q * 512:(q + 1) * 512])], writes=skeys)

            def trc(e, q=q, b_=b_, st_=st_):
                for j in range(4):
                    ins = e.transpose(out=banks[b_][:, j * 32:(j + 1) * 32], in_=st_[:, j * 128:(j + 1) * 128],
                                      identity=idf[0:32, 0:32])
                return ins
            P.op("pe", trc, reads=[skeys[0], "idf"], writes=[BK(b_)])
            P.op("dve", lambda e, q=q, b_=b_: e.tensor_copy(out=cstT[:, 4 * q:4 * q + 4, :].rearrange("p c r -> p (c r)"),
                                                            in_=banks[b_][:, 0:128]), reads=[BK(b_)], writes=[("cstT", 4 * q + j) for j in range(4)])

    def sample_conv_out():
        for q in range(11):
            b_ = nb()
            so_ = stg_out[q % 2]

            def trc(e, q=q, b_=b_):
                for j in range(4):
                    cc = 4 * q + j
                    ins = e.transpose(out=banks[b_][0:32, j * 128:(j + 1) * 128], in_=cstT[:, cc, :], identity=idf)
                return ins
            P.op("pe", trc, reads=[("cstT", 4 * q + j) for j in range(4)] + ["idf"], writes=[BK(b_)])
            P.op("act", lambda e, b_=b_, so_=so_: e.activation(out=so_, in_=banks[b_][0:32, :], func=AF.Copy),
                 reads=[BK(b_)], writes=["hn"])
            P.dma("pool", lambda e, q=q, so_=so_: [e.dma_start(out=o_cs[:, q * 512:(q + 1) * 512], in_=so_)], reads=["hn"])

    def sample_attention():
        P.op("dve", lambda e: e.memset(PTpad.rearrange("p b h q -> p (b h q)"), 0.0), reads=["bigfence"], writes=["PTpad"])
        for g in range(2):
            bs = nb()

            def mmS(e, g=g, bs=bs):
                e.matmul(banks[bs][0:64, 0:256], lhsT=kaT[:, g, 0:64], rhs=qaT[:, 4 * g:4 * g + 4, 0:64], start=True, stop=False)
                return e.matmul(banks[bs][0:64, 0:256], lhsT=idb[0:64, 0:64], rhs=sbias[:, 4 * g:4 * g + 4, :], start=False, stop=True)
            P.op("pe", mmS, reads=[("kaT", 0), ("qaT", 0), "idb", "sbias"], writes=[BK(bs)])
            P.op("act", lambda e, g=g, bs=bs: e.activation(out=PT[0][0:64, g * 256:(g + 1) * 256], in_=banks[bs][0:64, 0:256],
                                                           func=AF.Exp), reads=[BK(bs)], writes=[("PT", 0)])
        def emit_tr(b):
            bt = nb()
            ptv = banks[bt].bitcast(BF16)
            kk = b % 2

            def trk(e, b=b, ptv=ptv):
                for g in range(2):
                    ins = e.transpose(out=ptv[0:64, g * 128:(g + 1) * 128], in_=kcb[:, b, g * 64:(g + 1) * 64], identity=idb)
                return ins
            P.op("pe", trk, reads=["kcb", "idb"], writes=[BK(bt)])
            P.op("dve", lambda e, kk=kk, ptv=ptv: e.tensor_copy(out=kcT2[kk], in_=ptv[0:64, 0:256]),
                 reads=[BK(bt)], writes=[("kcT", kk)])

        def emit_score(b):
            kk = b % 2
            bs = nb()

            def mmS2(e, b=b, kk=kk, bs=bs):
                for g in range(2):
                    e.matmul(banks[bs][:, g * 16:(g + 1) * 16], lhsT=kcT2[kk][:, g * 128:(g + 1) * 128],
                             rhs=qaT[:, 4 * g:4 * g + 4, 4 * b:4 * b + 4], start=True, stop=False)
                    ins = e.matmul(banks[bs][:, g * 16:(g + 1) * 16], lhsT=idb, rhs=abias[:, 0, 4 * g:4 * g + 4, 0:4],
                                   start=False, stop=True)
                return ins
            P.op("pe", mmS2, reads=[("kcT", kk), ("qaT", 0), "idb", "abias"], writes=[BK(bs)])
            P.op("act", lambda e, b=b, bs=bs: e.activation(
                out=PTpad[:, b, :, 4 * b:4 * b + 4],
                in_=banks[bs][:, 0:32].rearrange("p (h q) -> p h q", q=4), func=AF.Exp),
                reads=[BK(bs), "PTpad"], writes=[("PTpad", b, 0), ("PTpad", b, 1)])

        emit_tr(0)
        for b in range(16):
            if b + 1 < 16:
                emit_tr(b + 1)
            emit_score(b)
        for g in range(2):
            bo = nb()
            po = banks[bo][0:64, 0:260].rearrange("p (h d) -> p h d", d=65)

            def mmPV(e, g=g, po=po):
                for hl in range(4):
                    h = 4 * g + hl
                    e.matmul(po[:, hl, :], lhsT=PT[0][0:64, h * 64:(h + 1) * 64], rhs=vaug[0:64, 0, g, :], start=True, stop=False)
                    for b in range(16):
                        ins = e.matmul(po[:, hl, :], lhsT=PTpad[:, b, h, :], rhs=vcaug[:, b, g, :], start=False, stop=(b == 15))
                return ins
            P.op("pe", mmPV, reads=[("PT", 0), ("vaug", 0), "vcaug"] + [("PTpad", b, g) for b in range(16)], writes=[BK(bo)])
            P.op("dve", lambda e, g=g, po=po: e.tensor_tensor(out=stat[0:64, 4:8], in0=po[:, :, 64], in1=esink[0:64, 4 * g:4 * g + 4],
                                                              op=ALU.add), reads=[BK(bo), "esink"], writes=["stat"])
            P.op("dve", lambda e: e.reciprocal(out=stat[0:64, 4:8], in_=stat[0:64, 4:8]), reads=["stat"], writes=["stat"])
            P.op("dve", lambda e, g=g, po=po: e.tensor_tensor(
                out=ya[0:64, g * 256:(g + 1) * 256].rearrange("p (h d) -> p h d", d=64), in0=po[:, :, 0:64],
                in1=stat[0:64, 4:8].unsqueeze(2).broadcast_to([64, 4, 64]), op=ALU.mult),
                reads=[BK(bo), "stat"], writes=[("ya", g)])
        transpose_to(ya, 64, 4, yaT, slice(0, 64), [("ya", 0), ("ya", 1)], ("yaT", 0))
        fence(["PTpad"] + [("PTpad", b, g) for b in range(16) for g in range(2)], ["bigfence"])

    def sample_mlstm():
        bn_ = nb()
        P.op("pe", lambda e: e.transpose(out=banks[bn_][:, 0:64], in_=n32, identity=idf[0:64, 0:64]),
             reads=["n32", "idf"], writes=[BK(bn_)])
        P.op("dve", lambda e: e.tensor_copy(out=nT, in_=banks[bn_][:, 0:64]), reads=[BK(bn_)], writes=["nT"])
        P.op("dve", lambda e: e.tensor_copy(out=Cs16[:, :, 128], in_=nT), reads=["nT", "bigfence"], writes=["Cs16n"])
        C32h = [C32q[:, 0:8, :], C32q[:, 8:16, :]]

        def loadC(e8):
            buf = C32h[e8 % 2]
            P.dma("pool", lambda e: [e.dma_start(out=buf, in_=C_s[8 * e8:8 * e8 + 8].rearrange("b d e -> d b e"))],
                  writes=[("C32h", e8 % 2)] + ([("xs", LP, 0), ("xs", LP, 1)] if e8 < 2 else []))
        loadC(0)
        for e8 in range(8):
            buf = C32h[e8 % 2]
            CK = ("C32h", e8 % 2)
            if e8 + 1 < 8:
                loadC(e8 + 1)
            P.op("act", lambda e, e8=e8, buf=buf: e.activation(out=Cs16[:, 8 * e8:8 * e8 + 8, 0:128], in_=buf, func=AF.Copy),
                 reads=[CK, "bigfence"], writes=[("Cs16", e8)])
            if e8 % 2 == 0:
                q4 = e8 // 2
                P.op("dve", lambda e, q4=q4: e.tensor_tensor(
                    out=ktz, in0=kt[0:64, 0, :].unsqueeze(1).broadcast_to([64, 4, 512]),
                    in1=onehot[:, 4 * q4:4 * q4 + 4].unsqueeze(2).broadcast_to([64, 4, 512]), op=ALU.mult),
                    reads=[("kt", 0), "onehot"], writes=["ktz", ("xs", SP_, 1)])
            for bl2 in range(2):
                b = 2 * e8 + bl2
                bl = b % 4
                bu1, bu2 = nb(), nb()

                def mmU(e, bl=bl, bu1=bu1, bu2=bu2):
                    for h in range(4):
                        dst = banks[bu1 if h < 2 else bu2][:, (h % 2) * 129:(h % 2) * 129 + 129]
                        ins = e.matmul(dst, lhsT=ktz[:, bl, h * 128:(h + 1) * 128], rhs=vmaug[0:64, 0, h, :], start=True, stop=True)
                    return ins
                P.op("pe", mmU, reads=["ktz", ("vmaug", 0)], writes=[BK(bu1), BK(bu2)])
                for half, bu in ((0, bu1), (1, bu2)):
                    i0_ = b * 4 + 2 * half
                    j0 = bl2 * 4 + 2 * half
                    uv = banks[bu][:, 0:258].rearrange("p (h d) -> p h d", d=129)
                    P.op("dve", lambda e, j0=j0, uv=uv, buf=buf: e.tensor_tensor(
                        out=buf[:, j0:j0 + 2, :], in0=buf[:, j0:j0 + 2, :], in1=uv[:, :, 0:128], op=ALU.add),
                        reads=[BK(bu), CK], writes=[CK])
                    P.op("dve", lambda e, i0_=i0_, uv=uv: e.tensor_tensor(
                        out=nTo[:, i0_:i0_ + 2], in0=nT[:, i0_:i0_ + 2], in1=uv[:, :, 128], op=ALU.add),
                        reads=[BK(bu), "nT"], writes=["nTo"])
                    for hh in range(2):
                        P.op("act", lambda e, i0_=i0_, j0=j0, buf=buf, hh=hh: e.activation(
                            out=buf[:, j0 + hh, :], in_=buf[:, j0 + hh, :], func=AF.Copy,
                            scale=alast[:, i0_ + hh:i0_ + hh + 1]), reads=["alast", CK], writes=[CK])
            P.dma("pool", lambda e, e8=e8, buf=buf: [e.dma_start(out=o_Cs[8 * e8:8 * e8 + 8].rearrange("b d e -> d b e"), in_=buf)],
                  reads=[CK])
        P.op("dve", lambda e: e.tensor_tensor(out=nTo, in0=nTo, in1=alast[:, 0:64], op=ALU.mult), reads=["nTo", "alast"], writes=["nTo"])
        bn2 = nb()
        P.op("pe", lambda e: e.transpose(out=banks[bn2][0:64, 0:128], in_=nTo, identity=idf), reads=["nTo", "idf"], writes=[BK(bn2)])
        P.op("dve", lambda e: e.tensor_copy(out=n32, in_=banks[bn2][0:64, 0:128]), reads=[BK(bn2)], writes=["n32"])
        P.dma("pool", lambda e: [e.dma_start(out=o_ns, in_=n32)], reads=["n32"])
        CS = [("Cs16", e8) for e8 in range(8)] + ["Cs16n"]
        bo1, bo2 = nb(), nb()
        for h in range(4):
            bs_ = nb()
            P.op("pe", lambda e, h=h, bs_=bs_: e.matmul(banks[bs_][0:64, 0:64], lhsT=ktT[:, h, 0:64], rhs=qmT[:, h, 0:64],
                                                        start=True, stop=True),
                 reads=[("ktT", 0), ("qmT", h)], writes=[BK(bs_)])
            P.op("dve", lambda e, h=h, bs_=bs_: e.tensor_tensor(out=PmT[0:64, h, 0:64], in0=banks[bs_][0:64, 0:64], in1=masks,
                                                                op=ALU.mult), reads=[BK(bs_), "masks"], writes=[("PmTs", h), "PmT"])
            P.op("dve", lambda e, h=h: e.tensor_tensor(out=qTpad, in0=qmT[:, h, 0:64].unsqueeze(1).broadcast_to([128, 16, 64]),
                                                       in1=blockmask, op=ALU.mult),
                 reads=[("qmT", h), "blockmask"], writes=["qTpad"])
            dst = banks[bo1 if h < 2 else bo2][0:64, (h % 2) * 129:(h % 2) * 129 + 129]

            def mmO(e, h=h, dst=dst):
                e.matmul(dst, lhsT=PmT[0:64, h, 0:64], rhs=vmaug[0:64, 0, h, :], start=True, stop=False)
                for b in range(16):
                    ins = e.matmul(dst, lhsT=qTpad[:, b, :], rhs=Cs16[:, b * 4 + h, :], start=False, stop=(b == 15))
                return ins
            P.op("pe", mmO, reads=[("PmTs", h), ("vmaug", 0), "qTpad"] + CS, writes=[BK(bo1 if h < 2 else bo2), ("bo_s", h)])
        mlstm_out_a(64, [(bo1, 0), (bo1, 1), (bo2, 0), (bo2, 1)], 0)
        mlstm_out_b(64, 0, 0)
        fence(CS, ["bigfence"])

    P.op("dve", lambda e: e.memset(stat, 0.0), writes=["stat", "bigfence"])
    for gi in range(NG):
        grp_state["idx"] = 4 if "A" in pf else 0
        run_group("P", gi)
        if grp_state["first"]:
            flush_stores(0)
            grp_state["first"] = False
            P.dma("pool", lambda e: [e.dma_start(out=kcb, in_=ck_s.rearrange("b k f -> k b f"))], writes=["kcb"])
            P.op("dve", lambda e: e.memset(vcaug.rearrange("p b g d -> p (b g d)"), 1.0), writes=["vcaug"])
            P.dma("pool", lambda e: [e.dma_start(out=vcaug[:, :, g, 0:64], in_=cv_s.rearrange("b k f -> k b f")[:, :, g * 64:(g + 1) * 64])
                                     for g in range(2)], writes=["vcaug"], n=2)
            P.dma("pool", lambda e: [e.dma_start(out=n32, in_=n_s)], writes=["n32"])
            P.dma("pool", lambda e: [e.dma_start(out=m0T, in_=m_s.rearrange("b h -> h b"), allow_slow_non_contiguous=True)],
                  writes=["m0T"])
            P.dma("pool", lambda e: [e.dma_start(out=o_ks[:, 0:124, :], in_=ck_s[:, 4:128, :])])
            P.dma("pool", lambda e: [e.dma_start(out=o_vs[:, 0:124, :], in_=cv_s[:, 4:128, :])])
    grp_state["idx"] = 4 if "A" in pf else 0
    run_group("S", 0)
    P.emit()
    return nc


_CACHE = {}


def kernel(**inputs):
    f32 = lambda a: np.ascontiguousarray(np.asarray(a, dtype=np.float32))
    inp = {k: f32(v) for k, v in inputs.items()}
    consts = _consts()
    if "nc" not in _CACHE:
        _CACHE["nc"] = build_nc()
    nc = _CACHE["nc"]
    in_maps = []
    for i in range(NCORES):
        sl = slice(16 * i, 16 * i + 16)
        m = {
            "x_p": inp["x_prompt"][i],
            "x_s": inp["x_sample"][sl].reshape(64, D),
            "ck_s": inp["cache_win_k"][0, sl].reshape(16, 128, 128),
            "cv_s": inp["cache_win_v"][0, sl].reshape(16, 128, 128),
            "C_s": inp["state_mlstm_C"][0, sl].reshape(64, 128, 128),
            "n_s": inp["state_mlstm_n"][0, sl].reshape(64, 128),
            "m_s": inp["state_mlstm_m"][0, sl].reshape(16, 4),
            "cst_s": inp["state_ffn_conv"][0, sl].reshape(32, F2),
            "p_p": inp["p_prompt"][0, i],
            "p_s": inp["p_sample"][0, sl].reshape(64, 256),
            "g_mix": inp["g_mix"], "w_in": inp["w_in"][0],
            "b_i": inp["b_i"].reshape(4, 1), "b_f": inp["b_f"].reshape(4, 1),
            "g_q": inp["g_q"], "g_k": inp["g_k"], "sinks": inp["sinks"], "g_hm": inp["g_hm"],
            "w_oa": inp["w_oa"][0], "w_om": inp["w_om"][0], "w_out": inp["w_out"][0],
            "g_ffn": inp["g_ffn"], "w_up": inp["w_up"][0], "conv_w": inp["conv_w"][0],
            "conv_b": inp["conv_b"], "w_down": inp["w_down"][0], "w_ple": inp["w_ple"][0],
            "w_pg": inp["w_ple_gate"][0],
        }
        for k, v in consts.items():
            m["c_" + k] = v
        in_maps.append({k: np.ascontiguousarray(v) for k, v in m.items()})
    res = run_bass_kernel_spmd(nc, in_maps, core_ids=list(range(NCORES)))
    r = res.results
    if DEBUG:
        _CACHE["dbg"] = {k: np.asarray(v) for k, v in r[0].items() if k.startswith("dbg_")}
    cat = lambda k: np.stack([np.asarray(r[i][k], dtype=np.float32) for i in range(NCORES)])
    catb = lambda k: np.concatenate([np.asarray(r[i][k], dtype=np.float32) for i in range(NCORES)], axis=0)
    y_p = cat("o_yp")
    y_s = catb("o_ys").reshape(128, 4, D)
    k_p = cat("o_kp").reshape(1, 8, 128, 2, 64)
    v_p = cat("o_vp").reshape(1, 8, 128, 2, 64)
    C_p = cat("o_Cp").reshape(1, 8, 4, 128, 128)
    n_p = cat("o_np").reshape(1, 8, 4, 128)
    m_p = cat("o_mp").reshape(1, 8, 4)
    c_p = cat("o_cp").reshape(1, 8, 2, F2)
    k_s = catb("o_ks").reshape(1, 128, 128, 2, 64)
    v_s = catb("o_vs").reshape(1, 128, 128, 2, 64)
    C_s = catb("o_Cs").reshape(1, 128, 4, 128, 128)
    n_s = catb("o_ns").reshape(1, 128, 4, 128)
    m_s = catb("o_ms").reshape(1, 128, 4)
    c_s = catb("o_cs").reshape(1, 128, 2, F2)
    return (y_p, y_s, k_p, v_p, C_p, n_p, m_p, c_p, k_s, v_s, C_s, n_s, m_s, c_s)
```

```python
import contextlib
import numpy as np
import concourse.bass as bass
import concourse.mybir as mybir
from concourse.bass_utils import run_bass_kernel_spmd

F32 = mybir.dt.float32
BF16 = mybir.dt.bfloat16
AF = mybir.ActivationFunctionType
ALU = mybir.AluOpType
AX = mybir.AxisListType

ENGS = ("pe", "act", "dve", "pool", "sp")
NCORES = 8
D = 1024
DIN = 4872
F2 = 5632
DFF = 2816
EPS = 1e-6
NEG = -30000.0


class Op:
    __slots__ = ("eng", "fn", "deps", "is_dma", "ndma", "needs_inc", "ordinal",
                 "sem", "target", "idx", "capwait")

    def __init__(self, eng, fn, is_dma, ndma):
        self.eng = eng
        self.fn = fn
        self.deps = []
        self.is_dma = is_dma
        self.ndma = ndma
        self.needs_inc = False
        self.ordinal = None
        self.sem = None
        self.target = None
        self.capwait = None


class Prog:
    NDMASEM = 16

    def __init__(self, nc):
        self.nc = nc
        self.ops = []
        self.last_w = {}
        self.readers = {}
        self.dma_count = {"sp": 0, "pool": 0, "act": 0}
        self.dma_hist = {"sp": [], "pool": [], "act": []}

    def _add(self, op, reads, writes):
        deps = set()
        for k in reads:
            w = self.last_w.get(k)
            if w is not None:
                deps.add(w)
        for k in writes:
            w = self.last_w.get(k)
            if w is not None:
                deps.add(w)
            for r in self.readers.get(k, ()):
                deps.add(r)
        deps.discard(op)
        op.deps = sorted(deps, key=lambda o: o.idx)
        for k in reads:
            self.readers.setdefault(k, []).append(op)
        for k in writes:
            self.last_w[k] = op
            self.readers[k] = []
        op.idx = len(self.ops)
        self.ops.append(op)
        return op

    def op(self, eng, fn, reads=(), writes=()):
        o = Op(eng, fn, False, 0)
        o.idx = len(self.ops)
        return self._add(o, reads, writes)

    def dma(self, q, fn, reads=(), writes=(), n=1):
        op = Op(q, fn, True, n)
        op.idx = len(self.ops)
        i = self.dma_count[q]
        self.dma_count[q] += 1
        op.sem = (q, i % self.NDMASEM)
        hist = self.dma_hist[q]
        prev = hist[i - self.NDMASEM] if i >= self.NDMASEM else None
        base = prev.target if prev is not None else 0
        op.target = base + 16 * n
        op.capwait = prev
        hist.append(op)
        return self._add(op, reads, writes)

    def emit(self):
        nc = self.nc
        for op in self.ops:
            for d in op.deps:
                if d.is_dma:
                    continue
                if d.eng == "pe" and op.eng == "pe" and not op.is_dma:
                    continue
                d.needs_inc = True
        cnt = {e: 0 for e in ENGS}
        for op in self.ops:
            if not op.is_dma and op.needs_inc:
                cnt[op.eng] += 1
                op.ordinal = cnt[op.eng]
        with contextlib.ExitStack() as st:
            esem = {e: st.enter_context(nc.semaphore("s_" + e)) for e in ENGS}
            dsem = {}
            for q in ("sp", "pool", "act"):
                for j in range(min(self.NDMASEM, self.dma_count[q])):
                    dsem[(q, j)] = st.enter_context(nc.semaphore("d_%s_%d" % (q, j)))
            block = st.enter_context(nc.Block())
            per_eng = {e: [o for o in self.ops if o.eng == e] for e in ENGS}

            def run(e, eng):
                waited = {}

                def wait(sem_key, sem, val):
                    if waited.get(sem_key, 0) >= val:
                        return
                    waited[sem_key] = val
                    eng.wait_ge(sem, val)

                for op in per_eng[e]:
                    for d in op.deps:
                        if d.is_dma:
                            wait(d.sem, dsem[d.sem], d.target)
                        else:
                            if d.eng == "pe" and e == "pe" and not op.is_dma:
                                continue
                            wait(d.eng, esem[d.eng], d.ordinal)
                    if op.is_dma:
                        if op.capwait is not None:
                            wait(op.capwait.sem, dsem[op.capwait.sem], op.capwait.target)
                        insts = op.fn(eng)
                        assert len(insts) == op.ndma, (len(insts), op.ndma)
                        for ins in insts:
                            ins.then_inc(dsem[op.sem], 16)
                    else:
                        ins = op.fn(eng)
                        if op.needs_inc:
                            ins.then_inc(esem[e], 1)
                if e in self.dma_hist:
                    for d in self.dma_hist[e][-self.NDMASEM:]:
                        wait(d.sem, dsem[d.sem], d.target)

            block.tensor(lambda eng: run("pe", eng))
            block.scalar(lambda eng: run("act", eng))
            block.vector(lambda eng: run("dve", eng))
            block.gpsimd(lambda eng: run("pool", eng))
            block.sync(lambda eng: run("sp", eng))


def _consts():
    c = {}
    c["ident"] = np.eye(128, dtype=np.float32)
    slopes = 2.0 ** (-(np.arange(8) + 1.0))
    i = np.arange(128)[None, :]
    j = np.arange(128)[:, None]
    ab = np.zeros((128, 2, 8, 128), np.float32)
    dprev = (i + 128 - j).astype(np.float32)
    dcur = (i - j).astype(np.float32)
    for h in range(8):
        ab[:, 0, h, :] = np.where(dprev <= 128, -slopes[h] * dprev, NEG)
        ab[:, 1, h, :] = np.where(dcur >= 0, -slopes[h] * dcur, NEG)
    c["abias"] = ab.reshape(128, 2 * 8 * 128)
    qi = np.arange(64)[None, :]
    kj = np.arange(64)[:, None]
    same = (qi // 4) == (kj // 4)
    ok = same & (kj <= qi)
    sb = np.zeros((64, 8, 64), np.float32)
    for h in range(8):
        sb[:, h, :] = np.where(ok, -slopes[h] * (qi - kj), NEG)
    c["sbias"] = sb.reshape(64, 8 * 64)
    c["maskp"] = (j <= i).astype(np.float32)
    c["masks"] = ok.astype(np.float32)
    oh = ((np.arange(64)[:, None] // 4) == np.arange(16)[None, :]).astype(np.float32)
    c["onehot"] = oh
    bm = ((np.arange(64)[None, :] // 4) == np.arange(16)[:, None]).astype(np.float32)
    c["blockmask"] = np.ascontiguousarray(np.broadcast_to(bm.reshape(1, 16 * 64), (128, 16 * 64)))
    c["ones"] = np.ones((4, 128), np.float32)
    return c


DEBUG = False


def build_nc():
    nc = bass.Bass("TRN2", target_bir_lowering=False)
    P = Prog(nc)

    def din(name, shape):
        return nc.dram_tensor(name, list(shape), F32, kind="ExternalInput").ap()

    def dout(name, shape):
        return nc.dram_tensor(name, list(shape), F32, kind="ExternalOutput").ap()

    x_p = din("x_p", (2048, D))
    x_s = din("x_s", (64, D))
    ck_s = din("ck_s", (16, 128, 128))
    cv_s = din("cv_s", (16, 128, 128))
    C_s = din("C_s", (64, 128, 128))
    n_s = din("n_s", (64, 128))
    m_s = din("m_s", (16, 4))
    cst_s = din("cst_s", (32, F2))
    p_p = din("p_p", (2048, 256))
    p_s = din("p_s", (64, 256))
    g_mix = din("g_mix", (1, D))
    w_in = din("w_in", (D, DIN))
    b_i = din("b_i", (4, 1))
    b_f = din("b_f", (4, 1))
    g_q = din("g_q", (1, 64))
    g_k = din("g_k", (1, 64))
    sinks = din("sinks", (1, 8))
    g_hm = din("g_hm", (1, 512))
    w_oa = din("w_oa", (512, D))
    w_om = din("w_om", (512, D))
    w_out = din("w_out", (D, D))
    g_ffn = din("g_ffn", (1, D))
    w_up = din("w_up", (D, F2))
    conv_w = din("conv_w", (3, F2))
    conv_b = din("conv_b", (1, F2))
    w_down = din("w_down", (DFF, D))
    w_ple = din("w_ple", (256, D))
    w_pg = din("w_pg", (D, D))
    c_ident = din("c_ident", (128, 128))
    c_abias = din("c_abias", (128, 2048))
    c_sbias = din("c_sbias", (64, 512))
    c_maskp = din("c_maskp", (128, 128))
    c_masks = din("c_masks", (64, 64))
    c_onehot = din("c_onehot", (64, 16))
    c_blockmask = din("c_blockmask", (128, 1024))
    c_ones = din("c_ones", (4, 128))

    o_yp = dout("o_yp", (2048, D))
    o_ys = dout("o_ys", (64, D))
    o_kp = dout("o_kp", (128, 128))
    o_vp = dout("o_vp", (128, 128))
    o_Cp = dout("o_Cp", (4, 128, 128))
    o_np = dout("o_np", (4, 128))
    o_mp = dout("o_mp", (4, 1))
    o_cp = dout("o_cp", (2, F2))
    o_ks = dout("o_ks", (16, 128, 128))
    o_vs = dout("o_vs", (16, 128, 128))
    o_Cs = dout("o_Cs", (64, 128, 128))
    o_ns = dout("o_ns", (64, 128))
    o_ms = dout("o_ms", (16, 4))
    o_cs = dout("o_cs", (32, F2))

    _cnt = [0]

    def dbg(name, ap, reads):
        if not DEBUG:
            return
        o = nc.dram_tensor("dbg_" + name, list(ap.shape), F32, kind="ExternalOutput").ap()
        q = "pool" if ap.dtype == BF16 else "sp"
        P.dma(q, lambda e: [e.dma_start(out=o, in_=ap)], reads=reads)

    def sb(shape, dt=F32, name=None):
        _cnt[0] += 1
        return nc.alloc_sbuf_tensor(name or ("t%d" % _cnt[0]), list(shape), dt).ap()

    banks = [nc.alloc_psum_tensor("bank%d" % i, [128, 512], F32).ap() for i in range(8)]
    bank_i = [0]

    def nb():
        i = bank_i[0] % 8
        bank_i[0] += 1
        return i

    def BK(i):
        return ("bank", i)

    idf = sb([128, 128]); idb = sb([128, 128], BF16)
    abias = sb([128, 2, 8, 128], BF16); sbias = sb([64, 8, 64], BF16)
    maskp = sb([128, 128]); masks = sb([64, 64]); onehot = sb([64, 16])
    blockmask = sb([128, 16, 64], BF16)
    ones4 = sb([4, 128])
    gmixT = sb([128, 8]); gffnT = sb([128, 8]); ghm_b = sb([128, 512])
    gqk_b = sb([128, 10, 64]); esink = sb([128, 8])
    bi_t = sb([4, 1]); nbf_t = sb([4, 1])
    cw = sb([128, 4, 44])
    cwraw = sb([44, 4, 128])

    def ld(dst, src, key, q="sp"):
        P.dma(q, lambda e: [e.dma_start(out=dst, in_=src)], writes=[key])

    ld(idf, c_ident, "idf")
    ld(maskp, c_maskp, "maskp")
    ld(masks, c_masks, "masks")
    ld(onehot, c_onehot, "onehot")
    ld(ones4, c_ones, "ones4")
    ld(abias.rearrange("p a h q -> p (a h q)"), c_abias, "abias", q="pool")
    ld(sbias.rearrange("p h q -> p (h q)"), c_sbias, "sbias", q="pool")
    ld(blockmask.rearrange("p b t -> p (b t)"), c_blockmask, "blockmask", q="pool")
    P.dma("sp", lambda e: [e.dma_start(out=gmixT, in_=g_mix[0].rearrange("(c p) -> p c", p=128), allow_slow_non_contiguous=True)],
          writes=["gmix"])
    P.dma("sp", lambda e: [e.dma_start(out=gffnT, in_=g_ffn[0].rearrange("(c p) -> p c", p=128), allow_slow_non_contiguous=True)],
          writes=["gffn"])
    ld(ghm_b, g_hm[0].partition_broadcast(128), "ghm")
    for h in range(8):
        ld(gqk_b[:, h, :], g_q[0].partition_broadcast(128), ("gqk", h))
    for h in range(2):
        ld(gqk_b[:, 8 + h, :], g_k[0].partition_broadcast(128), ("gqk", 8 + h))
    ld(esink, sinks[0].partition_broadcast(128), "esink")
    ld(bi_t, b_i, "bi")
    ld(nbf_t, b_f, "nbf")
    for j in range(3):
        ld(cwraw[:, j, :], conv_w[j].rearrange("(c p) -> c p", p=128), ("cwraw", j))
    ld(cwraw[:, 3, :], conv_b[0].rearrange("(c p) -> c p", p=128), ("cwraw", 3))

    P.op("dve", lambda e: e.tensor_copy(out=idb, in_=idf), reads=["idf"], writes=["idb"])
    P.op("dve", lambda e: e.tensor_scalar(out=gqk_b[:, 0:8, :], in0=gqk_b[:, 0:8, :], scalar1=0.125,
                                          scalar2=None, op0=ALU.mult),
         reads=[("gqk", h) for h in range(8)], writes=[("gqk", h) for h in range(8)])
    GQK = [("gqk", h) for h in range(10)]
    P.op("act", lambda e: e.activation(out=esink, in_=esink, func=AF.Exp), reads=["esink"], writes=["esink"])
    P.op("dve", lambda e: e.tensor_scalar(out=nbf_t, in0=nbf_t, scalar1=-1.0, scalar2=None, op0=ALU.mult),
         reads=["nbf"], writes=["nbf"])
    bcw = nb()

    def tr_cw(e):
        for j in range(4):
            ins = e.transpose(out=banks[bcw][:, j * 44:(j + 1) * 44], in_=cwraw[:, j, :], identity=idf[0:44, 0:44])
        return ins
    P.op("pe", tr_cw, reads=[("cwraw", j) for j in range(4)] + ["idf"], writes=[BK(bcw)])
    P.op("dve", lambda e: e.tensor_copy(out=cw.rearrange("p j c -> p (j c)"), in_=banks[bcw][:, 0:176]),
         reads=[BK(bcw)], writes=["cw"])

    NS = 8
    slots = [sb([128, 4096], BF16, name="slot%d" % i) for i in range(NS)]
    piece_i = [0]
    NPIECE = 40
    wscr = nc.dram_tensor("wscratch", [NPIECE, 128, 4096], BF16).ap()
    grp_state = {"first": True, "idx": 0, "pending": []}
    pf = {}
    STORE_LAG = 3

    def flush_stores(keep):
        while len(grp_state["pending"]) > keep:
            idx, sl, key, used = grp_state["pending"].pop(0)
            P.dma("pool", lambda e, idx=idx, sl=sl, used=used: [e.dma_start(out=wscr[idx][:, 0:used], in_=sl[:, 0:used])],
                  reads=[key], writes=[("scr", idx)])

    def wpiece(loads, used=4096):
        s = piece_i[0] % NS
        piece_i[0] += 1
        sl = slots[s]
        key = ("slot", s)
        idx = grp_state["idx"]
        grp_state["idx"] += 1
        assert idx < NPIECE
        if grp_state["first"]:
            P.dma("pool", lambda e: [e.dma_start(out=dv(sl), in_=src) for dv, src in loads],
                  writes=[key], n=len(loads))
            grp_state["pending"].append((idx, sl, key, used))
            flush_stores(STORE_LAG)
        else:
            P.dma("sp", lambda e: [e.dma_start(out=sl[:, 0:used], in_=wscr[idx][:, 0:used])], reads=[("scr", idx)], writes=[key])
        return sl, key

    def wview(sl, nk, ncols):
        return sl[:, 0:nk * ncols].rearrange("p (k n) -> p k n", n=ncols)

    w_in_v = w_in.rearrange("(k p) n -> p k n", p=128)
    w_up_v = w_up.rearrange("(k p) n -> p k n", p=128)
    w_out_v = w_out.rearrange("(k p) n -> p k n", p=128)
    w_pg_v = w_pg.rearrange("(k p) n -> p k n", p=128)
    w_oa_v = w_oa.rearrange("(k p) n -> p k n", p=128)
    w_om_v = w_om.rearrange("(k p) n -> p k n", p=128)
    w_ple_v = w_ple.rearrange("(k p) n -> p k n", p=128)
    w_down_v = w_down.rearrange("(k p) n -> p k n", p=128)

    TMAX = 256
    NTP = 2
    NG = 2048 // TMAX
    xs2 = [sb([128, NTP, D]) for _ in range(2)]
    ps_t = sb([128, NTP, 256])
    xnTf2 = [sb([128, 8, TMAX + 2], BF16) for _ in range(2)]
    ubb = [sb([128, TMAX + 2]) for _ in range(2)]
    xnb = sb([128, D], BF16)
    stat = sb([128, 16])
    tokA = sb([128, 768])
    qkn = sb([128, 640]); qknb2 = [sb([128, 640], BF16) for _ in range(2)]
    qaT = sb([64, 8, TMAX], BF16)
    kaT = sb([64, 2, (NTP + 1) * 128], BF16)
    vaug = sb([128, NTP + 1, 2, 65], BF16)
    PT = [sb([128, 512], BF16) for _ in range(2)]
    ya = sb([128, 512], BF16)
    yaT = sb([128, 4, TMAX], BF16)
    qmT = sb([128, 4, TMAX], BF16)
    kt = sb([128, NTP, 512], BF16)
    ktT = sb([128, 4, TMAX], BF16)
    vmaug = sb([128, NTP, 4, 129], BF16)
    osig = sb([128, NTP, 512], BF16)
    igs = sb([4, TMAX]); lfs = sb([4, TMAX]); Fb = sb([4, TMAX]); mbuf = sb([4, 1 + TMAX])
    Ab = sb([4, TMAX]); tmp4 = sb([4, TMAX])
    Y4 = sb([4, 64])
    scal = sb([128, NTP, 12])
    alast = sb([128, 64])
    Shat = sb([128, 4, 129]); Cb = sb([128, 4, 129], BF16)
    PmT = sb([128, 4, 128], BF16)
    hraw = sb([128, 4, 129]); hrawB = sb([128, 4, 129]); hraw2 = [hraw, hrawB]; hn = sb([128, 512]); hsq = qkn[:, 0:512]
    hm2 = [sb([128, 512], BF16) for _ in range(2)]
    hmT = sb([128, 4, TMAX], BF16)
    big = sb([128, 8256], BF16)
    mixedT = big[:, 0:8 * TMAX].rearrange("p (k t) -> p k t", t=TMAX)
    hT = big[:, 0:22 * TMAX].rearrange("p (k t) -> p k t", t=TMAX)
    sgaT = big[:, 8 * TMAX:16 * TMAX].rearrange("p (k t) -> p k t", t=TMAX)
    sgmT = big[:, 16 * TMAX:24 * TMAX].rearrange("p (k t) -> p k t", t=TMAX)
    sg1 = sb([128, TMAX]); sg2 = sb([128, TMAX])
    yb = [[sb([128, TMAX]) for _ in range(2)] for _ in range(2)]
    gact = [yb[0][0], yb[0][1]]
    convst = sb([128, 2, 44])
    pb16 = sb([128, 256], BF16)
    pT = sb([128, 2, TMAX], BF16)
    x2b = xnb
    junk = xnb

    P.op("dve", lambda e: e.memset(vaug.rearrange("p a g d -> p (a g d)"), 1.0), writes=["vaug_init"])
    P.op("dve", lambda e: e.memset(vmaug.rearrange("p a g d -> p (a g d)"), 1.0), writes=["vmaug_init"])
    P.op("dve", lambda e: e.memset(mbuf, 0.0), writes=["mbuf"])
    for par_ in range(2):
        P.op("dve", lambda e, par_=par_: e.memset(xnTf2[par_][:, :, 0:2], 0.0), writes=[("xcarry", par_)])
    P.op("dve", lambda e: e.memset(convst.rearrange("p j c -> p (j c)"), 0.0), writes=[("convst", cc) for cc in range(44)])

    kcb = sb([128, 16, 128], BF16)
    vcaug = sb([128, 16, 2, 65], BF16)
    kcT2 = [sb([64, 256], BF16) for _ in range(2)]
    PTpad = big[:, 0:8192].rearrange("p (b h q) -> p b h q", h=8, q=64)
    LP = (NG - 1) % 2
    SP_ = NG % 2
    C32q = xs2[LP].rearrange("p a b -> p (a b)").rearrange("p (b d) -> p b d", d=128); Cs16 = big[:, 0:8256].rearrange("p (b d) -> p b d", d=129)
    n32 = sb([64, 128]); nT = sb([128, 64]); nTo = sb([128, 64])
    Snewq = C32q
    m0T = sb([4, 16])
    qTpad = sb([128, 16, 64], BF16)
    ktz = xs2[SP_][0:64, 1, :].bitcast(BF16).rearrange("p (b f) -> p b f", f=512)
    stg_in = [cwraw[0:32].rearrange("p a b -> p (a b)"), hn[0:32, :]]
    stg_out = [hn[0:32, :]] * 2

    junk2 = qkn.bitcast(BF16)[:, 0:D]

    def rmsnorm_T(rows, t, g_b, gkey, xkey, dstT, dkey, xs):
        rms_chain(rows, t, xkey, xs)
        rms_tr(rows, t, g_b, gkey, dstT, dkey)

    def rms_tr(rows, t, g_b, gkey, dstT, dkey):
        transpose_to(xnb, rows, 8, dstT, slice(t * 128, t * 128 + rows), "xnb", dkey, gT=g_b, gkey=gkey)

    def rms_chain(rows, t, xkey, xs):
        P.op("act", lambda e: e.activation(out=junk2[:rows], in_=xs[:rows, t, :], func=AF.Square,
                                           accum_out=stat[:rows, 0:1]),
             reads=[xkey], writes=["qkn", "stat"])
        P.op("dve", lambda e: e.tensor_scalar(out=stat[:rows, 1:2], in0=stat[:rows, 0:1], scalar1=1.0 / D,
                                              scalar2=EPS, op0=ALU.mult, op1=ALU.add),
             reads=["stat"], writes=["stat"])
        P.op("act", lambda e: e.activation(out=stat[:rows, 2:3], in_=stat[:rows, 1:2], func=AF.Ln),
             reads=["stat"], writes=["stat"])
        P.op("act", lambda e: e.activation(out=stat[:rows, 3:4], in_=stat[:rows, 2:3], func=AF.Exp, scale=-0.5),
             reads=["stat"], writes=["stat"])
        P.op("dve", lambda e: e.tensor_scalar(out=xnb[:rows], in0=xs[:rows, t, :], scalar1=stat[:rows, 3:4],
                                              scalar2=None, op0=ALU.mult),
             reads=[xkey, "stat"], writes=["xnb"])

    def transpose_to(src, rows, nchunk, dstT, tsl, skey, dkey, gT=None, gkey=None):
        b = nb()
        pv = banks[b].bitcast(BF16)

        def tr(e):
            for c in range(nchunk):
                ins = e.transpose(out=pv[:, c * 128:c * 128 + rows], in_=src[:rows, c * 128:(c + 1) * 128],
                                  identity=idb[:rows, :rows])
            return ins
        skeys = list(skey) if isinstance(skey, list) else [skey]
        P.op("pe", tr, reads=skeys + ["idb"], writes=[BK(b)])
        pin = pv[:, 0:nchunk * 128].rearrange("p (c t) -> p c t", t=128)[:, :, 0:rows]
        if gT is None:
            P.op("act", lambda e: e.activation(out=dstT[:, 0:nchunk, tsl], in_=pin, func=AF.Copy),
                 reads=[BK(b)], writes=[dkey])
        else:
            P.op("dve", lambda e: e.tensor_tensor(out=dstT[:, 0:nchunk, tsl], in0=pin,
                                                  in1=gT.unsqueeze(2).broadcast_to([128, nchunk, rows]), op=ALU.mult),
                 reads=[BK(b), gkey], writes=[dkey])

    def load_x(kind, gi, t, xs, key):
        samp_ = kind == "S"
        R_ = 64 if samp_ else 128
        src = x_s if samp_ else x_p
        r0 = 0 if samp_ else gi * TMAX + t * 128
        P.dma("sp", lambda e: [e.dma_start(out=xs[:R_, t, :], in_=src[r0:r0 + R_, :])], writes=[key])

    def load_p(kind, gi, t):
        samp_ = kind == "S"
        R_ = 64 if samp_ else 128
        psrc_ = p_s if samp_ else p_p
        r0 = 0 if samp_ else gi * TMAX + t * 128
        P.dma("sp", lambda e: [e.dma_start(out=ps_t[:R_, t, :], in_=psrc_[r0:r0 + R_, :])], writes=[("ps", t)])

    def fence(old, new):
        P.op("dve", lambda e: e.memset(stat[:, 15:16], 0.0), writes=list(old) + list(new) + ["statf"])

    def run_group(kind, gi):
        samp = kind == "S"
        NT = 1 if samp else NTP
        TS = 64 if samp else 128
        T = NT * TS
        R = TS
        xsrc = x_s if samp else x_p
        psrc = p_s if samp else p_p
        yout = o_ys if samp else o_yp
        row0 = 0 if samp else gi * TMAX
        seq = NG if samp else gi
        gpar = seq % 2
        xs = xs2[gpar]
        xnT_full = xnTf2[gpar]
        xnT = xnT_full[:, :, 2:2 + TMAX]
        xky = lambda t: ("xs", gpar, t)
        nky = lambda t: ("xnT", gpar, t)
        XK = [xky(t) for t in range(NT)]

        if gi == 0 and not samp:
            for t in range(NT):
                load_x("P", 0, t, xs, xky(t))
                load_p("P", 0, t)
            for t in range(NT):
                rmsnorm_T(R, t, gmixT, "gmix", xky(t), xnT, nky(t), xs)
        XNT = [nky(t) for t in range(NT)]

        if "A" in pf:
            (wA1, kA1), (wA2, kA2) = pf.pop("A")
        else:
            wA1, kA1 = wpiece([(lambda sl: wview(sl, 8, 512), w_in_v[:, :, 0:512])])
            wA2, kA2 = wpiece([(lambda sl: wview(sl, 8, 256), w_in_v[:, :, 512:768])], used=2048)
        wA1v = wview(wA1, 8, 512)
        wA2v = wview(wA2, 8, 256)
        cur_slot = 0 if samp else None
        for t in range(NT):
            tsl = slice(t * 128, t * 128 + R)
            b1, b2 = nb(), nb()

            def mmA(e, t=t, tsl=tsl, b1=b1, b2=b2):
                for k in range(8):
                    e.matmul(banks[b1][:R, :], lhsT=xnT[:, k, tsl], rhs=wA1v[:, k, :], start=(k == 0), stop=(k == 7))
                for k in range(8):
                    ins = e.matmul(banks[b2][:R, 0:256], lhsT=xnT[:, k, tsl], rhs=wA2v[:, k, :],
                                   start=(k == 0), stop=(k == 7))
                return ins
            P.op("pe", mmA, reads=[nky(t), kA1, kA2], writes=[BK(b1), BK(b2)])
            P.op("act", lambda e, b1=b1: e.activation(out=tokA[:R, 0:512], in_=banks[b1][:R, :], func=AF.Copy),
                 reads=[BK(b1)], writes=["tokA0"])
            P.op("dve", lambda e, b2=b2: e.tensor_copy(out=tokA[:R, 512:768], in_=banks[b2][:R, 0:256]),
                 reads=[BK(b2)], writes=["tokA1"])
            TK = ["tokA0", "tokA1"]
            P.op("dve", lambda e: e.tensor_tensor(out=qkn[:R], in0=tokA[:R, 0:640], in1=tokA[:R, 0:640], op=ALU.mult),
                 reads=TK, writes=["qkn"])
            P.op("dve", lambda e: e.tensor_reduce(out=stat[:R, 4:14], in_=qkn[:R].rearrange("p (h d) -> p h d", d=64),
                                                  axis=AX.X, op=ALU.add),
                 reads=["qkn"], writes=["stat"])
            P.op("dve", lambda e: e.tensor_scalar(out=stat[:R, 4:14], in0=stat[:R, 4:14], scalar1=1.0 / 64, scalar2=EPS,
                                                  op0=ALU.mult, op1=ALU.add), reads=["stat"], writes=["stat"])
            P.op("act", lambda e: e.activation(out=stat[:R, 4:14], in_=stat[:R, 4:14], func=AF.Ln),
                 reads=["stat"], writes=["stat"])
            P.op("act", lambda e: e.activation(out=stat[:R, 4:14], in_=stat[:R, 4:14], func=AF.Exp, scale=-0.5),
                 reads=["stat"], writes=["stat"])
            P.op("dve", lambda e: e.tensor_tensor(out=qkn[:R].rearrange("p (h d) -> p h d", d=64),
                                                  in0=tokA[:R, 0:640].rearrange("p (h d) -> p h d", d=64),
                                                  in1=stat[:R, 4:14].unsqueeze(2).broadcast_to([R, 10, 64]), op=ALU.mult),
                 reads=TK + ["stat", "qkn"], writes=["qkn"])
            P.op("dve", lambda e: e.tensor_tensor(out=qkn[:R], in0=qkn[:R], in1=gqk_b[:R].rearrange("p h d -> p (h d)"),
                                                  op=ALU.mult), reads=["qkn"] + GQK, writes=["qkn"])
            qknb = qknb2[t % 2]
            P.op("act", lambda e, qknb=qknb: e.activation(out=qknb[:R], in_=qkn[:R], func=AF.Copy), reads=["qkn"], writes=[("qknb", t % 2)])
            slot = 0 if samp else t + 1
            P.op("dve", lambda e, slot=slot: e.tensor_copy(
                out=vaug[:R, slot, :, 0:64], in_=tokA[:R, 640:768].rearrange("p (g d) -> p g d", d=64)),
                reads=TK + ["vaug_init"], writes=[("vaug", slot)])
            if samp:
                pass
            elif gi == NG - 1 and t == NTP - 1:
                P.dma("pool", lambda e: [e.dma_start(out=o_kp, in_=qkn[:, 512:640])], reads=["qkn"])
                P.dma("pool", lambda e: [e.dma_start(out=o_vp, in_=tokA[:, 640:768])], reads=TK)


        if "I" in pf:
            wI, kI = pf.pop("I")
        else:
            wI, kI = wpiece([(lambda sl: wview(sl, 8, 8), w_in_v[:, :, 2816:2824])], used=64)
        wIv = wview(wI, 8, 8)
        bi_, bf_ = nb(), nb()

        def mmIF(e):
            for k in range(8):
                e.matmul(banks[bi_][0:4, 0:T], lhsT=wIv[:, k, 0:4], rhs=xnT[:, k, 0:T], start=(k == 0), stop=(k == 7))
            for k in range(8):
                ins = e.matmul(banks[bf_][0:4, 0:T], lhsT=wIv[:, k, 4:8], rhs=xnT[:, k, 0:T], start=(k == 0), stop=(k == 7))
            return ins
        P.op("pe", mmIF, reads=XNT + [kI], writes=[BK(bi_), BK(bf_)])
        P.op("act", lambda e: e.activation(out=igs[:, 0:T], in_=banks[bi_][0:4, 0:T], func=AF.Identity, bias=bi_t),
             reads=[BK(bi_), "bi"], writes=["igs"])
        P.op("act", lambda e: e.activation(out=lfs[:, 0:T], in_=banks[bf_][0:4, 0:T], func=AF.Exp, scale=-1.0, bias=nbf_t),
             reads=[BK(bf_), "nbf"], writes=["lfs"])
        P.op("act", lambda e: e.activation(out=lfs[:, 0:T], in_=lfs[:, 0:T], func=AF.Ln, bias=1.0),
             reads=["lfs"], writes=["lfs"])
        P.op("dve", lambda e: e.tensor_scalar(out=lfs[:, 0:T], in0=lfs[:, 0:T], scalar1=-1.0, scalar2=None, op0=ALU.mult),
             reads=["lfs"], writes=["lfs"])
        if not samp:
            for t in range(NT):
                tsl = slice(t * 128, (t + 1) * 128)
                P.op("dve", lambda e, tsl=tsl: e.tensor_tensor_scan(out=Fb[:, tsl], data0=ones4[:, 0:128], data1=lfs[:, tsl],
                                                                    initial=0.0, op0=ALU.mult, op1=ALU.add),
                     reads=["lfs", "ones4"], writes=["Fb"])
                P.op("dve", lambda e, t=t, tsl=tsl: e.tensor_tensor_scan(
                    out=mbuf[:, 1 + t * 128:1 + (t + 1) * 128], data0=lfs[:, tsl], data1=igs[:, tsl],
                    initial=mbuf[:, t * 128:t * 128 + 1], op0=ALU.add, op1=ALU.max),
                    reads=["lfs", "igs", "mbuf"], writes=["mbuf"])
                P.op("dve", lambda e, t=t, tsl=tsl: e.scalar_tensor_tensor(
                    out=tmp4[:, tsl], in0=Fb[:, tsl], scalar=mbuf[:, t * 128:t * 128 + 1], in1=igs[:, tsl],
                    op0=ALU.add, op1=ALU.subtract), reads=["Fb", "mbuf", "igs"], writes=["nBb"])
                P.op("dve", lambda e, t=t, tsl=tsl: e.scalar_tensor_tensor(
                    out=Ab[:, tsl], in0=Fb[:, tsl], scalar=mbuf[:, t * 128:t * 128 + 1],
                    in1=mbuf[:, 1 + t * 128:1 + (t + 1) * 128], op0=ALU.add, op1=ALU.subtract),
                    reads=["Fb", "mbuf"], writes=["Ab"])
            nBsrc = tmp4
            msrc = lambda tsl: mbuf[:, 1 + tsl.start:1 + tsl.stop]
        else:
            Fv = Fb[:, 0:64].rearrange("p (b t) -> p b t", t=4)
            lv = lfs[:, 0:64].rearrange("p (b t) -> p b t", t=4)
            iv = igs[:, 0:64].rearrange("p (b t) -> p b t", t=4)
            mv = mbuf[:, 1:65].rearrange("p (b t) -> p b t", t=4)
            P.op("dve", lambda e: e.tensor_copy(out=Fv[:, :, 0], in_=lv[:, :, 0]), reads=["lfs"], writes=["Fb"])
            for j in range(1, 4):
                P.op("dve", lambda e, j=j: e.tensor_tensor(out=Fv[:, :, j], in0=Fv[:, :, j - 1], in1=lv[:, :, j], op=ALU.add),
                     reads=["lfs", "Fb"], writes=["Fb"])
            for j in range(4):
                prev = m0T if j == 0 else mv[:, :, j - 1]
                P.op("dve", lambda e, j=j, prev=prev: e.tensor_tensor(out=mv[:, :, j], in0=prev, in1=lv[:, :, j], op=ALU.add),
                     reads=["lfs", "m0T", "mbuf"], writes=["mbuf"])
                P.op("dve", lambda e, j=j: e.tensor_tensor(out=mv[:, :, j], in0=mv[:, :, j], in1=iv[:, :, j], op=ALU.max),
                     reads=["igs", "mbuf"], writes=["mbuf"])
            m0b = m0T.unsqueeze(2).broadcast_to([4, 16, 4])
            P.op("dve", lambda e: e.tensor_tensor(out=Ab[:, 0:64].rearrange("p (b t) -> p b t", t=4), in0=Fv, in1=m0b,
                                                  op=ALU.add), reads=["Fb", "m0T"], writes=["Ab"])
            P.op("dve", lambda e: e.tensor_tensor(out=tmp4[:, 0:64], in0=Ab[:, 0:64], in1=igs[:, 0:64], op=ALU.subtract),
                 reads=["Ab", "igs"], writes=["nBb"])
            P.op("dve", lambda e: e.tensor_tensor(out=Ab[:, 0:64], in0=Ab[:, 0:64], in1=mbuf[:, 1:65], op=ALU.subtract),
                 reads=["Ab", "mbuf"], writes=["Ab"])
            nBsrc = tmp4
            msrc = lambda tsl: mbuf[:, 1 + tsl.start:1 + tsl.stop]
            P.dma("pool", lambda e: [e.dma_start(out=o_ms.rearrange("b h -> h b"),
                                               in_=mbuf[:, 1:65].rearrange("p (b t) -> p b t", t=4)[:, :, 3],
                                               allow_slow_non_contiguous=True)], reads=["mbuf"])
        if "Q" in pf:
            wQ, kQ = pf.pop("Q")
        else:
            wQ, kQ = wpiece([(lambda sl: wview(sl, 8, 512), w_in_v[:, :, 768:1280])])
        wQv = wview(wQ, 8, 512)
        for c in range(4):
            bq_ = nb()

            def mmQ(e, c=c, bq_=bq_):
                for k in range(8):
                    ins = e.matmul(banks[bq_][:, 0:T], lhsT=wQv[:, k, c * 128:(c + 1) * 128], rhs=xnT[:, k, 0:T],
                                   start=(k == 0), stop=(k == 7))
                return ins
            P.op("pe", mmQ, reads=XNT + [kQ], writes=[BK(bq_)])
            P.op("act", lambda e, c=c, bq_=bq_: e.activation(out=qmT[:, c, 0:T], in_=banks[bq_][:, 0:T], func=AF.Copy),
                 reads=[BK(bq_)], writes=[("qmT", c)])
        wV, kV = wpiece([(lambda sl: wview(sl, 8, 512), w_in_v[:, :, 1792:2304])])
        wO_, kO_ = wpiece([(lambda sl: wview(sl, 8, 512), w_in_v[:, :, 2304:2816])])
        wVv = wview(wV, 8, 512)
        wOv = wview(wO_, 8, 512)
        for t in range(NT):
            tsl = slice(t * 128, t * 128 + R)
            bv_, bo_ = nb(), nb()

            def mmV(e, tsl=tsl, bv_=bv_, bo_=bo_):
                for k in range(8):
                    e.matmul(banks[bv_][:R, :], lhsT=xnT[:, k, tsl], rhs=wVv[:, k, :], start=(k == 0), stop=(k == 7))
                for k in range(8):
                    ins = e.matmul(banks[bo_][:R, :], lhsT=xnT[:, k, tsl], rhs=wOv[:, k, :], start=(k == 0), stop=(k == 7))
                return ins
            P.op("pe", mmV, reads=[nky(t), kV, kO_], writes=[BK(bv_), BK(bo_)])
            P.op("act", lambda e, t=t, bv_=bv_: e.activation(
                out=vmaug[:R, t, :, 0:128], in_=banks[bv_][:R, :].rearrange("p (h d) -> p h d", d=128), func=AF.Copy),
                reads=[BK(bv_), "vmaug_init"], writes=[("vmaug", t)])
            P.op("act", lambda e, t=t, bo_=bo_: e.activation(out=osig[:R, t, :], in_=banks[bo_][:R, :], func=AF.Sigmoid),
                 reads=[BK(bo_)], writes=[("osig", t)])

        for t in range(NT):
            tsl = slice(t * 128, t * 128 + R)
            qknb = qknb2[t % 2]
            slot = 0 if samp else t + 1
            bq, bk = nb(), nb()
            pq = banks[bq].bitcast(BF16)
            pk = banks[bk].bitcast(BF16)

            def trq(e, pq=pq, pk=pk, qknb=qknb):
                for h in range(8):
                    e.transpose(out=pq[0:64, h * 128:h * 128 + R], in_=qknb[:R, h * 64:(h + 1) * 64], identity=idb[:R, :R])
                for h in range(2):
                    ins = e.transpose(out=pk[0:64, h * 128:h * 128 + R], in_=qknb[:R, 512 + h * 64:512 + (h + 1) * 64],
                                      identity=idb[:R, :R])
                return ins
            P.op("pe", trq, reads=[("qknb", t % 2), "idb"], writes=[BK(bq), BK(bk)])
            P.op("act", lambda e, pq=pq, tsl=tsl: e.activation(
                out=qaT[:, :, tsl], in_=pq[0:64, :].rearrange("p (h t) -> p h t", t=128)[:, :, 0:R], func=AF.Copy),
                reads=[BK(bq)], writes=[("qaT", t)])
            P.op("dve", lambda e, pk=pk, slot=slot: e.tensor_copy(
                out=kaT[:, :, slot * 128:slot * 128 + R],
                in_=pk[0:64, 0:256].rearrange("p (h t) -> p h t", t=128)[:, :, 0:R]),
                reads=[BK(bk)], writes=[("kaT", slot)])
        wK, kK = wpiece([(lambda sl: wview(sl, 8, 512), w_in_v[:, :, 1280:1792])])
        wKv = wview(wK, 8, 512)
        kbanks = []
        for t in range(NT):
            tsl = slice(t * 128, t * 128 + R)
            bk_ = nb()
            kbanks.append(bk_)

            def mmK(e, tsl=tsl, bk_=bk_):
                for k in range(8):
                    ins = e.matmul(banks[bk_][:R, :], lhsT=xnT[:, k, tsl], rhs=wKv[:, k, :], start=(k == 0), stop=(k == 7))
                return ins
            P.op("pe", mmK, reads=[nky(t), kK], writes=[BK(bk_)])
        for t in range(NT):
            tsl = slice(t * 128, t * 128 + R)
            bsc = nb()

            def trs(e, tsl=tsl, bsc=bsc):
                e.transpose(out=banks[bsc][:R, 0:4], in_=Ab[:, tsl], identity=idf[0:4, 0:4])
                e.transpose(out=banks[bsc][:R, 4:8], in_=nBsrc[:, tsl], identity=idf[0:4, 0:4])
                return e.transpose(out=banks[bsc][:R, 8:12], in_=msrc(tsl), identity=idf[0:4, 0:4])
            P.op("pe", trs, reads=["Ab", "nBb", "mbuf", "idf"], writes=[BK(bsc)])
            P.op("act", lambda e, t=t, bsc=bsc: e.activation(out=scal[:R, t, 0:4], in_=banks[bsc][:R, 0:4], func=AF.Exp),
                 reads=[BK(bsc)], writes=[("scal", t, 0)])
            P.op("act", lambda e, t=t, bsc=bsc: e.activation(out=scal[:R, t, 4:12], in_=banks[bsc][:R, 4:12],
                                                             func=AF.Exp, scale=-1.0),
                 reads=[BK(bsc)], writes=[("scal", t, 1)])
            P.op("dve", lambda e, t=t: e.tensor_scalar(out=scal[:R, t, 4:8], in0=scal[:R, t, 4:8], scalar1=128.0 ** -0.5,
                                                       scalar2=None, op0=ALU.mult),
                 reads=[("scal", t, 1)], writes=[("scal", t, 1)])
        nY = NT * 4 if not samp else 64
        if not samp:
            alv = Ab[:, 0:T].rearrange("p (t s) -> p t s", s=128)[:, :, 127:128].broadcast_to([4, NT, 4])
            idv = idf[0:4, 0:4].unsqueeze(1).broadcast_to([4, NT, 4])
            yv = Y4[:, 0:nY].rearrange("p (t n) -> p t n", n=4)
        else:
            alv = Ab[:, 0:64].rearrange("p (b s) -> p b s", s=4)[:, :, 3:4].broadcast_to([4, 16, 4])
            idv = idf[0:4, 0:4].unsqueeze(1).broadcast_to([4, 16, 4])
            yv = Y4[:, 0:64].rearrange("p (b n) -> p b n", n=4)
        P.op("dve", lambda e: e.tensor_tensor(out=yv, in0=idv, in1=alv, op=ALU.mult), reads=["Ab", "idf"], writes=["Y4"])
        bal = nb()
        P.op("pe", lambda e: e.matmul(banks[bal][:, 0:nY], lhsT=ones4[:, :], rhs=Y4[:, 0:nY], start=True, stop=True),
             reads=["Y4", "ones4"], writes=[BK(bal)])
        P.op("act", lambda e: e.activation(out=alast[:, 0:nY], in_=banks[bal][:, 0:nY], func=AF.Exp),
             reads=[BK(bal)], writes=["alast"])

        for t in range(NT):
            tsl = slice(t * 128, t * 128 + R)
            bk_ = kbanks[t]
            P.op("dve", lambda e, t=t, bk_=bk_: e.tensor_tensor(
                out=kt[:R, t, :].rearrange("p (h d) -> p h d", d=128),
                in0=banks[bk_][:R, :].rearrange("p (h d) -> p h d", d=128),
                in1=scal[:R, t, 4:8].unsqueeze(2).broadcast_to([R, 4, 128]), op=ALU.mult),
                reads=[BK(bk_), ("scal", t, 1)], writes=[("kt", t)])
            transpose_to(kt[:, t, :], R, 4, ktT, tsl, ("kt", t), ("ktT", t))
        s8_list = []
        mo_list = []
        if not samp:
            for t in range(NT):
                gt = gi * NTP + t
                tsl = slice(t * 128, (t + 1) * 128)
                bs_ = nb()

                def mmST(e, tsl=tsl, bs_=bs_):
                    for h in range(4):
                        ins = e.matmul(banks[bs_][:, h * 128:(h + 1) * 128], lhsT=ktT[:, h, tsl], rhs=qmT[:, h, tsl],
                                       start=True, stop=True)
                    return ins
                P.op("pe", mmST, reads=[("ktT", t)] + [("qmT", c) for c in range(4)], writes=[BK(bs_)])
                P.op("dve", lambda e, bs_=bs_: e.tensor_tensor(
                    out=PmT, in0=banks[bs_].rearrange("p (h t) -> p h t", t=128),
                    in1=maskp.unsqueeze(1).broadcast_to([128, 4, 128]), op=ALU.mult),
                    reads=[BK(bs_), "maskp"], writes=["PmT"])
                bu1, bu2 = nb(), nb()

                def mmU(e, t=t, bu1=bu1, bu2=bu2):
                    for h in range(4):
                        dst = banks[bu1 if h < 2 else bu2][:, (h % 2) * 129:(h % 2) * 129 + 129]
                        ins = e.matmul(dst, lhsT=kt[:, t, h * 128:(h + 1) * 128], rhs=vmaug[:, t, h, :], start=True, stop=True)
                    return ins
                P.op("pe", mmU, reads=[("kt", t), ("vmaug", t)], writes=[BK(bu1), BK(bu2)])
                bo1, bo2 = nb(), nb()

                def mmO(e, tsl=tsl, t=t, gt=gt, bo1=bo1, bo2=bo2):
                    for h in range(4):
                        dst = banks[bo1 if h < 2 else bo2][:, (h % 2) * 129:(h % 2) * 129 + 129]
                        if gt > 0:
                            e.matmul(dst, lhsT=qmT[:, h, tsl], rhs=Cb[:, h, :], start=True, stop=False)
                        ins = e.matmul(dst, lhsT=PmT[:, h, :], rhs=vmaug[:, t, h, :], start=(gt == 0), stop=True)
                    return ins
                P.op("pe", mmO, reads=["PmT", ("vmaug", t), "Cb"] + [("qmT", c) for c in range(4)],
                     writes=[BK(bo1), BK(bo2)])
                for h in range(4):
                    src = banks[bu1 if h < 2 else bu2][:, (h % 2) * 129:(h % 2) * 129 + 129]
                    if gt == 0:
                        P.op("dve", lambda e, h=h, src=src: e.tensor_copy(out=Shat[:, h, :], in_=src),
                             reads=[BK(bu1 if h < 2 else bu2)], writes=[("Shat", h)])
                    else:
                        pcol = 0
                        P.op("dve", lambda e, h=h, src=src, pcol=pcol: e.scalar_tensor_tensor(
                            out=Shat[:, h, :], in0=Shat[:, h, :], scalar=alprev[:, h:h + 1], in1=src,
                            op0=ALU.mult, op1=ALU.add),
                            reads=[BK(bu1 if h < 2 else bu2), ("Shat", h), "alprev"], writes=[("Shat", h)])
                P.op("dve", lambda e, t=t: e.tensor_copy(out=alprev, in_=alast[:, t * 4:t * 4 + 4]),
                     reads=["alast"] + [("Shat", h) for h in range(4)], writes=["alprev"])
                for h in range(4):
                    P.op("act", lambda e, h=h: e.activation(out=Cb[:, h, :], in_=Shat[:, h, :], func=AF.Copy,
                                                            scale=alprev[:, h:h + 1]),
                         reads=[("Shat", h), "alprev"], writes=["Cb"])
                mlstm_out_a(128, [(bo1, 0), (bo1, 1), (bo2, 0), (bo2, 1)], t % 2)
                mo_list.append(t)
        else:
            sample_mlstm()

        for t in range(NT):
            tsl = slice(t * 128, t * 128 + R)
            if not samp:
                gt = gi * NTP + t
                for g in range(2):
                    kinds = ([0] if gt > 0 else []) + [1]
                    ptk = []
                    for kind_ in kinds:
                        kslot = t if kind_ == 0 else t + 1
                        bs = nb()
                        pti = len(ptk)

                        def mmS(e, g=g, kslot=kslot, kind_=kind_, bs=bs, tsl=tsl):
                            e.matmul(banks[bs], lhsT=kaT[:, g, kslot * 128:(kslot + 1) * 128],
                                     rhs=qaT[:, 4 * g:4 * g + 4, tsl], start=True, stop=False)
                            return e.matmul(banks[bs], lhsT=idb, rhs=abias[:, kind_, 4 * g:4 * g + 4, :],
                                            start=False, stop=True)
                        P.op("pe", mmS, reads=[("kaT", kslot), ("qaT", t), "idb", "abias"], writes=[BK(bs)])
                        P.op("act", lambda e, bs=bs, pti=pti: e.activation(out=PT[pti], in_=banks[bs], func=AF.Exp),
                             reads=[BK(bs)], writes=[("PT", pti)])
                        ptk.append((pti, kslot))
                    bo = nb()
                    po = banks[bo][:, 0:260].rearrange("p (h d) -> p h d", d=65)

                    def mmPV(e, g=g, ptk=ptk, po=po):
                        for hl in range(4):
                            for n_, (pti, kslot) in enumerate(ptk):
                                ins = e.matmul(po[:, hl, :], lhsT=PT[pti][:, hl * 128:(hl + 1) * 128],
                                               rhs=vaug[:, kslot, g, :], start=(n_ == 0), stop=(n_ == len(ptk) - 1))
                        return ins
                    P.op("pe", mmPV, reads=[("PT", i_) for i_, _ in ptk] + [("vaug", ks_) for _, ks_ in ptk],
                         writes=[BK(bo)])
                    P.op("dve", lambda e, g=g, po=po: e.tensor_tensor(out=stat[:, 4:8], in0=po[:, :, 64],
                                                                      in1=esink[:, 4 * g:4 * g + 4], op=ALU.add),
                         reads=[BK(bo), "esink"], writes=["stat"])
                    P.op("dve", lambda e: e.reciprocal(out=stat[:, 4:8], in_=stat[:, 4:8]), reads=["stat"], writes=["stat"])
                    P.op("dve", lambda e, g=g, po=po: e.tensor_tensor(
                        out=ya[:, g * 256:(g + 1) * 256].rearrange("p (h d) -> p h d", d=64), in0=po[:, :, 0:64],
                        in1=stat[:, 4:8].unsqueeze(2).broadcast_to([128, 4, 64]), op=ALU.mult),
                        reads=[BK(bo), "stat"], writes=[("ya", g)])
                transpose_to(ya, 128, 4, yaT, tsl, [("ya", 0), ("ya", 1)], ("yaT", t))
        if not samp:
            P.op("dve", lambda e: e.tensor_copy(out=kaT[:, :, 0:128], in_=kaT[:, :, NTP * 128:(NTP + 1) * 128]),
                 reads=[("kaT", NTP)], writes=[("kaT", 0)])
            P.op("dve", lambda e: e.tensor_copy(out=vaug[:, 0, :, :], in_=vaug[:, NTP, :, :]),
                 reads=[("vaug", NTP)], writes=[("vaug", 0)])
        else:
            sample_attention()

        for t in mo_list:
            mlstm_out_b(128, t, t % 2)
        if samp:
            mlstm_out_b(64, 0, 0)
        if not samp and gi == NG - 1:
            for h in range(4):
                P.op("dve", lambda e, h=h: e.tensor_scalar(out=hraw[:, h, :], in0=Shat[:, h, :], scalar1=alprev[:, h:h + 1],
                                                           scalar2=None, op0=ALU.mult),
                     reads=[("Shat", h), "alprev"], writes=[("hraw", 0, 0), ("hraw", 0, 1)])
            P.dma("pool", lambda e: [e.dma_start(out=o_Cp.rearrange("h d e -> d h e"), in_=hraw[:, :, 0:128])], reads=[("hraw", 0, 0), ("hraw", 0, 1)])
            P.dma("pool", lambda e: [e.dma_start(out=o_np.rearrange("h d -> d h"), in_=hraw[:, :, 128],
                                               allow_slow_non_contiguous=True)], reads=[("hraw", 0, 0), ("hraw", 0, 1)])
            P.dma("pool", lambda e: [e.dma_start(out=o_mp, in_=mbuf[:, TMAX:TMAX + 1])], reads=["mbuf"])
        elif not samp:
            P.op("dve", lambda e: e.tensor_copy(out=mbuf[:, 0:1], in_=mbuf[:, TMAX:TMAX + 1]), reads=["mbuf"], writes=["mbuf"])
        wG = []
        for hf in range(2):
            wga, kga = wpiece([(lambda sl: wview(sl, 8, 512), w_in_v[:, :, 2824 + 512 * hf:2824 + 512 * (hf + 1)])])
            wgm, kgm = wpiece([(lambda sl: wview(sl, 8, 512), w_in_v[:, :, 3848 + 512 * hf:3848 + 512 * (hf + 1)])])
            wG.append((wview(wga, 8, 512), kga, wview(wgm, 8, 512), kgm))
        for c in range(8):
            bGA, bGM = nb(), nb()
            wGAv, kGA, wGMv, kGM = wG[c // 4]
            gsl = slice((c % 4) * 128, (c % 4 + 1) * 128)

            def mmG(e, gsl=gsl, bGA=bGA, bGM=bGM, wGAv=wGAv, wGMv=wGMv):
                for k in range(8):
                    e.matmul(banks[bGA][:, 0:T], lhsT=wGAv[:, k, gsl], rhs=xnT[:, k, 0:T], start=(k == 0), stop=(k == 7))
                for k in range(8):
                    ins = e.matmul(banks[bGM][:, 0:T], lhsT=wGMv[:, k, gsl], rhs=xnT[:, k, 0:T], start=(k == 0), stop=(k == 7))
                return ins
            P.op("pe", mmG, reads=XNT + [kGA, kGM], writes=[BK(bGA), BK(bGM)])
            P.op("act", lambda e, c=c, bGA=bGA: e.activation(out=sgaT[:, c, 0:T], in_=banks[bGA][:, 0:T], func=AF.Sigmoid),
                 reads=[BK(bGA), "bigfence"], writes=[("sga", c)])
            P.op("act", lambda e, c=c, bGM=bGM: e.activation(out=sgmT[:, c, 0:T], in_=banks[bGM][:, 0:T], func=AF.Sigmoid),
                 reads=[BK(bGM), "bigfence"], writes=[("sgm", c)])
        for t in range(NT):
            transpose_to(hm2[t % 2], R, 4, hmT, slice(t * 128, t * 128 + R), ("hm", t % 2), ("hmT", t))
        if gi == 0 and not samp:
            dbg("yaT", yaT, [("yaT", t) for t in range(NT)])
            dbg("qaT", qaT, [("qaT", t) for t in range(NT)])
        if gi == 0 and not samp:
            dbg("hmT", hmT, [("hmT", t) for t in range(NT)])
        wOA, kOA = wpiece([(lambda sl: wview(sl, 4, 1024), w_oa_v)])
        wOM, kOM = wpiece([(lambda sl: wview(sl, 4, 1024), w_om_v)])
        wOAv = wview(wOA, 4, 1024)
        wOMv = wview(wOM, 4, 1024)
        YAT = [("yaT", t) for t in range(NT)]
        HMT = [("hmT", t) for t in range(NT)]
        for c in range(8):
            csl = slice(c * 128, (c + 1) * 128)
            bA, bB = nb(), nb()

            def mmM(e, csl=csl, bA=bA, bB=bB):
                for k in range(4):
                    e.matmul(banks[bA][:, 0:T], lhsT=wOAv[:, k, csl], rhs=yaT[:, k, 0:T], start=(k == 0), stop=(k == 3))
                for k in range(4):
                    ins = e.matmul(banks[bB][:, 0:T], lhsT=wOMv[:, k, csl], rhs=hmT[:, k, 0:T], start=(k == 0), stop=(k == 3))
                return ins
            P.op("pe", mmM, reads=YAT + HMT + [kOA, kOM], writes=[BK(bA), BK(bB)])
            P.op("dve", lambda e, c=c, bA=bA: e.tensor_tensor(out=sg1[:, 0:T], in0=sgaT[:, c, 0:T], in1=banks[bA][:, 0:T], op=ALU.mult),
                 reads=[("sga", c), BK(bA)], writes=["sg1"])
            P.op("dve", lambda e, c=c, bB=bB: e.tensor_tensor(out=sg2[:, 0:T], in0=sgmT[:, c, 0:T], in1=banks[bB][:, 0:T], op=ALU.mult),
                 reads=[("sgm", c), BK(bB)], writes=["sg2"])
            mix_eng = "pool" if (not samp and not grp_state["first"]) else "dve"
            P.op(mix_eng, lambda e, c=c: e.tensor_tensor(out=mixedT[:, c, 0:T], in0=sg1[:, 0:T], in1=sg2[:, 0:T], op=ALU.add),
                 reads=["sg1", "sg2", "bigfence"], writes=[("mixedT", c)])
        MIX = [("mixedT", c) for c in range(8)]

        wOUh = []
        for half in range(2):
            wOU, kOU = wpiece([(lambda sl: wview(sl, 8, 512), w_out_v[:, :, half * 512:(half + 1) * 512])])
            wOUh.append((wview(wOU, 8, 512), kOU))
        for t in range(NT):
            tsl = slice(t * 128, t * 128 + R)
            for half in range(2):
                wOUv, kOU = wOUh[half]
                bb = nb()

                def mmOut1(e, tsl=tsl, bb=bb, wOUv=wOUv):
                    for k in range(5):
                        ins = e.matmul(banks[bb][:R, :], lhsT=mixedT[:, k, tsl], rhs=wOUv[:, k, :],
                                       start=(k == 0), stop=False)
                    return ins

                def mmOut2(e, tsl=tsl, bb=bb, wOUv=wOUv):
                    for k in range(5, 8):
                        ins = e.matmul(banks[bb][:R, :], lhsT=mixedT[:, k, tsl], rhs=wOUv[:, k, :],
                                       start=False, stop=(k == 7))
                    return ins
                P.op("pe", mmOut1, reads=MIX[0:5] + [kOU], writes=[BK(bb)])
                P.op("pe", mmOut2, reads=MIX[5:8] + [kOU], writes=[BK(bb)])
                P.op("dve", lambda e, t=t, half=half, bb=bb: e.tensor_tensor(
                    out=xs[:R, t, half * 512:(half + 1) * 512], in0=xs[:R, t, half * 512:(half + 1) * 512],
                    in1=banks[bb][:R, :], op=ALU.add), reads=[xky(t), BK(bb)], writes=[xky(t)])
            if gi == 0 and not samp:
                dbg("x1_%d" % t, xs[:, t, :], [xky(t)])
            if t == 0:
                rms_chain(R, 0, xky(0), xs)
            s8_list.append(t)
        if gi == 0 and not samp:
            dbg("mixedT", mixedT, MIX)
        HT = [("hT", c) for c in range(22)]
        fence(MIX + [("sga", c) for c in range(8)] + [("sgm", c) for c in range(8)], HT)

        for t in s8_list:
            if t > 0:
                rms_chain(R, t, xky(t), xs)
            rms_tr(R, t, gffnT, "gffn", xnT, nky(t))

        if gi == 0 and not samp:
            dbg("xn2T", xnT, XNT)
        if samp:
            sample_conv_state_load()
        nxt = None
        if not samp:
            nxt = ("P", gi + 1, NTP, 128) if gi + 1 < NG else ("S", 0, 1, 64)
            xs_n = xs2[1 - gpar]
            for t in range(nxt[2]):
                load_x(nxt[0], nxt[1], t, xs_n, ("xs", 1 - gpar, t))
        for i in range(11):
            wU, kU = wpiece([(lambda sl: wview(sl, 8, 512)[:, :, 0:256], w_up_v[:, :, 256 * i:256 * i + 256]),
                             (lambda sl: wview(sl, 8, 512)[:, :, 256:512], w_up_v[:, :, DFF + 256 * i:DFF + 256 * i + 256])])
            wUv = wview(wU, 8, 512)
            for sub in range(2):
                c = 2 * i + sub
                ba_, bb_ = nb(), nb()

                TU = T if samp else T + 2
                xsrcT = xnT if samp else xnT_full

                def mmUp(e, sub=sub, ba_=ba_, bb_=bb_, wUv=wUv):
                    for k in range(8):
                        e.matmul(banks[ba_][:, 0:TU], lhsT=wUv[:, k, sub * 128:(sub + 1) * 128], rhs=xsrcT[:, k, 0:TU],
                                 start=(k == 0), stop=(k == 7))
                    for k in range(8):
                        ins = e.matmul(banks[bb_][:, 0:TU], lhsT=wUv[:, k, 256 + sub * 128:256 + (sub + 1) * 128],
                                       rhs=xsrcT[:, k, 0:TU], start=(k == 0), stop=(k == 7))
                    return ins
                P.op("pe", mmUp, reads=XNT + [kU, ("xcarry", gpar)], writes=[BK(ba_), BK(bb_)])
                par = c % 2
                for which, bnk in ((0, ba_), (1, bb_)):
                    cc = c + 22 * which
                    conv_chunk(samp, T, which, par, bnk, cc, gi == NG - 1, not grp_state["first"])
                P.op("act", lambda e, par=par: e.activation(out=gact[par][:, 0:T], in_=yb[0][par][:, 0:T], func=AF.Gelu_apprx_tanh),
                     reads=[("yb", 0, par)], writes=[("yb", 0, par)])
                h_eng = "pool" if (not samp and not grp_state["first"]) else "dve"
                P.op(h_eng, lambda e, c=c, par=par: e.tensor_tensor(out=hT[:, c, 0:T], in0=gact[par][:, 0:T], in1=yb[1][par][:, 0:T],
                                                                    op=ALU.mult),
                     reads=[("yb", 0, par), ("yb", 1, par)], writes=[("hT", c)])
        if not samp:
            xnf_n = xnTf2[1 - gpar]
            P.op("act", lambda e: e.activation(out=xnf_n[:, :, 0:2], in_=xnT_full[:, :, T:T + 2], func=AF.Copy),
                 reads=XNT, writes=[("xcarry", 1 - gpar)])
        if samp:
            sample_conv_out()
        elif gi == NG - 1:
            bcv = nb()
            P.op("pe", lambda e: e.transpose(out=banks[bcv][0:88, 0:128], in_=convst.rearrange("p j c -> p (j c)"), identity=idf),
                 reads=[("convst", cc) for cc in range(44)] + ["idf"], writes=[BK(bcv)])
            cvp = sb([88, 128], name="cvp")
            P.op("dve", lambda e: e.tensor_copy(out=cvp, in_=banks[bcv][0:88, 0:128]), reads=[BK(bcv)], writes=["cvp"])
            for j in range(2):
                P.dma("pool", lambda e, j=j: [e.dma_start(out=o_cp[j].rearrange("(c p) -> c p", p=128), in_=cvp[j * 44:(j + 1) * 44, :])],
                      reads=["cvp"])

        if gi == 0 and not samp:
            dbg("hT", hT, HT)
        wD = []
        for (k0, nk) in ((0, 4), (4, 4), (8, 4), (12, 4), (16, 4), (20, 2)):
            w_, k_ = wpiece([(lambda sl, nk=nk: wview(sl, nk, 1024), w_down_v[:, k0:k0 + nk, :])], used=nk * 1024)
            wD.append((wview(w_, nk, 1024), k_))
        dn = []
        for t in range(NT):
            tsl = slice(t * 128, t * 128 + R)
            for half in range(2):
                bb = nb()
                dn.append((t, tsl, half, bb))

                def mmDn1(e, tsl=tsl, half=half, bb=bb):
                    for k in range(16):
                        ins = e.matmul(banks[bb][:R, :], lhsT=hT[:, k, tsl],
                                       rhs=wD[k // 4][0][:, k % 4, half * 512:(half + 1) * 512], start=(k == 0), stop=False)
                    return ins
                P.op("pe", mmDn1, reads=HT[0:16] + [k_ for _, k_ in wD[0:4]], writes=[BK(bb)])
        if not samp:
            for t in range(nxt[2]):
                rmsnorm_T(nxt[3], t, gmixT, "gmix", ("xs", 1 - gpar, t), xnf_n[:, :, 2:2 + TMAX], ("xnT", 1 - gpar, t), xs_n)
        for (t, tsl, half, bb) in dn:
            def mmDn2(e, tsl=tsl, half=half, bb=bb):
                for k in range(16, 22):
                    ins = e.matmul(banks[bb][:R, :], lhsT=hT[:, k, tsl],
                                   rhs=wD[k // 4][0][:, k % 4, half * 512:(half + 1) * 512], start=False, stop=(k == 21))
                return ins
            P.op("pe", mmDn2, reads=HT[16:22] + [k_ for _, k_ in wD[4:6]], writes=[BK(bb)])
            P.op("dve", lambda e, t=t, half=half, bb=bb: e.tensor_tensor(
                out=xs[:R, t, half * 512:(half + 1) * 512], in0=xs[:R, t, half * 512:(half + 1) * 512],
                in1=banks[bb][:R, :], op=ALU.add), reads=[xky(t), BK(bb)], writes=[xky(t)])
        if gi == 0 and not samp:
            dbg("x2", xs, XK)
        fence(HT, ["bigfence"])

        wPGh = []
        for half in range(2):
            w_, k_ = wpiece([(lambda sl: wview(sl, 8, 512), w_pg_v[:, :, half * 512:(half + 1) * 512])])
            wPGh.append((wview(w_, 8, 512), k_))
        wPL, kPL = wpiece([(lambda sl: wview(sl, 2, 1024), w_ple_v)], used=2048)
        wPLv = wview(wPL, 2, 1024)
        if not samp:
            sv = (grp_state["idx"], grp_state["first"])
            grp_state["idx"], grp_state["first"] = 0, False
            pf["A"] = (wpiece([(lambda sl: wview(sl, 8, 512), w_in_v[:, :, 0:512])]),
                       wpiece([(lambda sl: wview(sl, 8, 256), w_in_v[:, :, 512:768])], used=2048))
            pf["I"] = wpiece([(lambda sl: wview(sl, 8, 8), w_in_v[:, :, 2816:2824])], used=64)
            pf["Q"] = wpiece([(lambda sl: wview(sl, 8, 512), w_in_v[:, :, 768:1280])])
            grp_state["idx"], grp_state["first"] = sv
        for t in range(NT):
            tsl = slice(t * 128, t * 128 + R)
            P.op("act", lambda e, t=t: e.activation(out=x2b[:R], in_=xs[:R, t, :], func=AF.Copy),
                 reads=[xky(t)], writes=["xnb"])
            transpose_to(x2b, R, 8, xnT, tsl, "xnb", nky(t))
            P.op("act", lambda e, t=t: e.activation(out=pb16[:R], in_=ps_t[:R, t, :], func=AF.Copy),
                 reads=[("ps", t)], writes=["pb16"])
            transpose_to(pb16, R, 2, pT, tsl, "pb16", ("pT", t))
            for half in range(2):
                bg, bp = nb(), nb()
                hs = slice(half * 512, (half + 1) * 512)

                wPGv, kPG = wPGh[half]

                def mmP(e, tsl=tsl, hs=hs, bg=bg, bp=bp, wPGv=wPGv):
                    for k in range(8):
                        e.matmul(banks[bg][:R, :], lhsT=xnT[:, k, tsl], rhs=wPGv[:, k, :], start=(k == 0), stop=(k == 7))
                    for k in range(2):
                        ins = e.matmul(banks[bp][:R, :], lhsT=pT[:, k, tsl], rhs=wPLv[:, k, hs], start=(k == 0), stop=(k == 1))
                    return ins
                P.op("pe", mmP, reads=[nky(t), ("pT", t), kPG, kPL], writes=[BK(bg), BK(bp)])
                P.op("act", lambda e, bg=bg: e.activation(out=hn[:R, 0:512], in_=banks[bg][:R, :], func=AF.Sigmoid),
                     reads=[BK(bg)], writes=["hn"])
                P.op("dve", lambda e, bp=bp: e.tensor_tensor(out=hn[:R, 0:512], in0=hn[:R, 0:512], in1=banks[bp][:R, :],
                                                            op=ALU.mult), reads=["hn", BK(bp)], writes=["hn"])
                P.op("dve", lambda e, t=t, hs=hs: e.tensor_tensor(out=xs[:R, t, hs], in0=xs[:R, t, hs], in1=hn[:R, 0:512],
                                                                 op=ALU.add), reads=["hn", xky(t)], writes=[xky(t)])
            P.dma("sp", lambda e, t=t: [e.dma_start(out=yout[row0 + t * 128:row0 + t * 128 + R, :], in_=xs[:R, t, :])],
                  reads=[xky(t)])
            if not samp:
                if gi + 1 < NG:
                    load_p("P", gi + 1, t)
                elif t == 0:
                    load_p("S", 0, 0)

        if samp:
            TK = ["tokA0", "tokA1"]
            for b in range(16):
                P.dma("sp", lambda e, b=b: [e.dma_start(out=o_ks[b, 124:128, :], in_=qkn[4 * b:4 * b + 4, 512:640])],
                      reads=["qkn"])
                P.dma("sp", lambda e, b=b: [e.dma_start(out=o_vs[b, 124:128, :], in_=tokA[4 * b:4 * b + 4, 640:768])],
                      reads=TK)

    alprev = sb([128, 4])

    def mlstm_out_a(R, srcs, hi):
        hraw = hraw2[hi]
        for j, (bnk, half) in enumerate([(srcs[0][0], 0), (srcs[2][0], 1)]):
            P.op("act", lambda e, bnk=bnk, half=half: e.activation(
                out=hraw[:R, 2 * half:2 * half + 2, :], in_=banks[bnk][:R, 0:258].rearrange("p (h d) -> p h d", d=129),
                func=AF.Copy), reads=[BK(bnk)], writes=[("hraw", hi, half)])

    def mlstm_out_b(R, t, hi):
        hm = hm2[hi]
        hraw = hraw2[hi]
        HR = [("hraw", hi, 0), ("hraw", hi, 1)]
        al = scal[:R, t, 0:4]
        gm = scal[:R, t, 8:12]
        d_ = stat[:R, 4:8]
        P.op("act", lambda e: e.activation(out=d_, in_=hraw[:R, :, 128], func=AF.Abs),
             reads=HR, writes=["stat"])
        P.op("dve", lambda e: e.tensor_tensor(out=d_, in0=d_, in1=al, op=ALU.mult), reads=["stat", ("scal", t, 0)], writes=["stat"])
        P.op("dve", lambda e: e.tensor_tensor(out=d_, in0=d_, in1=gm, op=ALU.max), reads=["stat", ("scal", t, 1)], writes=["stat"])
        P.op("dve", lambda e: e.reciprocal(out=d_, in_=d_), reads=["stat"], writes=["stat"])
        P.op("dve", lambda e: e.tensor_tensor(out=d_, in0=d_, in1=al, op=ALU.mult), reads=["stat", ("scal", t, 0)], writes=["stat"])
        hn3 = hn[:R].rearrange("p (h d) -> p h d", d=128)
        P.op("dve", lambda e: e.tensor_tensor(out=hn3, in0=hraw[:R, :, 0:128], in1=d_.unsqueeze(2).broadcast_to([R, 4, 128]),
                                              op=ALU.mult), reads=HR + ["stat"], writes=["hn"])
        P.op("dve", lambda e: e.tensor_tensor(out=hsq[:R], in0=hn[:R], in1=hn[:R], op=ALU.mult), reads=["hn"], writes=["qkn"])
        r_ = stat[:R, 8:12]
        P.op("dve", lambda e: e.tensor_reduce(out=r_, in_=hsq[:R].rearrange("p (h d) -> p h d", d=128), axis=AX.X, op=ALU.add),
             reads=["qkn"], writes=["stat"])
        P.op("dve", lambda e: e.tensor_scalar(out=r_, in0=r_, scalar1=1.0 / 128, scalar2=EPS, op0=ALU.mult, op1=ALU.add),
             reads=["stat"], writes=["stat"])
        P.op("act", lambda e: e.activation(out=r_, in_=r_, func=AF.Ln), reads=["stat"], writes=["stat"])
        P.op("act", lambda e: e.activation(out=r_, in_=r_, func=AF.Exp, scale=-0.5), reads=["stat"], writes=["stat"])
        P.op("dve", lambda e: e.tensor_tensor(out=hn3, in0=hn3, in1=r_.unsqueeze(2).broadcast_to([R, 4, 128]), op=ALU.mult),
             reads=["hn", "stat"], writes=["hn"])
        P.op("dve", lambda e: e.tensor_tensor(out=hn[:R], in0=hn[:R], in1=ghm_b[:R], op=ALU.mult), reads=["hn", "ghm"], writes=["hn"])
        P.op("dve", lambda e: e.tensor_tensor(out=hm[:R], in0=hn[:R], in1=osig[:R, t, :], op=ALU.mult),
             reads=["hn", ("osig", t)], writes=[("hm", hi)])

    def conv_chunk(samp, T, which, par, bnk, cc, last, use_pool):
        y = yb[which][par]
        w0 = cw[:, 0, cc:cc + 1]
        w1 = cw[:, 1, cc:cc + 1]
        w2 = cw[:, 2, cc:cc + 1]
        bb = cw[:, 3, cc:cc + 1]
        YK = ("yb", which, par)
        if not samp:
            ps = banks[bnk]
            yf = y[:, 0:T]
            P.op("act", lambda e: e.activation(out=yf, in_=ps[:, 2:T + 2], func=AF.Identity, scale=w2, bias=bb),
                 reads=[BK(bnk), "cw"], writes=[YK])
            if which == 1 and use_pool:
                t1 = ubb[par][:, 0:T]
                TK1 = ("t1", par)
                P.op("act", lambda e: e.activation(out=t1, in_=ps[:, 1:T + 1], func=AF.Identity, scale=w1),
                     reads=[BK(bnk), "cw"], writes=[TK1])
                P.op("dve", lambda e: e.scalar_tensor_tensor(out=yf, in0=ps[:, 0:T], scalar=w0, in1=yf,
                                                             op0=ALU.mult, op1=ALU.add), reads=[BK(bnk), YK, "cw"], writes=[YK])
                P.op("pool", lambda e: e.tensor_tensor(out=yf, in0=yf, in1=t1, op=ALU.add), reads=[TK1, YK], writes=[YK])
            else:
                P.op("dve", lambda e: e.scalar_tensor_tensor(out=yf, in0=ps[:, 1:T + 1], scalar=w1, in1=yf,
                                                             op0=ALU.mult, op1=ALU.add), reads=[BK(bnk), YK, "cw"], writes=[YK])
                P.op("dve", lambda e: e.scalar_tensor_tensor(out=yf, in0=ps[:, 0:T], scalar=w0, in1=yf,
                                                             op0=ALU.mult, op1=ALU.add), reads=[BK(bnk), YK, "cw"], writes=[YK])
            if last:
                P.op("act", lambda e: e.activation(out=convst[:, :, cc], in_=ps[:, T:T + 2], func=AF.Copy),
                     reads=[BK(bnk)], writes=[("convst", cc)])
            return
        pf = banks[bnk][:, 0:64].rearrange("p (b s) -> p b s", s=4)
        yf = y[:, 0:64].rearrange("p (b s) -> p b s", s=4)
        SK = ("cstT", cc)
        st = cstT[:, cc, :].rearrange("p (b j) -> p b j", j=2)
        sh = lambda a, lo, hi: a[:, :, lo:hi]
        L = 4
        P.op("act", lambda e: e.activation(out=yf, in_=pf, func=AF.Identity, scale=w2, bias=bb),
             reads=[BK(bnk), "cw"], writes=[YK])
        P.op("dve", lambda e: e.scalar_tensor_tensor(out=sh(yf, 1, L), in0=sh(pf, 0, L - 1), scalar=w1, in1=sh(yf, 1, L),
                                                     op0=ALU.mult, op1=ALU.add), reads=[BK(bnk), YK, "cw"], writes=[YK])
        P.op("dve", lambda e: e.scalar_tensor_tensor(out=sh(yf, 2, L), in0=sh(pf, 0, L - 2), scalar=w0, in1=sh(yf, 2, L),
                                                     op0=ALU.mult, op1=ALU.add), reads=[BK(bnk), YK, "cw"], writes=[YK])
        P.op("dve", lambda e: e.scalar_tensor_tensor(out=sh(yf, 0, 2), in0=sh(st, 0, 2), scalar=w0, in1=sh(yf, 0, 2),
                                                     op0=ALU.mult, op1=ALU.add), reads=[SK, YK, "cw"], writes=[YK])
        P.op("dve", lambda e: e.scalar_tensor_tensor(out=sh(yf, 0, 1), in0=sh(st, 1, 2), scalar=w1, in1=sh(yf, 0, 1),
                                                     op0=ALU.mult, op1=ALU.add), reads=[SK, YK, "cw"], writes=[YK])
        P.op("dve", lambda e: e.tensor_copy(out=st, in_=sh(pf, L - 2, L)), reads=[BK(bnk)], writes=[SK])

    cstT = sb([128, 44, 32])

    def sample_conv_state_load():
        for q in range(11):
            b_ = nb()
            st_ = stg_in[q % 2]
            skeys = [("stg_in", 0)] + [("cwraw", j) for j in range(4)] if q % 2 == 0 else ["hn"]
            P.dma("pool", lambda e, q=q, st_=st_: [e.dma_start(out=st_, in_=cst_s[:, q * 512:(q + 1) * 512])], writes=skeys)

            def trc(e, q=q, b_=b_, st_=st_):
                for j in range(4):
                    ins = e.transpose(out=banks[b_][:, j * 32:(j + 1) * 32], in_=st_[:, j * 128:(j + 1) * 128],
                                      identity=idf[0:32, 0:32])
                return ins
            P.op("pe", trc, reads=[skeys[0], "idf"], writes=[BK(b_)])
            P.op("dve", lambda e, q=q, b_=b_: e.tensor_copy(out=cstT[:, 4 * q:4 * q + 4, :].rearrange("p c r -> p (c r)"),
                                                            in_=banks[b_][:, 0:128]), reads=[BK(b_)], writes=[("cstT", 4 * q + j) for j in range(4)])

    def sample_conv_out():
        for q in range(11):
            b_ = nb()
            so_ = stg_out[q % 2]

            def trc(e, q=q, b_=b_):
                for j in range(4):
                    cc = 4 * q + j
                    ins = e.transpose(out=banks[b_][0:32, j * 128:(j + 1) * 128], in_=cstT[:, cc, :], identity=idf)
                return ins
            P.op("pe", trc, reads=[("cstT", 4 * q + j) for j in range(4)] + ["idf"], writes=[BK(b_)])
            P.op("act", lambda e, b_=b_, so_=so_: e.activation(out=so_, in_=banks[b_][0:32, :], func=AF.Copy),
                 reads=[BK(b_)], writes=["hn"])
            P.dma("pool", lambda e, q=q, so_=so_: [e.dma_start(out=o_cs[:, q * 512:(q + 1) * 512], in_=so_)], reads=["hn"])

    def sample_attention():
        P.op("dve", lambda e: e.memset(PTpad.rearrange("p b h q -> p (b h q)"), 0.0), reads=["bigfence"], writes=["PTpad"])
        for g in range(2):
            bs = nb()

            def mmS(e, g=g, bs=bs):
                e.matmul(banks[bs][0:64, 0:256], lhsT=kaT[:, g, 0:64], rhs=qaT[:, 4 * g:4 * g + 4, 0:64], start=True, stop=False)
                return e.matmul(banks[bs][0:64, 0:256], lhsT=idb[0:64, 0:64], rhs=sbias[:, 4 * g:4 * g + 4, :], start=False, stop=True)
            P.op("pe", mmS, reads=[("kaT", 0), ("qaT", 0), "idb", "sbias"], writes=[BK(bs)])
            P.op("act", lambda e, g=g, bs=bs: e.activation(out=PT[0][0:64, g * 256:(g + 1) * 256], in_=banks[bs][0:64, 0:256],
                                                           func=AF.Exp), reads=[BK(bs)], writes=[("PT", 0)])
        def emit_tr(b):
            bt = nb()
            ptv = banks[bt].bitcast(BF16)
            kk = b % 2

            def trk(e, b=b, ptv=ptv):
                for g in range(2):
                    ins = e.transpose(out=ptv[0:64, g * 128:(g + 1) * 128], in_=kcb[:, b, g * 64:(g + 1) * 64], identity=idb)
                return ins
            P.op("pe", trk, reads=["kcb", "idb"], writes=[BK(bt)])
            P.op("dve", lambda e, kk=kk, ptv=ptv: e.tensor_copy(out=kcT2[kk], in_=ptv[0:64, 0:256]),
                 reads=[BK(bt)], writes=[("kcT", kk)])

        def emit_score(b):
            kk = b % 2
            bs = nb()

            def mmS2(e, b=b, kk=kk, bs=bs):
                for g in range(2):
                    e.matmul(banks[bs][:, g * 16:(g + 1) * 16], lhsT=kcT2[kk][:, g * 128:(g + 1) * 128],
                             rhs=qaT[:, 4 * g:4 * g + 4, 4 * b:4 * b + 4], start=True, stop=False)
                    ins = e.matmul(banks[bs][:, g * 16:(g + 1) * 16], lhsT=idb, rhs=abias[:, 0, 4 * g:4 * g + 4, 0:4],
                                   start=False, stop=True)
                return ins
            P.op("pe", mmS2, reads=[("kcT", kk), ("qaT", 0), "idb", "abias"], writes=[BK(bs)])
            P.op("act", lambda e, b=b, bs=bs: e.activation(
                out=PTpad[:, b, :, 4 * b:4 * b + 4],
                in_=banks[bs][:, 0:32].rearrange("p (h q) -> p h q", q=4), func=AF.Exp),
                reads=[BK(bs), "PTpad"], writes=[("PTpad", b, 0), ("PTpad", b, 1)])

        emit_tr(0)
        for b in range(16):
            if b + 1 < 16:
                emit_tr(b + 1)
            emit_score(b)
        for g in range(2):
            bo = nb()
            po = banks[bo][0:64, 0:260].rearrange("p (h d) -> p h d", d=65)

            def mmPV(e, g=g, po=po):
                for hl in range(4):
                    h = 4 * g + hl
                    e.matmul(po[:, hl, :], lhsT=PT[0][0:64, h * 64:(h + 1) * 64], rhs=vaug[0:64, 0, g, :], start=True, stop=False)
                    for b in range(16):
                        ins = e.matmul(po[:, hl, :], lhsT=PTpad[:, b, h, :], rhs=vcaug[:, b, g, :], start=False, stop=(b == 15))
                return ins
            P.op("pe", mmPV, reads=[("PT", 0), ("vaug", 0), "vcaug"] + [("PTpad", b, g) for b in range(16)], writes=[BK(bo)])
            P.op("dve", lambda e, g=g, po=po: e.tensor_tensor(out=stat[0:64, 4:8], in0=po[:, :, 64], in1=esink[0:64, 4 * g:4 * g + 4],
                                                              op=ALU.add), reads=[BK(bo), "esink"], writes=["stat"])
            P.op("dve", lambda e: e.reciprocal(out=stat[0:64, 4:8], in_=stat[0:64, 4:8]), reads=["stat"], writes=["stat"])
            P.op("dve", lambda e, g=g, po=po: e.tensor_tensor(
                out=ya[0:64, g * 256:(g + 1) * 256].rearrange("p (h d) -> p h d", d=64), in0=po[:, :, 0:64],
                in1=stat[0:64, 4:8].unsqueeze(2).broadcast_to([64, 4, 64]), op=ALU.mult),
                reads=[BK(bo), "stat"], writes=[("ya", g)])
        transpose_to(ya, 64, 4, yaT, slice(0, 64), [("ya", 0), ("ya", 1)], ("yaT", 0))
        fence(["PTpad"] + [("PTpad", b, g) for b in range(16) for g in range(2)], ["bigfence"])

    def sample_mlstm():
        bn_ = nb()
        P.op("pe", lambda e: e.transpose(out=banks[bn_][:, 0:64], in_=n32, identity=idf[0:64, 0:64]),
             reads=["n32", "idf"], writes=[BK(bn_)])
        P.op("dve", lambda e: e.tensor_copy(out=nT, in_=banks[bn_][:, 0:64]), reads=[BK(bn_)], writes=["nT"])
        P.op("dve", lambda e: e.tensor_copy(out=Cs16[:, :, 128], in_=nT), reads=["nT", "bigfence"], writes=["Cs16n"])
        C32h = [C32q[:, 0:8, :], C32q[:, 8:16, :]]

        def loadC(e8):
            buf = C32h[e8 % 2]
            P.dma("pool", lambda e: [e.dma_start(out=buf, in_=C_s[8 * e8:8 * e8 + 8].rearrange("b d e -> d b e"))],
                  writes=[("C32h", e8 % 2)] + ([("xs", LP, 0), ("xs", LP, 1)] if e8 < 2 else []))
        loadC(0)
        for e8 in range(8):
            buf = C32h[e8 % 2]
            CK = ("C32h", e8 % 2)
            if e8 + 1 < 8:
                loadC(e8 + 1)
            P.op("act", lambda e, e8=e8, buf=buf: e.activation(out=Cs16[:, 8 * e8:8 * e8 + 8, 0:128], in_=buf, func=AF.Copy),
                 reads=[CK, "bigfence"], writes=[("Cs16", e8)])
            if e8 % 2 == 0:
                q4 = e8 // 2
                P.op("dve", lambda e, q4=q4: e.tensor_tensor(
                    out=ktz, in0=kt[0:64, 0, :].unsqueeze(1).broadcast_to([64, 4, 512]),
                    in1=onehot[:, 4 * q4:4 * q4 + 4].unsqueeze(2).broadcast_to([64, 4, 512]), op=ALU.mult),
                    reads=[("kt", 0), "onehot"], writes=["ktz", ("xs", SP_, 1)])
            for bl2 in range(2):
                b = 2 * e8 + bl2
                bl = b % 4
                bu1, bu2 = nb(), nb()

                def mmU(e, bl=bl, bu1=bu1, bu2=bu2):
                    for h in range(4):
                        dst = banks[bu1 if h < 2 else bu2][:, (h % 2) * 129:(h % 2) * 129 + 129]
                        ins = e.matmul(dst, lhsT=ktz[:, bl, h * 128:(h + 1) * 128], rhs=vmaug[0:64, 0, h, :], start=True, stop=True)
                    return ins
                P.op("pe", mmU, reads=["ktz", ("vmaug", 0)], writes=[BK(bu1), BK(bu2)])
                for half, bu in ((0, bu1), (1, bu2)):
                    i0_ = b * 4 + 2 * half
                    j0 = bl2 * 4 + 2 * half
                    uv = banks[bu][:, 0:258].rearrange("p (h d) -> p h d", d=129)
                    P.op("dve", lambda e, j0=j0, uv=uv, buf=buf: e.tensor_tensor(
                        out=buf[:, j0:j0 + 2, :], in0=buf[:, j0:j0 + 2, :], in1=uv[:, :, 0:128], op=ALU.add),
                        reads=[BK(bu), CK], writes=[CK])
                    P.op("dve", lambda e, i0_=i0_, uv=uv: e.tensor_tensor(
                        out=nTo[:, i0_:i0_ + 2], in0=nT[:, i0_:i0_ + 2], in1=uv[:, :, 128], op=ALU.add),
                        reads=[BK(bu), "nT"], writes=["nTo"])
                    for hh in range(2):
                        P.op("act", lambda e, i0_=i0_, j0=j0, buf=buf, hh=hh: e.activation(
                            out=buf[:, j0 + hh, :], in_=buf[:, j0 + hh, :], func=AF.Copy,
                            scale=alast[:, i0_ + hh:i0_ + hh + 1]), reads=["alast", CK], writes=[CK])
            P.dma("pool", lambda e, e8=e8, buf=buf: [e.dma_start(out=o_Cs[8 * e8:8 * e8 + 8].rearrange("b d e -> d b e"), in_=buf)],
                  reads=[CK])
        P.op("dve", lambda e: e.tensor_tensor(out=nTo, in0=nTo, in1=alast[:, 0:64], op=ALU.mult), reads=["nTo", "alast"], writes=["nTo"])
        bn2 = nb()
        P.op("pe", lambda e: e.transpose(out=banks[bn2][0:64, 0:128], in_=nTo, identity=idf), reads=["nTo", "idf"], writes=[BK(bn2)])
        P.op("dve", lambda e: e.tensor_copy(out=n32, in_=banks[bn2][0:64, 0:128]), reads=[BK(bn2)], writes=["n32"])
        P.dma("pool", lambda e: [e.dma_start(out=o_ns, in_=n32)], reads=["n32"])
        CS = [("Cs16", e8) for e8 in range(8)] + ["Cs16n"]
        bo1, bo2 = nb(), nb()
        for h in range(4):
            bs_ = nb()
            P.op("pe", lambda e, h=h, bs_=bs_: e.matmul(banks[bs_][0:64, 0:64], lhsT=ktT[:, h, 0:64], rhs=qmT[:, h, 0:64],
                                                        start=True, stop=True),
                 reads=[("ktT", 0), ("qmT", h)], writes=[BK(bs_)])
            P.op("dve", lambda e, h=h, bs_=bs_: e.tensor_tensor(out=PmT[0:64, h, 0:64], in0=banks[bs_][0:64, 0:64], in1=masks,
                                                                op=ALU.mult), reads=[BK(bs_), "masks"], writes=[("PmTs", h), "PmT"])
            P.op("dve", lambda e, h=h: e.tensor_tensor(out=qTpad, in0=qmT[:, h, 0:64].unsqueeze(1).broadcast_to([128, 16, 64]),
                                                       in1=blockmask, op=ALU.mult),
                 reads=[("qmT", h), "blockmask"], writes=["qTpad"])
            dst = banks[bo1 if h < 2 else bo2][0:64, (h % 2) * 129:(h % 2) * 129 + 129]

            def mmO(e, h=h, dst=dst):
                e.matmul(dst, lhsT=PmT[0:64, h, 0:64], rhs=vmaug[0:64, 0, h, :], start=True, stop=False)
                for b in range(16):
                    ins = e.matmul(dst, lhsT=qTpad[:, b, :], rhs=Cs16[:, b * 4 + h, :], start=False, stop=(b == 15))
                return ins
            P.op("pe", mmO, reads=[("PmTs", h), ("vmaug", 0), "qTpad"] + CS, writes=[BK(bo1 if h < 2 else bo2), ("bo_s", h)])
        mlstm_out_a(64, [(bo1, 0), (bo1, 1), (bo2, 0), (bo2, 1)], 0)
        fence(CS, ["bigfence"])

    P.op("dve", lambda e: e.memset(stat, 0.0), writes=["stat", "bigfence"])
    for gi in range(NG):
        grp_state["idx"] = 4 if "A" in pf else 0
        run_group("P", gi)
        if grp_state["first"]:
            flush_stores(0)
            grp_state["first"] = False
            P.dma("pool", lambda e: [e.dma_start(out=kcb, in_=ck_s.rearrange("b k f -> k b f"))], writes=["kcb"])
            P.op("dve", lambda e: e.memset(vcaug.rearrange("p b g d -> p (b g d)"), 1.0), writes=["vcaug"])
            P.dma("pool", lambda e: [e.dma_start(out=vcaug[:, :, g, 0:64], in_=cv_s.rearrange("b k f -> k b f")[:, :, g * 64:(g + 1) * 64])
                                     for g in range(2)], writes=["vcaug"], n=2)
            P.dma("pool", lambda e: [e.dma_start(out=n32, in_=n_s)], writes=["n32"])
            P.dma("pool", lambda e: [e.dma_start(out=m0T, in_=m_s.rearrange("b h -> h b"), allow_slow_non_contiguous=True)],
                  writes=["m0T"])
            P.dma("pool", lambda e: [e.dma_start(out=o_ks[:, 0:124, :], in_=ck_s[:, 4:128, :])])
            P.dma("pool", lambda e: [e.dma_start(out=o_vs[:, 0:124, :], in_=cv_s[:, 4:128, :])])
    grp_state["idx"] = 4 if "A" in pf else 0
    run_group("S", 0)
    P.emit()
    return nc


_CACHE = {}


def kernel(**inputs):
    f32 = lambda a: np.ascontiguousarray(np.asarray(a, dtype=np.float32))
    inp = {k: f32(v) for k, v in inputs.items()}
    consts = _consts()
    if "nc" not in _CACHE:
        _CACHE["nc"] = build_nc()
    nc = _CACHE["nc"]
    in_maps = []
    for i in range(NCORES):
        sl = slice(16 * i, 16 * i + 16)
        m = {
            "x_p": inp["x_prompt"][i],
            "x_s": inp["x_sample"][sl].reshape(64, D),
            "ck_s": inp["cache_win_k"][0, sl].reshape(16, 128, 128),
            "cv_s": inp["cache_win_v"][0, sl].reshape(16, 128, 128),
            "C_s": inp["state_mlstm_C"][0, sl].reshape(64, 128, 128),
            "n_s": inp["state_mlstm_n"][0, sl].reshape(64, 128),
            "m_s": inp["state_mlstm_m"][0, sl].reshape(16, 4),
            "cst_s": inp["state_ffn_conv"][0, sl].reshape(32, F2),
            "p_p": inp["p_prompt"][0, i],
            "p_s": inp["p_sample"][0, sl].reshape(64, 256),
            "g_mix": inp["g_mix"], "w_in": inp["w_in"][0],
            "b_i": inp["b_i"].reshape(4, 1), "b_f": inp["b_f"].reshape(4, 1),
            "g_q": inp["g_q"], "g_k": inp["g_k"], "sinks": inp["sinks"], "g_hm": inp["g_hm"],
            "w_oa": inp["w_oa"][0], "w_om": inp["w_om"][0], "w_out": inp["w_out"][0],
            "g_ffn": inp["g_ffn"], "w_up": inp["w_up"][0], "conv_w": inp["conv_w"][0],
            "conv_b": inp["conv_b"], "w_down": inp["w_down"][0], "w_ple": inp["w_ple"][0],
            "w_pg": inp["w_ple_gate"][0],
        }
        for k, v in consts.items():
            m["c_" + k] = v
        in_maps.append({k: np.ascontiguousarray(v) for k, v in m.items()})
    res = run_bass_kernel_spmd(nc, in_maps, core_ids=list(range(NCORES)))
    r = res.results
    if DEBUG:
        _CACHE["dbg"] = {k: np.asarray(v) for k, v in r[0].items() if k.startswith("dbg_")}
    cat = lambda k: np.stack([np.asarray(r[i][k], dtype=np.float32) for i in range(NCORES)])
    catb = lambda k: np.concatenate([np.asarray(r[i][k], dtype=np.float32) for i in range(NCORES)], axis=0)
    y_p = cat("o_yp")
    y_s = catb("o_ys").reshape(128, 4, D)
    k_p = cat("o_kp").reshape(1, 8, 128, 2, 64)
    v_p = cat("o_vp").reshape(1, 8, 128, 2, 64)
    C_p = cat("o_Cp").reshape(1, 8, 4, 128, 128)
    n_p = cat("o_np").reshape(1, 8, 4, 128)
    m_p = cat("o_mp").reshape(1, 8, 4)
    c_p = cat("o_cp").reshape(1, 8, 2, F2)
    k_s = catb("o_ks").reshape(1, 128, 128, 2, 64)
    v_s = catb("o_vs").reshape(1, 128, 128, 2, 64)
    C_s = catb("o_Cs").reshape(1, 128, 4, 128, 128)
    n_s = catb("o_ns").reshape(1, 128, 4, 128)
    m_s = catb("o_ms").reshape(1, 128, 4)
    c_s = catb("o_cs").reshape(1, 128, 2, F2)
    return (y_p, y_s, k_p, v_p, C_p, n_p, m_p, c_p, k_s, v_s, C_s, n_s, m_s, c_s)
```
